# Optimizing a Trainium2 kernel written in Bass

```python
import math
import jax, jax.numpy as jnp
from jax import lax
import numpy as np

D_MODEL = 1024
BATCH = 32
SEQ = 2048
DEPTH = 4

N_A_LAYERS = DEPTH // 2
N_B_LAYERS = DEPTH - N_A_LAYERS
SSM_EXPAND = 2
SSM_D_INNER = SSM_EXPAND * D_MODEL
SSM_HEAD_DIM = 64
SSM_HEADS = SSM_D_INNER // SSM_HEAD_DIM
SSM_GROUPS = 4
SSM_HEADS_PER_GROUP = SSM_HEADS // SSM_GROUPS
SSM_STATE = 128
SSM_CONV = 4
SSM_CHUNK = 128
SSM_CONV_DIM = SSM_D_INNER + 2 * SSM_GROUPS * SSM_STATE
SSM_IN_DIM = 2 * SSM_D_INNER + 2 * SSM_GROUPS * SSM_STATE + SSM_HEADS
SB_HEADS = 16
SB_HEAD_DIM = 64
SB_WIDTH = SB_HEADS * SB_HEAD_DIM
SB_BLOCK = 128
D_FF = 2816
FFN_CONV = 3
PLE_DIM = 256
NORM_EPS = 1e-6
SSM_NORM_EPS = 1e-5

kernel_name = "yoco_mamba2_stickbreak_hybrid"


def rms_norm(x, gain, eps=NORM_EPS):
    xf = x.astype(jnp.float32)
    y = xf * lax.rsqrt(jnp.mean(xf * xf, axis=-1, keepdims=True) + eps)
    return (y * gain.astype(jnp.float32)).astype(x.dtype)


def causal_depthwise_conv(u, w, b):
    k = w.shape[0]
    out = lax.conv_general_dilated(
        u, w[:, None, :].astype(u.dtype), window_strides=(1,), padding=[(k - 1, 0)],
        dimension_numbers=("NWC", "WIO", "NWC"), feature_group_count=u.shape[-1])
    return out + b.astype(u.dtype)


def ssd_chunked(x, dt, a, b_mat, c_mat):
    bsz, seq = x.shape[0], x.shape[1]
    nc, L = seq // SSM_CHUNK, SSM_CHUNK
    G, R, P, N = SSM_GROUPS, SSM_HEADS_PER_GROUP, SSM_HEAD_DIM, SSM_STATE
    xr = (x * dt[..., None]).reshape(bsz, nc, L, G, R, P)
    br = b_mat.reshape(bsz, nc, L, G, N)
    cr = c_mat.reshape(bsz, nc, L, G, N)
    a_dt = (dt * a).reshape(bsz, nc, L, G, R).transpose(0, 1, 3, 4, 2)
    a_cs = jnp.cumsum(a_dt, axis=-1)
    tri = jnp.tril(jnp.ones((L, L), dtype=bool))
    decay_in = jnp.exp(jnp.where(tri, a_cs[..., :, None] - a_cs[..., None, :], -jnp.inf))
    cb = jnp.einsum("bclgn,bcsgn->bcgls", cr, br)
    y_diag = jnp.einsum("bcgls,bcgrls,bcsgrp->bclgrp", cb, decay_in, xr)
    decay_states = jnp.exp(a_cs[..., -1:] - a_cs)
    states = jnp.einsum("bclgn,bcgrl,bclgrp->bcgrpn", br, decay_states, xr)
    states = jnp.concatenate([jnp.zeros_like(states[:, :1]), states], axis=1)
    chunk_sum = jnp.pad(a_cs[..., -1].transpose(0, 2, 3, 1), ((0, 0), (0, 0), (0, 0), (1, 0)))
    chunk_cs = jnp.cumsum(chunk_sum, axis=-1)
    tri_c = jnp.tril(jnp.ones((nc + 1, nc + 1), dtype=bool))
    decay_chunk = jnp.exp(jnp.where(tri_c, chunk_cs[..., :, None] - chunk_cs[..., None, :], -jnp.inf))
    new_states = jnp.einsum("bgrzc,bcgrpn->bzgrpn", decay_chunk, states)
    prev_states = new_states[:, :-1]
    y_off = jnp.einsum("bclgn,bcgrpn,bcgrl->bclgrp", cr, prev_states, jnp.exp(a_cs))
    return (y_diag + y_off).reshape(bsz, seq, SSM_HEADS, P)


def mamba2_mixer(h, in_proj, conv_w, conv_b, dt_bias, a_log, d_skip, norm_w, out_proj):
    bsz, seq, _ = h.shape
    zxbcdt = h @ in_proj
    z = zxbcdt[..., :SSM_D_INNER]
    xbc = zxbcdt[..., SSM_D_INNER:SSM_D_INNER + SSM_CONV_DIM]
    dt_raw = zxbcdt[..., SSM_D_INNER + SSM_CONV_DIM:]
    xbc = jax.nn.silu(causal_depthwise_conv(xbc, conv_w, conv_b)).astype(jnp.float32)
    xs = xbc[..., :SSM_D_INNER].reshape(bsz, seq, SSM_HEADS, SSM_HEAD_DIM)
    bm = xbc[..., SSM_D_INNER:SSM_D_INNER + SSM_GROUPS * SSM_STATE].reshape(bsz, seq, SSM_GROUPS, SSM_STATE)
    cm = xbc[..., SSM_D_INNER + SSM_GROUPS * SSM_STATE:].reshape(bsz, seq, SSM_GROUPS, SSM_STATE)
    dt = jax.nn.softplus(dt_raw.astype(jnp.float32) + dt_bias.astype(jnp.float32))
    a = -jnp.exp(a_log.astype(jnp.float32))
    y = ssd_chunked(xs, dt, a, bm, cm) + xs * d_skip.astype(jnp.float32)[:, None]
    g = (y.reshape(bsz, seq, SSM_D_INNER) * jax.nn.silu(z.astype(jnp.float32))).reshape(bsz, seq, SSM_GROUPS, -1)
    g = g * lax.rsqrt(jnp.mean(g * g, axis=-1, keepdims=True) + SSM_NORM_EPS)
    g = g.reshape(bsz, seq, SSM_D_INNER) * norm_w.astype(jnp.float32)
    return g.astype(h.dtype) @ out_proj


def stick_breaking_attention(q, k, v):
    seq = q.shape[1]
    scale = SB_HEAD_DIM ** -0.5
    outs = []
    for blk in range(seq // SB_BLOCK):
        q0 = blk * SB_BLOCK
        q1 = q0 + SB_BLOCK
        z = jnp.einsum("bthd,bshd->bhts", q[:, q0:q1], k[:, :q1]).astype(jnp.float32) * scale
        causal = (q0 + jnp.arange(SB_BLOCK))[:, None] > jnp.arange(q1)[None, :]
        log_beta = jax.nn.log_sigmoid(z)
        log_keep = jnp.where(causal, jax.nn.log_sigmoid(-z), 0.0)
        log_rest = lax.cumsum(log_keep, axis=3, reverse=True) - log_keep
        w = jnp.where(causal, jnp.exp(log_beta + log_rest), 0.0)
        outs.append(jnp.einsum("bhts,bshd->bthd", w.astype(v.dtype), v[:, :q1]))
    return jnp.concatenate(outs, axis=1)


def conv_glu_ffn(h, w_up, conv_w, conv_b, w_down):
    u = causal_depthwise_conv(h @ w_up, conv_w, conv_b)
    gate, val = jnp.split(u, 2, axis=-1)
    return (jax.nn.silu(gate) * val) @ w_down


def per_layer_embedding(h, p_i, norm_gain, w_gate, w_proj):
    gate = jax.nn.sigmoid(rms_norm(h, norm_gain) @ w_gate)
    return gate * (p_i @ w_proj)


def setup_inputs(seed: int = 0) -> dict:
    key = jax.random.key(seed)
    ks = iter(jax.random.split(key, 32))

    def normal(shape, scale):
        return scale * jax.random.normal(next(ks), shape, jnp.float32)

    def gain(shape):
        return 1.0 + normal(shape, 0.02)

    x = normal((BATCH, SEQ, D_MODEL), 1.0)
    p = normal((DEPTH, BATCH, SEQ, PLE_DIM), 1.0)
    attn_norm = gain((DEPTH, D_MODEL))
    ffn_norm = gain((DEPTH, D_MODEL))
    ple_norm = gain((DEPTH, D_MODEL))
    ssm_in_proj = normal((N_A_LAYERS, D_MODEL, SSM_IN_DIM), D_MODEL ** -0.5)
    ssm_conv_w = normal((N_A_LAYERS, SSM_CONV, SSM_CONV_DIM), SSM_CONV ** -0.5)
    ssm_conv_b = normal((N_A_LAYERS, SSM_CONV_DIM), 0.01)
    u = jax.random.uniform(next(ks), (N_A_LAYERS, SSM_HEADS), jnp.float32)
    dt0 = jnp.exp(u * (math.log(0.1) - math.log(0.001)) + math.log(0.001))
    ssm_dt_bias = dt0 + jnp.log(-jnp.expm1(-dt0))
    ssm_a_log = jnp.log(jax.random.uniform(next(ks), (N_A_LAYERS, SSM_HEADS), jnp.float32, 1.0, 16.0))
    ssm_d = 1.0 + normal((N_A_LAYERS, SSM_HEADS), 0.1)
    ssm_norm = gain((N_A_LAYERS, SSM_D_INNER))
    ssm_out_proj = normal((N_A_LAYERS, SSM_D_INNER, D_MODEL), SSM_D_INNER ** -0.5)
    kv_norm = gain((D_MODEL,))
    w_kv = normal((D_MODEL, 2 * SB_WIDTH), D_MODEL ** -0.5)
    w_q = normal((N_B_LAYERS, D_MODEL, SB_WIDTH), D_MODEL ** -0.5)
    w_o = normal((N_B_LAYERS, SB_WIDTH, D_MODEL), SB_WIDTH ** -0.5)
    ffn_up = normal((DEPTH, D_MODEL, 2 * D_FF), D_MODEL ** -0.5)
    ffn_conv_w = normal((DEPTH, FFN_CONV, 2 * D_FF), FFN_CONV ** -0.5)
    ffn_conv_b = normal((DEPTH, 2 * D_FF), 0.01)
    ffn_down = normal((DEPTH, D_FF, D_MODEL), D_FF ** -0.5)
    ple_gate = normal((DEPTH, D_MODEL, D_MODEL), D_MODEL ** -0.5)
    ple_proj = normal((DEPTH, PLE_DIM, D_MODEL), PLE_DIM ** -0.5)
    final_norm = gain((D_MODEL,))
    return {"x": x, "p": p, "attn_norm": attn_norm, "ffn_norm": ffn_norm, "ple_norm": ple_norm,
            "ssm_in_proj": ssm_in_proj, "ssm_conv_w": ssm_conv_w, "ssm_conv_b": ssm_conv_b,
            "ssm_dt_bias": ssm_dt_bias, "ssm_a_log": ssm_a_log, "ssm_d": ssm_d,
            "ssm_norm": ssm_norm, "ssm_out_proj": ssm_out_proj, "kv_norm": kv_norm, "w_kv": w_kv,
            "w_q": w_q, "w_o": w_o, "ffn_up": ffn_up, "ffn_conv_w": ffn_conv_w,
            "ffn_conv_b": ffn_conv_b, "ffn_down": ffn_down, "ple_gate": ple_gate,
            "ple_proj": ple_proj, "final_norm": final_norm}


def reference(x, p, attn_norm, ffn_norm, ple_norm, ssm_in_proj, ssm_conv_w, ssm_conv_b,
              ssm_dt_bias, ssm_a_log, ssm_d, ssm_norm, ssm_out_proj, kv_norm, w_kv,
              w_q, w_o, ffn_up, ffn_conv_w, ffn_conv_b, ffn_down, ple_gate, ple_proj,
              final_norm):
    bsz, seq, _ = x.shape
    h = x
    k_shared = None
    v_shared = None
    for i in range(DEPTH):
        hn = rms_norm(h, attn_norm[i])
        if i < N_A_LAYERS:
            mix = mamba2_mixer(hn, ssm_in_proj[i], ssm_conv_w[i], ssm_conv_b[i], ssm_dt_bias[i],
                               ssm_a_log[i], ssm_d[i], ssm_norm[i], ssm_out_proj[i])
        else:
            j = i - N_A_LAYERS
            q = (hn @ w_q[j]).reshape(bsz, seq, SB_HEADS, SB_HEAD_DIM)
            o = stick_breaking_attention(q, k_shared, v_shared)
            mix = o.reshape(bsz, seq, SB_WIDTH) @ w_o[j]
        h = h + mix
        h = h + conv_glu_ffn(rms_norm(h, ffn_norm[i]), ffn_up[i], ffn_conv_w[i], ffn_conv_b[i], ffn_down[i])
        h = h + per_layer_embedding(h, p[i], ple_norm[i], ple_gate[i], ple_proj[i])
        if i == N_A_LAYERS - 1:
            kv = rms_norm(h, kv_norm) @ w_kv
            k_shared = kv[..., :SB_WIDTH].reshape(bsz, seq, SB_HEADS, SB_HEAD_DIM)
            v_shared = kv[..., SB_WIDTH:].reshape(bsz, seq, SB_HEADS, SB_HEAD_DIM)
    return rms_norm(h, final_norm)
```

```python
import numpy as np
import concourse.bass as bass
import concourse.mybir as mybir
from concourse.bass_utils import run_bass_kernel_spmd

F32 = mybir.dt.float32
BF16 = mybir.dt.bfloat16
AF = mybir.ActivationFunctionType
ALU = mybir.AluOpType

ENG = ["pe", "act", "dve", "pool", "sp"]
N_DMA_SEMS = 8

D = 1024
S = 2048
NT = 4
TT = 512
KC = 8
DFF = 2816
NFF = 22
PLE = 256
DEPTH = 4
NA = 2


class Sched:
    def __init__(self):
        self.ops = {e: [] for e in ENG}
        self.lastw = {}
        self.readers = {}
        self.dma_names = ["dma_%s%d" % (q, j) for q in ("sp", "pool") for j in range(N_DMA_SEMS)]
        self.dma_cnt = {n: 0 for n in self.dma_names}
        self.dma_rr = {"sp": 0, "pool": 0}

    def _need(self, deps, me, res, pos, allow_same=False):
        if res == me and not allow_same:
            return
        if deps.get(res, 0) < pos:
            deps[res] = pos

    def _deps(self, me, reads, writes, allow_same=False):
        deps = {}
        for k in reads:
            w = self.lastw.get(k)
            if w:
                self._need(deps, me, w[0], w[1], allow_same)
            if isinstance(k, tuple) and k[0] == "ps":
                for r, p in self.readers.get(k, {}).items():
                    self._need(deps, me, r, p, False)
        for k in writes:
            w = self.lastw.get(k)
            if w:
                self._need(deps, me, w[0], w[1], allow_same)
            for r, p in self.readers.get(k, {}).items():
                self._need(deps, me, r, p, allow_same)
        return deps

    def _mark(self, res, pos, reads, writes):
        for k in reads:
            self.readers.setdefault(k, {})[res] = pos
        for k in writes:
            self.lastw[k] = (res, pos)
            self.readers[k] = {}

    def op(self, eng, fn, reads=(), writes=()):
        deps = self._deps(eng, reads, writes, allow_same=(eng != "pe"))
        self.ops[eng].append(dict(fn=fn, deps=deps, dma=None))
        self._mark(eng, len(self.ops[eng]), reads, writes)

    def dma(self, eng, fn, reads=(), writes=()):
        j = self.dma_rr[eng]
        self.dma_rr[eng] = (j + 1) % N_DMA_SEMS
        res = "dma_%s%d" % (eng, j)
        k = self.dma_cnt[res] + 1
        self.dma_cnt[res] = k
        deps = self._deps(eng, reads, writes, allow_same=True)
        if k > 1:
            self._need(deps, eng, res, k - 1)
        self.ops[eng].append(dict(fn=fn, deps=deps, dma=res))
        self._mark(res, k, reads, writes)

    def fence(self):
        tail = {}
        for e in ENG:
            for pos in range(len(self.ops[e]), 0, -1):
                o = self.ops[e][pos - 1]
                if o["fn"] is not None and o["dma"] is None:
                    tail[e] = pos
                    break
        dtail = {n: c for n, c in self.dma_cnt.items() if c}
        for e in ENG:
            deps = {r: p for r, p in tail.items() if r != e}
            deps.update(dtail)
            self.ops[e].append(dict(fn=None, deps=deps, dma=None))

    def emit(self, nc):
        sig = {e: set() for e in ENG}
        for e in ENG:
            for o in self.ops[e]:
                for r, p in o["deps"].items():
                    if r in sig:
                        sig[r].add(p)
        rank = {}
        for e in ENG:
            for i, p in enumerate(sorted(sig[e])):
                rank[(e, p)] = i + 1
        from contextlib import ExitStack
        with ExitStack() as st:
            sems = {e: st.enter_context(nc.semaphore("s_" + e)) for e in ENG}
            for n in self.dma_names:
                sems[n] = st.enter_context(nc.semaphore("s_" + n))
            block = st.enter_context(nc.Block())
            engobj = {}

            def run(ename, eng):
                seen = {}
                for pos, o in enumerate(self.ops[ename], start=1):
                    for r, p in o["deps"].items():
                        val = 16 * p if r.startswith("dma") else rank[(r, p)]
                        if seen.get(r, 0) >= val:
                            continue
                        eng.wait_ge(sems[r], val)
                        seen[r] = val
                    if o["fn"] is None:
                        continue
                    ins = o["fn"](eng)
                    if o["dma"] is not None:
                        ins.then_inc(sems[o["dma"]], 16)
                    elif pos in sig[ename]:
                        ins.then_inc(sems[ename], 1)
                if ename == "sp":
                    for n, c in self.dma_cnt.items():
                        if c and seen.get(n, 0) < 16 * c:
                            eng.wait_ge(sems[n], 16 * c)

            @block.tensor
            def _(e):
                run("pe", e)

            @block.scalar
            def _(e):
                run("act", e)

            @block.vector
            def _(e):
                run("dve", e)

            @block.gpsimd
            def _(e):
                run("pool", e)

            @block.sync
            def _(e):
                run("sp", e)


def _fm(v):
    v = np.asarray(v, np.float32)
    return np.ascontiguousarray(v.reshape(-1, 128).T)


class VecLayout:
    def __init__(self):
        self.off = {}
        self.n = 0

    def add(self, name, ncols):
        self.off[name] = self.n
        self.n += ncols


def vec_layout():
    L = VecLayout()
    for i in range(DEPTH):
        L.add(("attn_norm", i), 8)
        L.add(("ffn_norm", i), 8)
        L.add(("ple_norm", i), 8)
        for k in range(3):
            L.add(("ffn_cw", i, k), 44)
        L.add(("ffn_cb", i), 44)
    L.add("kv_norm", 8)
    L.add("final_norm", 8)
    for i in range(NA):
        for k in range(4):
            L.add(("ssm_cw", i, k), 24)
        L.add(("ssm_cb", i), 24)
        L.add(("ssm_dfm", i), 16)
    return L


def pack_vecs(inp):
    L = vec_layout()
    out = np.zeros((128, L.n), np.float32)

    def put(name, v):
        a = _fm(v)
        out[:, L.off[name]:L.off[name] + a.shape[1]] = a

    for i in range(DEPTH):
        put(("attn_norm", i), inp["attn_norm"][i])
        put(("ffn_norm", i), inp["ffn_norm"][i])
        put(("ple_norm", i), inp["ple_norm"][i])
        for k in range(3):
            put(("ffn_cw", i, k), inp["ffn_conv_w"][i][k])
        put(("ffn_cb", i), inp["ffn_conv_b"][i])
    put("kv_norm", inp["kv_norm"])
    put("final_norm", inp["final_norm"])
    for i in range(NA):
        for k in range(4):
            put(("ssm_cw", i, k), inp["ssm_conv_w"][i][k])
        put(("ssm_cb", i), inp["ssm_conv_b"][i])
        put(("ssm_dfm", i), np.repeat(np.asarray(inp["ssm_d"][i], np.float32), 64))
    return out


def pack_bvecs(inp):
    out = np.zeros((128, NA * 3072), np.float32)
    for i in range(NA):
        o = i * 3072
        out[:, o:o + 512] = np.tile(np.asarray(inp["ssm_dt_bias"][i], np.float32), 16)[None, :]
        out[:, o + 512:o + 1024] = np.tile(np.asarray(inp["ssm_a_log"][i], np.float32), 16)[None, :]
        out[:, o + 1024:o + 3072] = np.asarray(inp["ssm_norm"][i], np.float32)[None, :]
    return out


def make_consts():
    k = np.arange(128)
    c = np.zeros((128, 768), np.float32)
    am = ((k[:, None] + k[None, :]) >= 128).astype(np.float32)
    c[:, 512:640] = am
    c[:, 640:768] = (am - 1.0) * 30000.0
    c[:, 0:128] = (k[:, None] <= k[None, :])
    c[:, 128:256] = (k[:, None] > k[None, :])
    c[:, 256:384] = 1.0
    c[:, 384:512] = np.eye(128, dtype=np.float32)
    return c


class Prog:
    def __init__(self, nseq, cfg):
        self.nseq = nseq
        self.cfg = cfg
        self.nc = nc = bass.Bass("TRN2", target_bir_lowering=False)
        self.S = Sched()
        self.VL = vec_layout()
        di = lambda name, shape: nc.dram_tensor(name, shape, F32, kind="ExternalInput").ap()
        self.xT = di("xT", [nseq, D, S])
        self.pT = di("pT", [DEPTH, nseq, PLE, S])
        self.vecs_d = di("vecs", [128, self.VL.n])
        self.ffn_up = di("ffn_up", [DEPTH, D, 2 * DFF])
        self.ffn_down = di("ffn_down", [DEPTH, DFF, D])
        self.ple_gate = di("ple_gate", [DEPTH, D, D])
        self.ple_proj = di("ple_proj", [DEPTH, PLE, D])
        self.consts_d = di("consts", [128, 768])
        self.bvecs_d = di("bvecs", [128, NA * 3072])
        self.ssm_in = di("ssm_in_proj", [NA, D, 5152])
        self.ssm_out = di("ssm_out_proj", [NA, 2048, D])
        self.w_kv = di("w_kv", [D, 2048])
        self.w_q = di("w_q", [2, D, D])
        self.w_o = di("w_o", [2, D, D])
        self.kTd = nc.dram_tensor("kTd", [8, 128, S], BF16).ap()
        self.Vd = nc.dram_tensor("Vd", [8, 2, 128, 16 * 64], BF16).ap()
        self.outT = nc.dram_tensor("outT", [nseq, D, S], F32, kind="ExternalOutput").ap()

        A = nc.alloc_sbuf_tensor
        self.hT = A("hT", [128, KC, S], F32)
        self.hn = A("hn", [128, KC, S], BF16)
        self.vecs = A("vecsb", [128, self.VL.n], F32)
        self.wbuf = [A("wbuf%d" % i, [128, 6144], BF16) for i in range(2)]
        self.ones = A("ones", [128, 128], BF16)
        self.arena = A("arena", [128, 19968], F32)
        self.cst = A("cst", [128, 768], F32)
        self.amask = self.cst[:, 512:640]
        self.negmask = self.cst[:, 640:768]
        self.onesb = A("onesb", [128, 512], BF16)
        self.identb = A("identb", [128, 128], BF16)
        self.tri = self.cst[:, 0:128]
        self.strictT = self.cst[:, 128:256]
        self.onesf = self.cst[:, 256:384]
        self.ps = [nc.alloc_psum_tensor("ps%d" % i, [128, 512], F32) for i in range(8)]
        self.ps_rr = 0
        self.ps_lo = 0
        self.w_rr = 0
        print("sbuf remaining", nc.sbuf_bytes_remaining)

    def vcol(self, name, c):
        o = self.VL.off[name] + c
        return self.vecs[:, o:o + 1]

    def next_ps(self):
        lo = self.ps_lo
        i = self.ps_rr
        self.ps_rr = (i + 1) % (8 - lo)
        return lo + i

    def carve(self, off_f32, n, dtype, shape=None):
        ap = self.arena[:, off_f32:off_f32 + n]
        if dtype == BF16:
            ap = ap.bitcast(BF16)
        return ap

    def load_w(self, wd, rows, c0, ncols, eng="pool"):
        r0, nk = rows
        b = self.w_rr
        self.w_rr = 1 - b
        dst = self.wbuf[b][:, 0:nk * ncols].rearrange("p (k m) -> p k m", k=nk)
        src = wd[r0:r0 + nk * 128, c0:c0 + ncols].rearrange("(k p) m -> p k m", p=128)
        self.S.dma(eng, lambda e: e.dma_start(out=dst, in_=src), reads=(), writes=[("w", b)])
        return b, dst

    def rmsnorm(self, gain_name, out_bf16=True, out_ap_fn=None, eps=1e-6, reverse=False):
        S_ = self.S
        sq = self.carve(0, 2048, BF16)
        rr = [self.carve(2048, 512, F32), self.carve(2560, 512, F32)]
        for tt in range(NT):
            t0 = tt * TT
            pb = self.next_ps()
            ps = self.ps[pb]
            for c in range(KC):
                o = sq[:, c * 512:(c + 1) * 512]
                i = self.hT[:, c, t0:t0 + TT]
                S_.op("act", lambda e, o=o, i=i: e.activation(o, i, AF.Square),
                      reads=[("h", c, tt)], writes=[("sq", c)])
            for c in range(KC):
                r_ = sq[:, c * 512:(c + 1) * 512]
                S_.op("pe", lambda e, ps=ps, r_=r_, c=c: e.matmul(ps[:, :], self.ones[:, :], r_, start=(c == 0), stop=(c == KC - 1)),
                      reads=[("sq", c), "ones"], writes=[("ps", pb)])
            r = rr[tt % 2]
            rk = ("rstd", tt % 2)
            S_.op("act", lambda e, r=r, ps=ps: e.activation(r, ps[:, :], AF.Sqrt, bias=eps, scale=1.0 / D),
                  reads=[("ps", pb)], writes=[rk])
            S_.op("dve", lambda e, r=r: e.reciprocal(r, r), reads=[rk], writes=[rk])
            for c in range(KC):
                if reverse:
                    o = self.hn[:, c, (3 - tt) * TT:(4 - tt) * TT][:, ::-1]
                    wk = ("hn", c, 3 - tt)
                elif out_ap_fn is None:
                    o = self.hn[:, c, t0:t0 + TT]
                    wk = ("hn", c, tt)
                else:
                    o, wk = out_ap_fn(c, tt)
                i = self.hT[:, c, t0:t0 + TT]
                g = self.vcol(gain_name, c)
                S_.op("dve", lambda e, o=o, i=i, g=g, r=r: e.scalar_tensor_tensor(o, i, g, r, ALU.mult, ALU.mult),
                      reads=[("h", c, tt), rk, "vecs"], writes=[wk])

    def linear(self, wd, k_rows, col_groups, rhs_fn, epi_fn, weng="pool"):
        S_ = self.S
        r0, nk = k_rows

        def load(gi):
            cols = col_groups[gi]
            b = self.w_rr
            self.w_rr = 1 - b
            n = len(cols)
            dst = self.wbuf[b][:, 0:nk * n * 128].rearrange("p (k m) -> p k m", k=nk)
            contiguous = all(cols[i + 1] == cols[i] + 128 for i in range(n - 1))
            if contiguous:
                src = wd[r0:r0 + nk * 128, cols[0]:cols[0] + n * 128].rearrange("(k p) m -> p k m", p=128)
                S_.dma(weng, lambda e, dst=dst, src=src: e.dma_start(out=dst, in_=src), reads=(), writes=[("w", b)])
            else:
                for i, c0 in enumerate(cols):
                    src = wd[r0:r0 + nk * 128, c0:c0 + 128].rearrange("(k p) m -> p k m", p=128)
                    d2 = dst[:, :, i * 128:(i + 1) * 128]
                    S_.dma(weng, lambda e, d2=d2, src=src: e.dma_start(out=d2, in_=src), reads=(),
                           writes=[("w", b, i)] + ([("w", b)] if i == 0 else []))
            return b, dst, (not contiguous)

        nxt = load(0)
        for gi, cols in enumerate(col_groups):
            b, wt, split = nxt
            if gi + 1 < len(col_groups):
                nxt = load(gi + 1)
            for mi in range(len(cols)):
                wkeys = [("w", b)] + ([("w", b, mi)] if split else [])
                for tt in range(NT):
                    pb = self.next_ps()
                    ps = self.ps[pb]
                    for k in range(nk):
                        rhs, rkey = rhs_fn(k, tt)
                        lhsT = wt[:, k, mi * 128:(mi + 1) * 128]
                        S_.op("pe", lambda e, ps=ps, lhsT=lhsT, rhs=rhs, k=k: e.matmul(ps[:, :], lhsT, rhs, start=(k == 0), stop=(k == nk - 1)),
                              reads=wkeys + [rkey], writes=[("ps", pb)])
                    epi_fn(gi, mi, tt, ps, ("ps", pb))

    def ffn(self, li):
        S_ = self.S
        self.rmsnorm(("ffn_norm", li))
        S_.fence()
        g = self.carve(0, 11264, BF16)
        Ug = self.carve(11264, 2052, F32)
        Uv = self.carve(11264 + 2052, 2052, F32)
        acc0 = 11264 + 2 * 2052
        accs = [[self.carve(acc0 + (2 * p + q) * 512, 512, F32) for q in range(2)] for p in range(2)]
        S_.op("dve", lambda e: e.memset(Ug[:, 0:2], 0.0), writes=[("U", 0, -1)])
        S_.op("dve", lambda e: e.memset(Uv[:, 0:2], 0.0), writes=[("U", 1, -1)])
        U = [Ug, Uv]
        wup = self.ffn_up[li]
        wdn = self.ffn_down[li]
        cnt = [0]
        for half in range(2):
            j0 = half * 11
            groups = []
            for jj in range(0, 11, 2):
                js = [j0 + jj] + ([j0 + jj + 1] if jj + 1 < 11 else [])
                cols = []
                for j in js:
                    cols += [j * 128, DFF + j * 128]
                groups.append(cols)

            def rhs_fn(k, tt):
                return self.hn[:, k, tt * TT:(tt + 1) * TT], ("hn", k, tt)

            self._ffn_up_group(wup, groups, rhs_fn, j0, li, U, accs, g)

            def rhs2(k, tt):
                return g[:, k * S + tt * TT: k * S + (tt + 1) * TT], ("g", k, tt)

            def epi2(gi, mi, tt, ps, pkey):
                c = gi * 4 + mi
                hv = self.hT[:, c, tt * TT:(tt + 1) * TT]
                S_.op("dve", lambda e: e.tensor_tensor(hv, ps[:, :], hv, ALU.add),
                      reads=[pkey, ("h", c, tt)], writes=[("h", c, tt)])

            self.linear(wdn, (half * 11 * 128, 11), [[0, 128, 256, 384], [512, 640, 768, 896]], rhs2, epi2)
        S_.fence()

    def _ffn_up_group(self, wup, groups, rhs_fn, j0, li, U, accs, g):
        S_ = self.S
        nk = KC

        def load(gi):
            cols = groups[gi]
            b = self.w_rr
            self.w_rr = 1 - b
            n = len(cols)
            dst = self.wbuf[b][:, 0:nk * n * 128].rearrange("p (k m) -> p k m", k=nk)
            npair = n // 2
            srcg = wup[:, cols[0]:cols[0] + npair * 128].rearrange("(k p) m -> p k m", p=128)
            srcv = wup[:, cols[1]:cols[1] + npair * 128].rearrange("(k p) m -> p k m", p=128)
            dg = dst[:, :, 0:npair * 128]
            dv = dst[:, :, npair * 128:2 * npair * 128]
            S_.dma("pool", lambda e: e.dma_start(out=dg, in_=srcg), reads=(), writes=[("w", b), ("w", b, 0)])
            S_.dma("pool", lambda e: e.dma_start(out=dv, in_=srcv), reads=(), writes=[("w", b, 1)])
            return b, dst, npair

        nxt = load(0)
        it = 0
        for gi in range(len(groups)):
            b, wt, npair = nxt
            if gi + 1 < len(groups):
                nxt = load(gi + 1)
            for pj in range(npair):
                j = j0 + gi * 2 + pj
                jj = j - j0
                for tt in range(NT):
                    t0 = tt * TT
                    par = it % 2
                    it += 1
                    pbs = []
                    for q in range(2):
                        pb = self.next_ps()
                        ps = self.ps[pb]
                        pbs.append(pb)
                        for k in range(nk):
                            rhs, rkey = rhs_fn(k, tt)
                            lhsT = wt[:, k, (q * npair + pj) * 128:(q * npair + pj + 1) * 128]
                            S_.op("pe", lambda e, ps=ps, lhsT=lhsT, rhs=rhs, k=k: e.matmul(ps[:, :], lhsT, rhs, start=(k == 0), stop=(k == nk - 1)),
                                  reads=[("w", b), ("w", b, q), rkey], writes=[("ps", pb)])
                    for q in range(2):
                        pb = pbs[q]
                        ps = self.ps[pb]
                        pkey = ("ps", pb)
                        acc = accs[par][q]
                        ch = q * NFF + j
                        Uq = U[q]
                        w2 = self.vcol(("ffn_cw", li, 2), ch)
                        w1 = self.vcol(("ffn_cw", li, 1), ch)
                        w0 = self.vcol(("ffn_cw", li, 0), ch)
                        bb = self.vcol(("ffn_cb", li), ch)
                        S_.op("act", lambda e, Uq=Uq, ps=ps, t0=t0: e.activation(Uq[:, 2 + t0:2 + t0 + TT], ps[:, :], AF.Identity),
                              reads=[pkey], writes=[("U", q, tt)])
                        S_.op("act", lambda e, acc=acc, ps=ps, bb=bb, w2=w2: e.activation(acc, ps[:, :], AF.Identity, bias=bb, scale=w2),
                              reads=[pkey, "vecs"], writes=[("acc", par, q)])
                        S_.op("dve", lambda e, acc=acc, Uq=Uq, w1=w1, t0=t0: e.scalar_tensor_tensor(acc, Uq[:, 1 + t0:1 + t0 + TT], w1, acc, ALU.mult, ALU.add),
                              reads=[("U", q, tt), ("U", q, tt - 1), "vecs"], writes=[("acc", par, q)])
                        S_.op("dve", lambda e, acc=acc, Uq=Uq, w0=w0, t0=t0: e.scalar_tensor_tensor(acc, Uq[:, t0:t0 + TT], w0, acc, ALU.mult, ALU.add),
                              reads=[("U", q, tt), ("U", q, tt - 1), "vecs"], writes=[("acc", par, q)])
                    ag, av = accs[par]
                    S_.op("act", lambda e, ag=ag: e.activation(ag, ag, AF.Silu), reads=[("acc", par, 0)], writes=[("acc", par, 0)])
                    go = g[:, jj * S + t0: jj * S + t0 + TT]
                    S_.op("pool", lambda e, go=go, ag=ag, av=av: e.tensor_tensor(go, ag, av, ALU.mult),
                          reads=[("acc", par, 0), ("acc", par, 1)], writes=[("g", jj, tt)])

    def mamba(self, li, si):
        S_ = self.S
        A = self.carve
        self.rmsnorm(("attn_norm", li))
        S_.fence()
        win = self.ssm_in[li]
        wout = self.ssm_out[li]
        bv = self.bvecs_d
        v3 = lambda ap, c: ap.rearrange("p (c h) -> p c h", c=c)
        dt = A(0, 512, F32); adt = A(512, 512, F32); acs = A(1024, 512, F32)
        dS = A(1536, 512, F32); Ea = A(2048, 512, F32); Etot = A(2560, 512, F32)
        dtb = A(3072, 512, F32); ea = A(3584, 512, F32)
        normw = A(4096, 512, F32)
        wdt = A(4608, 128, BF16).rearrange("p (k m) -> p k m", k=8)
        diagD = A(4736, 256, BF16)
        xc = A(4992, 4096, BF16)
        BT = A(9088, 1024, BF16)
        CT = A(10112, 1024, BF16)
        T0 = 11136
        U = A(T0, 2052, F32)
        acc = [A(T0 + 2052 + i * 512, 512, F32) for i in range(2)]
        o = T0
        xdt = [A(o + i * 256, 256, BF16) for i in range(2)]; o += 512
        xsc = [A(o + i * 256, 256, BF16) for i in range(2)]; o += 512
        Btok = [A(o + i * 64, 64, BF16) for i in range(2)]; o += 128
        CBm = [A(o + i * 128, 128, F32) for i in range(2)]; o += 256
        rhsD = A(o, 1024, F32); o += 1024
        MT = [A(o + i * 512, 512, BF16) for i in range(2)]; o += 1024
        zs = A(o, 512, F32); o += 512
        t1 = [A(o + i * 512, 512, F32) for i in range(2)]; o += 1024
        gn = [A(o + i * 256, 256, BF16) for i in range(2)]; o += 512
        gnT = [A(o + i * 1024, 1024, BF16) for i in range(2)]; o += 2048
        state = A(o, 512, F32); o += 512
        state_bf = A(o, 256, BF16); o += 256
        ssb = [A(o + i * 2, 1, F32) for i in range(2)]; o += 4
        rsb = [A(o + i * 2, 1, F32) for i in range(2)]; o += 4
        assert o <= 19968, o

        S_.dma("sp", lambda e: e.dma_start(out=dtb, in_=bv[:, li * 3072:li * 3072 + 512]), writes=["dtb"])
        S_.dma("sp", lambda e: e.dma_start(out=ea, in_=bv[:, li * 3072 + 512:li * 3072 + 1024]), writes=["ea"])
        wsrc = win[:, 5120:5152].rearrange("(k p) m -> p k m", p=128)
        S_.dma("pool", lambda e: e.dma_start(out=wdt, in_=wsrc), writes=["wdt"])
        pb = self.next_ps(); ps = self.ps[pb]
        for c in range(16):
            for k in range(KC):
                S_.op("pe", lambda e, c=c, k=k, ps=ps: e.matmul(ps[:, c * 32:(c + 1) * 32], self.hn[:, k, c * 128:(c + 1) * 128], wdt[:, k, :], start=(k == 0), stop=(k == KC - 1)),
                      reads=["wdt", ("hn", k, c // 4)], writes=[("ps", pb)])
        S_.op("dve", lambda e, ps=ps: e.tensor_tensor(dt, ps[:, :], dtb, ALU.add), reads=[("ps", pb), "dtb"], writes=["dt"])
        S_.op("act", lambda e: e.activation(dt, dt, AF.Exp), reads=["dt"], writes=["dt"])
        S_.op("act", lambda e: e.activation(dt, dt, AF.Ln, bias=1.0), reads=["dt"], writes=["dt"])
        S_.op("act", lambda e: e.activation(ea, ea, AF.Exp), reads=["ea"], writes=["ea"])
        S_.op("dve", lambda e: e.scalar_tensor_tensor(adt, dt, -1.0, ea, ALU.mult, ALU.mult), reads=["dt", "ea"], writes=["adt"])
        pa = self.next_ps(); psA = self.ps[pa]
        pbb = self.next_ps(); psB = self.ps[pbb]
        S_.op("pe", lambda e: e.matmul(psA[:, :], self.tri, adt, start=True, stop=True), reads=["cst", "adt"], writes=[("ps", pa)])
        S_.op("pe", lambda e: e.matmul(psB[:, :], self.onesf, adt, start=True, stop=True), reads=["cst", "adt"], writes=[("ps", pbb)])
        S_.op("act", lambda e: e.activation(acs, psA[:, :], AF.Identity), reads=[("ps", pa)], writes=["acs"])
        S_.op("act", lambda e: e.activation(Ea, psA[:, :], AF.Exp), reads=[("ps", pa)], writes=["Ea"])
        S_.op("act", lambda e: e.activation(Etot, psB[:, :], AF.Exp), reads=[("ps", pbb)], writes=["Etot"])
        S_.op("dve", lambda e: e.tensor_tensor(dS, psB[:, :], acs, ALU.subtract), reads=[("ps", pbb), "acs"], writes=["dS"])
        S_.op("act", lambda e: e.activation(dS, dS, AF.Exp), reads=["dS"], writes=["dS"])

        def do_group(g):
            S_.fence()
            b = self.w_rr
            self.w_rr = 1 - b
            wt = self.wbuf[b][:, 0:6144].rearrange("p (k m) -> p k m", k=8)
            for i, (c0, n, d0) in enumerate([(2048 + 512 * g, 512, 0), (4096 + 128 * g, 128, 512), (4608 + 128 * g, 128, 640)]):
                src = win[:, c0:c0 + n].rearrange("(k p) m -> p k m", p=128)
                dst = wt[:, :, d0:d0 + n]
                S_.dma("pool", lambda e, dst=dst, src=src: e.dma_start(out=dst, in_=src),
                       writes=[("w", b, i)] + ([("w", b)] if i == 0 else []))
            S_.dma("sp", lambda e, g=g: e.dma_start(out=normw, in_=bv[:, li * 3072 + 1024 + 512 * g:li * 3072 + 1536 + 512 * g]), writes=["normw"])
            S_.op("dve", lambda e: e.memset(U[:, 0:3], 0.0), writes=[("U", -1)])
            it = 0
            for j in range(6):
                ch = 4 * g + j if j < 4 else (16 + g if j == 4 else 20 + g)
                wi = 0 if j < 4 else j - 3
                for tt in range(NT):
                    t0 = tt * TT
                    if j < 4:
                        dest = xc[:, j * S + t0:j * S + t0 + TT]; dkey = ("xc", j, tt)
                    elif j == 4:
                        dest = BT[:, t0:t0 + TT]; dkey = ("BT", tt)
                    else:
                        dest = CT[:, t0:t0 + TT]; dkey = ("CT", tt)
                    pb = self.next_ps(); ps = self.ps[pb]
                    for k in range(KC):
                        S_.op("pe", lambda e, ps=ps, k=k, j=j, t0=t0: e.matmul(ps[:, :], wt[:, k, j * 128:(j + 1) * 128], self.hn[:, k, t0:t0 + TT], start=(k == 0), stop=(k == KC - 1)),
                              reads=[("w", b), ("w", b, wi), ("hn", k, tt)], writes=[("ps", pb)])
                    par = it % 2
                    it += 1
                    a_ = acc[par]
                    w3 = self.vcol(("ssm_cw", li, 3), ch); w2 = self.vcol(("ssm_cw", li, 2), ch)
                    w1 = self.vcol(("ssm_cw", li, 1), ch); w0 = self.vcol(("ssm_cw", li, 0), ch)
                    bb = self.vcol(("ssm_cb", li), ch)
                    S_.op("act", lambda e, ps=ps, t0=t0: e.activation(U[:, 3 + t0:3 + t0 + TT], ps[:, :], AF.Identity), reads=[("ps", pb)], writes=[("U", tt)])
                    S_.op("act", lambda e, ps=ps, a_=a_, bb=bb, w3=w3: e.activation(a_, ps[:, :], AF.Identity, bias=bb, scale=w3),
                          reads=[("ps", pb), "vecs"], writes=[("macc", par)])
                    for sh, w in ((2, w2), (1, w1), (0, w0)):
                        S_.op("dve", lambda e, a_=a_, w=w, sh=sh, t0=t0: e.scalar_tensor_tensor(a_, U[:, sh + t0:sh + t0 + TT], w, a_, ALU.mult, ALU.add),
                              reads=[("U", tt), ("U", tt - 1), "vecs"], writes=[("macc", par)])
                    S_.op("act", lambda e, dest=dest, a_=a_: e.activation(dest, a_, AF.Silu), reads=[("macc", par)], writes=[dkey])

            S_.fence()
            bz = self.w_rr
            bo = 1 - bz
            Wz = self.wbuf[bz][:, 0:4096].rearrange("p (k m) -> p k m", k=8)
            Wo = self.wbuf[bo][:, 0:4096].rearrange("p (k m) -> p k m", k=4)
            srcz = win[:, 512 * g:512 * g + 512].rearrange("(k p) m -> p k m", p=128)
            srco = wout[512 * g:512 * g + 512, :].rearrange("(k p) m -> p k m", p=128)
            S_.dma("pool", lambda e: e.dma_start(out=Wz, in_=srcz), writes=[("w", bz)])
            S_.dma("pool", lambda e: e.dma_start(out=Wo, in_=srco), writes=[("w", bo)])
            for j in range(4):
                dj = diagD[:, j * 128:(j + 1) * 128]
                sc = self.vcol(("ssm_dfm", li), 4 * g + j)
                S_.op("dve", lambda e, dj=dj, sc=sc: e.tensor_scalar(dj, self.identb[:, :], sc, None, ALU.mult), reads=["identb", "vecs"], writes=[("diagD", j)])
            S_.op("dve", lambda e: e.memset(state, 0.0), writes=["state"])
            S_.op("pool", lambda e: e.memset(state_bf, 0.0), writes=["state_bf"])
            h8 = slice(8 * g, 8 * g + 8)
            bc64 = lambda ap, c: v3(ap, 16)[:, c, h8].unsqueeze(2).to_broadcast([128, 8, 64])
            r64 = lambda ap: ap.rearrange("p (r d) -> p r d", r=8)

            def stageA(c):
                par = c % 2
                tt = c // 4
                l0 = c * 128
                pbx = self.next_ps(); psx = self.ps[pbx][:, :].bitcast(BF16)
                for j in range(4):
                    S_.op("pe", lambda e, j=j: e.transpose(psx[:, j * 128:(j + 1) * 128], xc[:, j * S + l0:j * S + l0 + 128], self.identb[:, :]),
                          reads=[("xc", j, tt), "identb"], writes=[("ps", pbx)])
                S_.op("pe", lambda e: e.transpose(psx[:, 512:640], BT[:, l0:l0 + 128], self.identb[:, :]), reads=[("BT", tt), "identb"], writes=[("ps", pbx)])
                S_.op("dve", lambda e: e.tensor_tensor(r64(xdt[par]), r64(psx[:, 0:512]), bc64(dt, c), ALU.mult), reads=[("ps", pbx), "dt"], writes=[("xdt", par)])
                S_.op("pool", lambda e: e.tensor_tensor(r64(xsc[par]), r64(xdt[par]), bc64(dS, c), ALU.mult), reads=[("xdt", par), "dS"], writes=[("xsc", par)])
                S_.op("act", lambda e: e.activation(Btok[par], psx[:, 512:640], AF.Identity), reads=[("ps", pbx)], writes=[("Btok", par)])
                pbc = self.next_ps(); psc = self.ps[pbc]
                S_.op("pe", lambda e: e.matmul(psc[:, 0:128], BT[:, l0:l0 + 128], CT[:, l0:l0 + 128], start=True, stop=True),
                      reads=[("BT", tt), ("CT", tt)], writes=[("ps", pbc)])
                S_.op("dve", lambda e: e.tensor_tensor(CBm[par], psc[:, 0:128], self.tri, ALU.mult), reads=[("ps", pbc), "cst"], writes=[("CBm", par)])
                r128 = lambda ap: ap.rearrange("p (r d) -> p r d", r=8)
                S_.op("dve", lambda e: e.tensor_tensor(r128(rhsD), self.tri.unsqueeze(1).to_broadcast([128, 8, 128]),
                                                       v3(adt, 16)[:, c, h8].unsqueeze(2).to_broadcast([128, 8, 128]), ALU.mult),
                      reads=["cst", "adt"], writes=["rhsD"])
                for i in range(2):
                    pbd = self.next_ps(); psd = self.ps[pbd]
                    S_.op("pe", lambda e, psd=psd, i=i: e.matmul(psd[:, :], self.strictT, rhsD[:, i * 512:(i + 1) * 512], start=True, stop=True),
                          reads=["cst", "rhsD"], writes=[("ps", pbd)])
                    S_.op("act", lambda e, psd=psd, i=i: e.activation(MT[par][:, i * 512:(i + 1) * 512], psd[:, :], AF.Exp), reads=[("ps", pbd)], writes=[("MT", par, i)])
                S_.op("dve", lambda e: e.tensor_tensor(r128(MT[par]), r128(MT[par]), CBm[par].unsqueeze(1).to_broadcast([128, 8, 128]), ALU.mult),
                      reads=[("MT", par, 0), ("MT", par, 1), ("CBm", par)], writes=[("MT", par, 0), ("MT", par, 1)])
                pbz = self.next_ps(); psz = self.ps[pbz]
                for k in range(KC):
                    S_.op("pe", lambda e, k=k: e.matmul(psz[:, :], self.hn[:, k, l0:l0 + 128], Wz[:, k, :], start=(k == 0), stop=(k == KC - 1)),
                          reads=[("w", bz), ("hn", k, tt)], writes=[("ps", pbz)])
                S_.op("act", lambda e: e.activation(zs, psz[:, :], AF.Silu), reads=[("ps", pbz)], writes=["zs"])
                pby = self.next_ps(); psy = self.ps[pby]
                for j in range(4):
                    S_.op("pe", lambda e, j=j: e.matmul(psy[:, j * 128:(j + 1) * 128], xc[:, j * S + l0:j * S + l0 + 128], diagD[:, j * 128:(j + 1) * 128], start=True, stop=False),
                          reads=[("xc", j, tt), ("diagD", j)], writes=[("ps", pby)])
                    for r in (2 * j, 2 * j + 1):
                        S_.op("pe", lambda e, r=r, j=j: e.matmul(psy[:, r * 64:(r + 1) * 64], MT[par][:, r * 128:(r + 1) * 128], xdt[par][:, r * 64:(r + 1) * 64], start=False, stop=(r == 2 * j + 1)),
                              reads=[("MT", par, 0), ("MT", par, 1), ("xdt", par)], writes=[("ps", pby)])
                pbo = self.next_ps(); pso = self.ps[pbo]
                S_.op("pe", lambda e: e.matmul(pso[:, :], CT[:, l0:l0 + 128], state_bf, start=True, stop=True), reads=[("CT", tt), "state_bf"], writes=[("ps", pbo)])
                pbs = self.next_ps(); pss = self.ps[pbs]
                S_.op("pe", lambda e: e.matmul(pss[:, :], Btok[par], xsc[par], start=True, stop=True), reads=[("Btok", par), ("xsc", par)], writes=[("ps", pbs)])
                T = t1[par]
                S_.op("dve", lambda e: e.tensor_tensor(r64(T), r64(pso[:, :]), bc64(Ea, c), ALU.mult), reads=[("ps", pbo), "Ea"], writes=[("t1", par)])
                S_.op("dve", lambda e: e.tensor_tensor(T, T, psy[:, :], ALU.add), reads=[("ps", pby), ("t1", par)], writes=[("t1", par)])
                S_.op("dve", lambda e: e.tensor_tensor(T, T, zs, ALU.mult), reads=["zs", ("t1", par)], writes=[("t1", par)])
                S_.op("act", lambda e: e.activation(zs, T, AF.Square, accum_out=ssb[par]), reads=[("t1", par)], writes=["zs", ("ss", par)])
                S_.op("act", lambda e: e.activation(rsb[par], ssb[par], AF.Sqrt, bias=1e-5, scale=1.0 / 512), reads=[("ss", par)], writes=[("rs", par)])
                S_.op("dve", lambda e: e.reciprocal(rsb[par], rsb[par]), reads=[("rs", par)], writes=[("rs", par)])
                S_.op("dve", lambda e: e.scalar_tensor_tensor(gn[par], T, rsb[par], normw, ALU.mult, ALU.mult), reads=[("t1", par), ("rs", par), "normw"], writes=[("gn", par)])
                S_.op("dve", lambda e: e.tensor_tensor(r64(state), r64(state), bc64(Etot, c), ALU.mult), reads=["state", "Etot"], writes=["state"])
                S_.op("dve", lambda e: e.tensor_tensor(state, state, pss[:, :], ALU.add), reads=["state", ("ps", pbs)], writes=["state"])
                S_.op("pool", lambda e: e.tensor_copy(state_bf, state), reads=["state"], writes=["state_bf"])

            def stageB(c):
                par = c % 2
                tt = c // 4
                q = c % 4
                G = gnT[tt % 2].rearrange("p (j t) -> p j t", j=4)
                pbt = self.next_ps(); pst = self.ps[pbt][:, :].bitcast(BF16)
                for j in range(4):
                    S_.op("pe", lambda e, j=j: e.transpose(pst[:, j * 128:(j + 1) * 128], gn[par][:, j * 128:(j + 1) * 128], self.identb[:, :]),
                          reads=[("gn", par), "identb"], writes=[("ps", pbt)])
                S_.op("act", lambda e: e.activation(G[:, :, q * 128:(q + 1) * 128], pst[:, 0:512].rearrange("p (j t) -> p j t", j=4), AF.Identity),
                      reads=[("ps", pbt)], writes=[("gnT", tt % 2, q)])
                if q == 3:
                    for m in range(KC):
                        pbm = self.next_ps(); psm = self.ps[pbm]
                        for k in range(4):
                            S_.op("pe", lambda e, m=m, k=k, psm=psm: e.matmul(psm[:, :], Wo[:, k, m * 128:(m + 1) * 128], G[:, k, :], start=(k == 0), stop=(k == 3)),
                                  reads=[("w", bo)] + [("gnT", tt % 2, qq) for qq in range(4)], writes=[("ps", pbm)])
                        hv = self.hT[:, m, tt * TT:(tt + 1) * TT]
                        S_.op("dve", lambda e, hv=hv, psm=psm: e.tensor_tensor(hv, psm[:, :], hv, ALU.add), reads=[("ps", pbm), ("h", m, tt)], writes=[("h", m, tt)])

            for c in range(17):
                if c < 16:
                    stageA(c)
                if c >= 1:
                    stageB(c - 1)

        for g in range(4):
            do_group(g)
        S_.fence()

    def kv_stage(self, si):
        S_ = self.S
        A = self.carve
        self.rmsnorm("kv_norm", reverse=True)
        S_.fence()
        stg = [A(i * 256, 256, BF16) for i in range(4)]
        it = [0]

        def rhs_fn(k, tt):
            return self.hn[:, k, tt * TT:(tt + 1) * TT], ("hn", k, tt)

        def epi(gi, mi, tt, ps, pkey):
            hp = gi * 4 + mi
            par = it[0] % 4
            it[0] += 1
            sb = stg[par]
            S_.op("act", lambda e: e.activation(sb, ps[:, :], AF.Identity), reads=[pkey], writes=[("stg", par)])
            dst = self.kTd[hp, :, tt * TT:(tt + 1) * TT]
            S_.dma("sp", lambda e: e.dma_start(out=dst, in_=sb), reads=[("stg", par)], writes=[("kTd", hp)])

        self.linear(self.w_kv, (0, 8), [[0, 128, 256, 384], [512, 640, 768, 896]], rhs_fn, epi)
        Wv = []
        for cg in range(2):
            b = cg
            wt = self.wbuf[b][:, 0:4096].rearrange("p (k m) -> p k m", k=8)
            src = self.w_kv[:, 1024 + cg * 512:1024 + (cg + 1) * 512].rearrange("(k p) m -> p k m", p=128)
            S_.dma("pool", lambda e, wt=wt, src=src: e.dma_start(out=wt, in_=src), writes=[("w", b)])
            Wv.append(wt)
        for jb in range(16):
            for cg in range(2):
                pb = self.next_ps(); ps = self.ps[pb]
                for k in range(KC):
                    S_.op("pe", lambda e, ps=ps, k=k, jb=jb, cg=cg: e.matmul(ps[:, :], self.hn[:, k, jb * 128:(jb + 1) * 128], Wv[cg][:, k, :], start=(k == 0), stop=(k == KC - 1)),
                          reads=[("w", cg), ("hn", k, jb // 4)], writes=[("ps", pb)])
                par = it[0] % 4
                it[0] += 1
                sb = stg[par]
                S_.op("act", lambda e, sb=sb, ps=ps: e.activation(sb, ps[:, :], AF.Identity), reads=[("ps", pb)], writes=[("stg", par)])
                dst = self.Vd[cg * 4:(cg + 1) * 4, :, :, jb * 64:(jb + 1) * 64].rearrange("h f j d -> j h f d")
                srcv = sb.rearrange("p (h f d) -> p h f d", h=4, f=2)
                S_.dma("sp", lambda e, dst=dst, srcv=srcv: e.dma_start(out=dst, in_=srcv), reads=[("stg", par)],
                       writes=[("Vd", cg * 4 + hh) for hh in range(4)])
        S_.fence()

    def attention(self, li, si):
        S_ = self.S
        A = self.carve
        j_ = li - NA
        scale = 0.125
        self.rmsnorm(("attn_norm", li))
        S_.fence()
        oT = A(0, 8192, BF16).rearrange("p (c t) -> p c t", c=8)
        o = 8192
        kT = [A(o + i * 1024, 1024, BF16) for i in range(2)]; o += 2048
        Vb = [A(o + i * 2048, 2048, BF16) for i in range(2)]; o += 4096
        qT = [A(o + i * 1024, 1024, BF16) for i in range(2)]; o += 2048
        eb = [A(o + i * 512, 512, F32) for i in range(2)]; o += 1024
        cb = [A(o + i * 512, 512, F32) for i in range(2)]; o += 1024
        wb = [A(o + i * 256, 256, BF16) for i in range(2)]; o += 512
        wTb = [A(o + i * 256, 256, BF16) for i in range(2)]; o += 512
        assert o <= 19968, o
        Wq = []
        for cg in range(2):
            wt = self.wbuf[cg][:, 0:4096].rearrange("p (k m) -> p k m", k=8)
            src = self.w_q[j_][:, cg * 512:(cg + 1) * 512].rearrange("(k p) m -> p k m", p=128)
            S_.dma("pool", lambda e, wt=wt, src=src: e.dma_start(out=wt, in_=src), writes=[("w", cg)])
            Wq.append(wt)
        for i in range(2):
            vz = Vb[i].rearrange("p (f b c d) -> p f b c d", f=2, b=16, c=2)
            S_.op("dve", lambda e, vz=vz: e.memset(vz[:, 0, :, 1, :], 0.0), writes=[("Vz", i)])
            S_.op("dve", lambda e, vz=vz: e.memset(vz[:, 1, :, 0, :], 0.0), writes=[("Vz", i)])
        self.ps_lo = 2
        self.ps_rr = 0
        seg_it = [0]
        po_it = [0]

        def load_hp(hp):
            bpar = hp % 2
            S_.dma("sp", lambda e: e.dma_start(out=kT[bpar], in_=self.kTd[hp]), reads=[("kTd", hp)], writes=[("kT", bpar)])
            vz = Vb[bpar].rearrange("p (f b c d) -> p f b c d", f=2, b=16, c=2)
            for f in range(2):
                src = self.Vd[hp, f].rearrange("j (b d) -> j b d", b=16)
                dst = vz[:, f, :, f, :]
                S_.dma("sp", lambda e, dst=dst, src=src: e.dma_start(out=dst, in_=src), reads=[("Vd", hp)], writes=[(("VA", "VB")[f], bpar)])

        def q_proj(hp):
            cg, hl = hp // 4, hp % 4
            for tt in range(NT):
                pb = self.next_ps(); ps = self.ps[pb]
                for k in range(KC):
                    S_.op("pe", lambda e, ps=ps, k=k, tt=tt: e.matmul(ps[:, :], Wq[cg][:, k, hl * 128:(hl + 1) * 128], self.hn[:, k, tt * TT:(tt + 1) * TT], start=(k == 0), stop=(k == KC - 1)),
                          reads=[("w", cg), ("hn", k, tt)], writes=[("ps", pb)])
                S_.op("act", lambda e, ps=ps, tt=tt: e.activation(qT[hp % 2][:, tt * TT:(tt + 1) * TT], ps[:, :], AF.Identity), reads=[("ps", pb)], writes=[("qT", hp % 2, tt)])

        class Seg:
            pass

        segs = []
        for hp in range(8):
            for i in range(16):
                nseg = (i + 1 + 3) // 4
                lst = [(half, sg) for half in range(2) for sg in range(nseg)]
                for n_, (half, sg) in enumerate(lst):
                    g = Seg()
                    g.hp, g.i, g.half, g.sg = hp, i, half, sg
                    g.first = (n_ == 0)
                    g.last = (n_ == len(lst) - 1)
                    g.bpar = hp % 2
                    g.rows = slice(half * 64, half * 64 + 64)
                    j0 = S - (i + 1) * 128
                    g.js = j0 + sg * 512
                    g.n = min(512, S - g.js)
                    segs.append(g)
        for idx, g in enumerate(segs):
            g.par = idx % 2
            g.par4 = idx % 4
        qb = {}
        for g in segs:
            key = (g.hp, g.i)
            if key not in qb:
                qb[key] = len(qb) % 2
            g.pbo = qb[key]

        def stA(g):
            n = g.n
            g.pb = self.next_ps()
            ps = self.ps[g.pb]
            E = eb[g.par]
            S_.op("pe", lambda e: e.matmul(ps[:, 0:n], qT[g.hp % 2][g.rows, g.i * 128:(g.i + 1) * 128], kT[g.bpar][g.rows, g.js:g.js + n], start=True, stop=True),
                  reads=[("qT", g.hp % 2, g.i // 4), ("kT", g.bpar)], writes=[("ps", g.pb)])
            S_.op("act", lambda e: e.activation(E[:, 0:n], ps[:, 0:n], AF.Exp, scale=scale), reads=[("ps", g.pb)], writes=[("e", g.par)])
            S_.op("act", lambda e: e.activation(E[:, 0:n], E[:, 0:n], AF.Ln, bias=1.0), reads=[("e", g.par)], writes=[("e", g.par)])

        def stB(g):
            n = g.n
            ps = self.ps[g.pb]
            E = eb[g.par]; C = cb[g.par]
            if g.sg == 0:
                S_.op("dve", lambda e: e.tensor_tensor(E[:, 0:128], E[:, 0:128], self.amask, ALU.mult), reads=[("e", g.par), "cst"], writes=[("e", g.par)])
                init = 0.0
                rd = []
            else:
                init = cb[1 - g.par][:, 511:512]
                rd = [("c", 1 - g.par)]
            S_.op("dve", lambda e: e.tensor_tensor_scan(C[:, 0:n], self.onesb[:, 0:n], E[:, 0:n], init, ALU.mult, ALU.add),
                  reads=[("e", g.par), "onesb"] + rd, writes=[("c", g.par)])
            S_.op("dve", lambda e: e.scalar_tensor_tensor(E[:, 0:n], ps[:, 0:n], scale, C[:, 0:n], ALU.mult, ALU.subtract),
                  reads=[("ps", g.pb), ("c", g.par)], writes=[("e", g.par)])
            if g.sg == 0:
                S_.op("dve", lambda e: e.tensor_tensor(E[:, 0:128], E[:, 0:128], self.negmask, ALU.add), reads=[("e", g.par), "cst"], writes=[("e", g.par)])

        def stC(g):
            n = g.n
            nb = n // 128
            E = eb[g.par]; W = wb[g.par]
            S_.op("act", lambda e: e.activation(W[:, 0:n], E[:, 0:n], AF.Exp), reads=[("e", g.par)], writes=[("wsb", g.par)])
            g.pbt = self.next_ps()
            pst = self.ps[g.pbt][:, :].bitcast(BF16)
            for b_ in range(nb):
                S_.op("pe", lambda e, b_=b_: e.transpose(pst[:, b_ * 128:(b_ + 1) * 128], W[:, b_ * 128:(b_ + 1) * 128], self.identb[:, :]),
                      reads=[("wsb", g.par), "identb"], writes=[("ps", g.pbt)])

        def stD(g):
            n = g.n
            nb = n // 128
            pst = self.ps[g.pbt][:, :].bitcast(BF16)
            WT = wTb[g.par]
            pso = self.ps[g.pbo]; pok = ("ps", g.pbo)
            S_.op("act", lambda e: e.activation(WT[:, 0:n], pst[:, 0:n], AF.Identity), reads=[("ps", g.pbt)], writes=[("wT", g.par)])
            base = 0 if g.half == 0 else 2048
            vkey = ("VA", g.bpar) if g.half == 0 else ("VB", g.bpar)
            for b_ in range(nb):
                jb = (g.js + b_ * 128) // 128
                off = base + jb * 128
                lhsT = Vb[g.bpar][:, off:off + 128]
                S_.op("pe", lambda e, b_=b_, lhsT=lhsT: e.matmul(pso[:, 0:128], lhsT, WT[:, b_ * 128:(b_ + 1) * 128],
                                                               start=(g.first and b_ == 0), stop=(g.last and b_ == nb - 1)),
                      reads=[("wT", g.par), vkey, ("Vz", g.bpar)], writes=[pok])
            if g.last:
                S_.op("act", lambda e: e.activation(oT[:, g.hp, g.i * 128:(g.i + 1) * 128], pso[:, 0:128], AF.Identity), reads=[pok], writes=[("oT", g.hp, g.i // 4)])

        load_hp(0)
        NS = len(segs)
        for k in range(NS + 2):
            if k < NS:
                g = segs[k]
                if g.i == 0 and g.first:
                    if g.hp + 1 < 8:
                        load_hp(g.hp + 1)
                    q_proj(g.hp)
                stA(g)
                stB(g)
            if 0 <= k - 1 < NS:
                stC(segs[k - 1])
            if 0 <= k - 2 < NS:
                stD(segs[k - 2])
        self.ps_lo = 0
        self.ps_rr = 0

        def rhs_o(k, tt):
            return oT[:, k, tt * TT:(tt + 1) * TT], ("oT", k, tt)

        def epi_o(gi, mi, tt, ps, pkey):
            c = gi * 4 + mi
            hv = self.hT[:, c, tt * TT:(tt + 1) * TT]
            S_.op("dve", lambda e: e.tensor_tensor(hv, ps[:, :], hv, ALU.add), reads=[pkey, ("h", c, tt)], writes=[("h", c, tt)])

        self.linear(self.w_o[j_], (0, 8), [[0, 128, 256, 384], [512, 640, 768, 896]], rhs_o, epi_o)
        S_.fence()

    def ple(self, li, si):
        S_ = self.S
        self.rmsnorm(("ple_norm", li))
        S_.fence()
        pb16 = self.carve(0, 2048, BF16)
        sg = [self.carve(2048 + i * 512, 512, F32) for i in range(2)]
        src = self.pT[li, si].rearrange("(k p) t -> p k t", p=128)
        dst = pb16.rearrange("p (k t) -> p k t", k=2)
        S_.dma("pool", lambda e: e.dma_start(out=dst, in_=src), reads=(), writes=["pT"])
        wp = self.carve(3072, 1024, BF16).rearrange("p (k m) -> p k m", k=2)
        srcw = self.ple_proj[li].rearrange("(k p) m -> p k m", p=128)
        S_.dma("pool", lambda e: e.dma_start(out=wp, in_=srcw), reads=(), writes=["wp"])
        it = [0]

        def rhs_fn(k, tt):
            return self.hn[:, k, tt * TT:(tt + 1) * TT], ("hn", k, tt)

        def epi(gi, mi, tt, ps, pkey):
            c = gi * 4 + mi
            t0 = tt * TT
            par = it[0] % 2
            it[0] += 1
            s = sg[par]
            S_.op("act", lambda e: e.activation(s, ps[:, :], AF.Sigmoid), reads=[pkey], writes=[("sg", par)])
            pb2 = self.next_ps()
            ps2 = self.ps[pb2]
            for k in range(2):
                S_.op("pe", lambda e, k=k: e.matmul(ps2[:, :], wp[:, k, c * 128:(c + 1) * 128], pb16[:, k * S + t0:k * S + t0 + TT], start=(k == 0), stop=(k == 1)),
                      reads=["wp", "pT"], writes=[("ps", pb2)])
            S_.op("dve", lambda e: e.tensor_tensor(s, s, ps2[:, :], ALU.mult), reads=[("sg", par), ("ps", pb2)], writes=[("sg", par)])
            hv = self.hT[:, c, t0:t0 + TT]
            S_.op("dve", lambda e: e.tensor_tensor(hv, hv, s, ALU.add), reads=[("sg", par), ("h", c, tt)], writes=[("h", c, tt)])

        self.linear(self.ple_gate[li], (0, 8), [[0, 128, 256, 384], [512, 640, 768, 896]], rhs_fn, epi)
        S_.fence()

    def build(self):
        S_ = self.S
        nc = self.nc
        cfg = self.cfg
        S_.op("dve", lambda e: e.memset(self.ones[:, :], 1.0), writes=["ones"])
        S_.op("dve", lambda e: e.memset(self.onesb[:, :], 1.0), writes=["onesb"])
        S_.dma("sp", lambda e: e.dma_start(out=self.vecs[:, :], in_=self.vecs_d[:, :]), writes=["vecs"])
        S_.dma("sp", lambda e: e.dma_start(out=self.cst[:, :], in_=self.consts_d[:, :]), writes=["cst"])
        S_.op("dve", lambda e: e.tensor_copy(self.identb[:, :], self.cst[:, 384:512]), reads=["cst"], writes=["identb"])
        for si in range(self.nseq):
            for c in range(KC):
                src = self.xT[si, c * 128:(c + 1) * 128, :]
                dst = self.hT[:, c, :]
                S_.dma("sp", lambda e, dst=dst, src=src: e.dma_start(out=dst, in_=src),
                       writes=[("h", c, tt) for tt in range(NT)])
            for li in cfg["layers"]:
                if cfg.get("mixer", True):
                    if li < NA:
                        self.mamba(li, si)
                    else:
                        self.attention(li, si)
                if cfg.get("ffn", True):
                    self.ffn(li)
                if cfg.get("ple", True):
                    self.ple(li, si)
                if li == NA - 1 and cfg.get("mixer", True) and any(l >= NA for l in cfg["layers"]):
                    self.kv_stage(si)
            S_.fence()
            ob2 = self.carve(4096, 2 * 4096, F32)

            self._final_norm_store(si, ob2)
            S_.fence()
        S_.emit(nc)
        return nc

    def _final_norm_store(self, si, ob2):
        S_ = self.S
        eps = 1e-6
        sq = self.carve(0, 2048, BF16)
        rr = [self.carve(2048, 512, F32), self.carve(2560, 512, F32)]
        for tt in range(NT):
            t0 = tt * TT
            pb = self.next_ps()
            ps = self.ps[pb]
            for c in range(KC):
                o = sq[:, c * 512:(c + 1) * 512]
                i = self.hT[:, c, t0:t0 + TT]
                S_.op("act", lambda e, o=o, i=i: e.activation(o, i, AF.Square), reads=[("h", c, tt)], writes=[("sq", c)])
            for c in range(KC):
                r_ = sq[:, c * 512:(c + 1) * 512]
                S_.op("pe", lambda e, ps=ps, r_=r_, c=c: e.matmul(ps[:, :], self.ones[:, :], r_, start=(c == 0), stop=(c == KC - 1)),
                      reads=[("sq", c), "ones"], writes=[("ps", pb)])
            r = rr[tt % 2]
            rk = ("rstd", tt % 2)
            S_.op("act", lambda e, r=r, ps=ps: e.activation(r, ps[:, :], AF.Sqrt, bias=eps, scale=1.0 / D), reads=[("ps", pb)], writes=[rk])
            S_.op("dve", lambda e, r=r: e.reciprocal(r, r), reads=[rk], writes=[rk])
            for c in range(KC):
                o = ob2[:, (tt % 2) * 4096 + c * 512:(tt % 2) * 4096 + (c + 1) * 512]
                i = self.hT[:, c, t0:t0 + TT]
                g = self.vcol("final_norm", c)
                S_.op("dve", lambda e, o=o, i=i, g=g, r=r: e.scalar_tensor_tensor(o, i, g, r, ALU.mult, ALU.mult),
                      reads=[("h", c, tt), rk, "vecs"], writes=[("ob", tt % 2, c)])
                dst = self.outT[si, c * 128:(c + 1) * 128, t0:t0 + TT]
                S_.dma("sp", lambda e, dst=dst, o=o: e.dma_start(out=dst, in_=o), reads=[("ob", tt % 2, c)], writes=[])


NCORES = 8
_prog_cache = {}


def get_prog(nseq, cfg):
    key = (nseq, repr(sorted(cfg.items())))
    if key not in _prog_cache:
        p = Prog(nseq, cfg)
        p.build()
        _prog_cache[key] = p
    return _prog_cache[key]


def make_in_maps(inp, batch_ids_per_core):
    vecs = pack_vecs(inp)
    shared = dict(vecs=vecs, consts=make_consts(), bvecs=pack_bvecs(inp),
                  ssm_in_proj=np.ascontiguousarray(inp["ssm_in_proj"], np.float32),
                  ssm_out_proj=np.ascontiguousarray(inp["ssm_out_proj"], np.float32),
                  w_kv=np.ascontiguousarray(inp["w_kv"], np.float32),
                  w_q=np.ascontiguousarray(inp["w_q"], np.float32),
                  w_o=np.ascontiguousarray(inp["w_o"], np.float32),
                  ffn_up=np.ascontiguousarray(inp["ffn_up"], np.float32),
                  ffn_down=np.ascontiguousarray(inp["ffn_down"], np.float32),
                  ple_gate=np.ascontiguousarray(inp["ple_gate"], np.float32),
                  ple_proj=np.ascontiguousarray(inp["ple_proj"], np.float32))
    in_maps = []
    for ids in batch_ids_per_core:
        xT = np.ascontiguousarray(np.transpose(inp["x"][ids], (0, 2, 1)))
        pT = np.ascontiguousarray(np.transpose(inp["p"][:, ids], (0, 1, 3, 2)))
        m = dict(shared)
        m["xT"] = xT
        m["pT"] = pT
        in_maps.append(m)
    return in_maps


def run_cores(inp, batch_ids_per_core, cfg):
    nseq = len(batch_ids_per_core[0])
    prog = get_prog(nseq, cfg)
    in_maps = make_in_maps(inp, batch_ids_per_core)
    res = run_bass_kernel_spmd(prog.nc, in_maps, core_ids=list(range(len(in_maps))))
    outs = []
    for r in res.results:
        outs.append(np.transpose(r["outT"], (0, 2, 1)))
    return outs


FULL_CFG = dict(layers=(0, 1, 2, 3), mixer=True, ffn=True, ple=True)


def kernel(**inp):
    inp = {k: np.asarray(v) for k, v in inp.items()}
    B = inp["x"].shape[0]
    per = B // NCORES
    ids = [[c * per + g for g in range(per)] for c in range(NCORES)]
    outs = run_cores(inp, ids, FULL_CFG)
    out = np.empty((B, S, D), np.float32)
    for c in range(NCORES):
        for g in range(per):
            out[c * per + g] = outs[c][g]
    return out
```

```python
import numpy as np
import concourse.bass as bass
import concourse.mybir as mybir
from concourse.bass_utils import run_bass_kernel_spmd

F32 = mybir.dt.float32
BF16 = mybir.dt.bfloat16
AF = mybir.ActivationFunctionType
ALU = mybir.AluOpType

ENG = ["pe", "act", "dve", "pool", "sp"]
N_DMA_SEMS = 8

D = 1024
S = 2048
NT = 4
TT = 512
KC = 8
DFF = 2816
NFF = 22
PLE = 256
DEPTH = 4
NA = 2


class Sched:
    def __init__(self):
        self.ops = {e: [] for e in ENG}
        self.lastw = {}
        self.readers = {}
        self.dma_names = ["dma_%s%d" % (q, j) for q in ("sp", "pool") for j in range(N_DMA_SEMS)]
        self.dma_cnt = {n: 0 for n in self.dma_names}
        self.dma_rr = {"sp": 0, "pool": 0}

    def _need(self, deps, me, res, pos, allow_same=False):
        if res == me and not allow_same:
            return
        if deps.get(res, 0) < pos:
            deps[res] = pos

    def _deps(self, me, reads, writes, allow_same=False):
        deps = {}
        for k in reads:
            w = self.lastw.get(k)
            if w:
                self._need(deps, me, w[0], w[1], allow_same)
            if isinstance(k, tuple) and k[0] == "ps":
                for r, p in self.readers.get(k, {}).items():
                    self._need(deps, me, r, p, False)
        for k in writes:
            w = self.lastw.get(k)
            if w:
                self._need(deps, me, w[0], w[1], allow_same)
            for r, p in self.readers.get(k, {}).items():
                self._need(deps, me, r, p, allow_same)
        return deps

    def _mark(self, res, pos, reads, writes):
        for k in reads:
            self.readers.setdefault(k, {})[res] = pos
        for k in writes:
            self.lastw[k] = (res, pos)
            self.readers[k] = {}

    def op(self, eng, fn, reads=(), writes=()):
        deps = self._deps(eng, reads, writes, allow_same=(eng != "pe"))
        self.ops[eng].append(dict(fn=fn, deps=deps, dma=None))
        self._mark(eng, len(self.ops[eng]), reads, writes)

    def dma(self, eng, fn, reads=(), writes=()):
        j = self.dma_rr[eng]
        self.dma_rr[eng] = (j + 1) % N_DMA_SEMS
        res = "dma_%s%d" % (eng, j)
        k = self.dma_cnt[res] + 1
        self.dma_cnt[res] = k
        deps = self._deps(eng, reads, writes, allow_same=True)
        if k > 1:
            self._need(deps, eng, res, k - 1)
        self.ops[eng].append(dict(fn=fn, deps=deps, dma=res))
        self._mark(res, k, reads, writes)

    def fence(self):
        tail = {}
        for e in ENG:
            for pos in range(len(self.ops[e]), 0, -1):
                o = self.ops[e][pos - 1]
                if o["fn"] is not None and o["dma"] is None:
                    tail[e] = pos
                    break
        dtail = {n: c for n, c in self.dma_cnt.items() if c}
        for e in ENG:
            deps = {r: p for r, p in tail.items() if r != e}
            deps.update(dtail)
            self.ops[e].append(dict(fn=None, deps=deps, dma=None))

    def emit(self, nc):
        sig = {e: set() for e in ENG}
        for e in ENG:
            for o in self.ops[e]:
                for r, p in o["deps"].items():
                    if r in sig:
                        sig[r].add(p)
        rank = {}
        for e in ENG:
            for i, p in enumerate(sorted(sig[e])):
                rank[(e, p)] = i + 1
        from contextlib import ExitStack
        with ExitStack() as st:
            sems = {e: st.enter_context(nc.semaphore("s_" + e)) for e in ENG}
            for n in self.dma_names:
                sems[n] = st.enter_context(nc.semaphore("s_" + n))
            block = st.enter_context(nc.Block())
            engobj = {}

            def run(ename, eng):
                seen = {}
                for pos, o in enumerate(self.ops[ename], start=1):
                    for r, p in o["deps"].items():
                        val = 16 * p if r.startswith("dma") else rank[(r, p)]
                        if seen.get(r, 0) >= val:
                            continue
                        eng.wait_ge(sems[r], val)
                        seen[r] = val
                    if o["fn"] is None:
                        continue
                    ins = o["fn"](eng)
                    if o["dma"] is not None:
                        ins.then_inc(sems[o["dma"]], 16)
                    elif pos in sig[ename]:
                        ins.then_inc(sems[ename], 1)
                if ename == "sp":
                    for n, c in self.dma_cnt.items():
                        if c and seen.get(n, 0) < 16 * c:
                            eng.wait_ge(sems[n], 16 * c)

            @block.tensor
            def _(e):
                run("pe", e)

            @block.scalar
            def _(e):
                run("act", e)

            @block.vector
            def _(e):
                run("dve", e)

            @block.gpsimd
            def _(e):
                run("pool", e)

            @block.sync
            def _(e):
                run("sp", e)


def _fm(v):
    v = np.asarray(v, np.float32)
    return np.ascontiguousarray(v.reshape(-1, 128).T)


class VecLayout:
    def __init__(self):
        self.off = {}
        self.n = 0

    def add(self, name, ncols):
        self.off[name] = self.n
        self.n += ncols


def vec_layout():
    L = VecLayout()
    for i in range(DEPTH):
        L.add(("attn_norm", i), 8)
        L.add(("ffn_norm", i), 8)
        L.add(("ple_norm", i), 8)
        for k in range(3):
            L.add(("ffn_cw", i, k), 44)
        L.add(("ffn_cb", i), 44)
    L.add("kv_norm", 8)
    L.add("final_norm", 8)
    for i in range(NA):
        for k in range(4):
            L.add(("ssm_cw", i, k), 24)
        L.add(("ssm_cb", i), 24)
        L.add(("ssm_dfm", i), 16)
    return L


def pack_vecs(inp):
    L = vec_layout()
    out = np.zeros((128, L.n), np.float32)

    def put(name, v):
        a = _fm(v)
        out[:, L.off[name]:L.off[name] + a.shape[1]] = a

    for i in range(DEPTH):
        put(("attn_norm", i), inp["attn_norm"][i])
        put(("ffn_norm", i), inp["ffn_norm"][i])
        put(("ple_norm", i), inp["ple_norm"][i])
        for k in range(3):
            put(("ffn_cw", i, k), inp["ffn_conv_w"][i][k])
        put(("ffn_cb", i), inp["ffn_conv_b"][i])
    put("kv_norm", inp["kv_norm"])
    put("final_norm", inp["final_norm"])
    for i in range(NA):
        for k in range(4):
            put(("ssm_cw", i, k), inp["ssm_conv_w"][i][k])
        put(("ssm_cb", i), inp["ssm_conv_b"][i])
        put(("ssm_dfm", i), np.repeat(np.asarray(inp["ssm_d"][i], np.float32), 64))
    return out


def pack_bvecs(inp):
    out = np.zeros((128, NA * 3072), np.float32)
    for i in range(NA):
        o = i * 3072
        out[:, o:o + 512] = np.tile(np.asarray(inp["ssm_dt_bias"][i], np.float32), 16)[None, :]
        out[:, o + 512:o + 1024] = np.tile(np.asarray(inp["ssm_a_log"][i], np.float32), 16)[None, :]
        out[:, o + 1024:o + 3072] = np.asarray(inp["ssm_norm"][i], np.float32)[None, :]
    return out


def make_consts():
    k = np.arange(128)
    c = np.zeros((128, 768), np.float32)
    am = ((k[:, None] + k[None, :]) >= 128).astype(np.float32)
    c[:, 512:640] = am
    c[:, 640:768] = (am - 1.0) * 30000.0
    c[:, 0:128] = (k[:, None] <= k[None, :])
    c[:, 128:256] = (k[:, None] > k[None, :])
    c[:, 256:384] = 1.0
    c[:, 384:512] = np.eye(128, dtype=np.float32)
    return c


class Prog:
    def __init__(self, nseq, cfg):
        self.nseq = nseq
        self.cfg = cfg
        self.nc = nc = bass.Bass("TRN2", target_bir_lowering=False)
        self.S = Sched()
        self.VL = vec_layout()
        di = lambda name, shape: nc.dram_tensor(name, shape, F32, kind="ExternalInput").ap()
        self.xT = di("xT", [nseq, D, S])
        self.pT = di("pT", [DEPTH, nseq, PLE, S])
        self.vecs_d = di("vecs", [128, self.VL.n])
        self.ffn_up = di("ffn_up", [DEPTH, D, 2 * DFF])
        self.ffn_down = di("ffn_down", [DEPTH, DFF, D])
        self.ple_gate = di("ple_gate", [DEPTH, D, D])
        self.ple_proj = di("ple_proj", [DEPTH, PLE, D])
        self.consts_d = di("consts", [128, 768])
        self.bvecs_d = di("bvecs", [128, NA * 3072])
        self.ssm_in = di("ssm_in_proj", [NA, D, 5152])
        self.ssm_out = di("ssm_out_proj", [NA, 2048, D])
        self.w_kv = di("w_kv", [D, 2048])
        self.w_q = di("w_q", [2, D, D])
        self.w_o = di("w_o", [2, D, D])
        self.kTd = nc.dram_tensor("kTd", [8, 128, S], BF16).ap()
        self.Vd = nc.dram_tensor("Vd", [8, 2, 128, 16 * 64], BF16).ap()
        self.outT = nc.dram_tensor("outT", [nseq, D, S], F32, kind="ExternalOutput").ap()

        A = nc.alloc_sbuf_tensor
        self.hT = A("hT", [128, KC, S], F32)
        self.hn = A("hn", [128, KC, S], BF16)
        self.vecs = A("vecsb", [128, self.VL.n], F32)
        self.wbuf = [A("wbuf%d" % i, [128, 6144], BF16) for i in range(2)]
        self.ones = A("ones", [128, 128], BF16)
        self.arena = A("arena", [128, 19968], F32)
        self.cst = A("cst", [128, 768], F32)
        self.amask = self.cst[:, 512:640]
        self.negmask = self.cst[:, 640:768]
        self.onesb = A("onesb", [128, 512], BF16)
        self.identb = A("identb", [128, 128], BF16)
        self.tri = self.cst[:, 0:128]
        self.strictT = self.cst[:, 128:256]
        self.onesf = self.cst[:, 256:384]
        self.ps = [nc.alloc_psum_tensor("ps%d" % i, [128, 512], F32) for i in range(8)]
        self.ps_rr = 0
        self.ps_lo = 0
        self.w_rr = 0
        print("sbuf remaining", nc.sbuf_bytes_remaining)

    def vcol(self, name, c):
        o = self.VL.off[name] + c
        return self.vecs[:, o:o + 1]

    def next_ps(self):
        lo = self.ps_lo
        i = self.ps_rr
        self.ps_rr = (i + 1) % (8 - lo)
        return lo + i

    def carve(self, off_f32, n, dtype, shape=None):
        ap = self.arena[:, off_f32:off_f32 + n]
        if dtype == BF16:
            ap = ap.bitcast(BF16)
        return ap

    def load_w(self, wd, rows, c0, ncols, eng="pool"):
        r0, nk = rows
        b = self.w_rr
        self.w_rr = 1 - b
        dst = self.wbuf[b][:, 0:nk * ncols].rearrange("p (k m) -> p k m", k=nk)
        src = wd[r0:r0 + nk * 128, c0:c0 + ncols].rearrange("(k p) m -> p k m", p=128)
        self.S.dma(eng, lambda e: e.dma_start(out=dst, in_=src), reads=(), writes=[("w", b)])
        return b, dst

    def rmsnorm(self, gain_name, out_bf16=True, out_ap_fn=None, eps=1e-6, reverse=False, top=False):
        S_ = self.S
        if top:
            nsl = 4
            sq = self.carve(17920, 1024, BF16)
            rr = [self.carve(18944, 512, F32), self.carve(19456, 512, F32)]
            tg = "T"
        else:
            nsl = 8
            sq = self.carve(0, 2048, BF16)
            rr = [self.carve(2048, 512, F32), self.carve(2560, 512, F32)]
            tg = "L"
        for tt in range(NT):
            t0 = tt * TT
            pb = self.next_ps()
            ps = self.ps[pb]
            for c in range(KC):
                sl = c % nsl
                o = sq[:, sl * 512:(sl + 1) * 512]
                i = self.hT[:, c, t0:t0 + TT]
                S_.op("act", lambda e, o=o, i=i: e.activation(o, i, AF.Square),
                      reads=[("h", c, tt)], writes=[("sq", tg, sl)])
                S_.op("pe", lambda e, ps=ps, o=o, c=c: e.matmul(ps[:, :], self.ones[:, :], o, start=(c == 0), stop=(c == KC - 1)),
                      reads=[("sq", tg, sl), "ones"], writes=[("ps", pb)])
            r = rr[tt % 2]
            rk = ("rstd", tg, tt % 2)
            S_.op("act", lambda e, r=r, ps=ps: e.activation(r, ps[:, :], AF.Sqrt, bias=eps, scale=1.0 / D),
                  reads=[("ps", pb)], writes=[rk])
            S_.op("dve", lambda e, r=r: e.reciprocal(r, r), reads=[rk], writes=[rk])
            for c in range(KC):
                if reverse:
                    o = self.hn[:, c, (3 - tt) * TT:(4 - tt) * TT][:, ::-1]
                    wk = ("hn", c, 3 - tt)
                elif out_ap_fn is None:
                    o = self.hn[:, c, t0:t0 + TT]
                    wk = ("hn", c, tt)
                else:
                    o, wk = out_ap_fn(c, tt)
                i = self.hT[:, c, t0:t0 + TT]
                g = self.vcol(gain_name, c)
                S_.op("dve", lambda e, o=o, i=i, g=g, r=r: e.scalar_tensor_tensor(o, i, g, r, ALU.mult, ALU.mult),
                      reads=[("h", c, tt), rk, "vecs"], writes=[wk])

    def linear(self, wd, k_rows, col_groups, rhs_fn, epi_fn, weng="pool"):
        S_ = self.S
        r0, nk = k_rows

        def load(gi):
            cols = col_groups[gi]
            b = self.w_rr
            self.w_rr = 1 - b
            n = len(cols)
            dst = self.wbuf[b][:, 0:nk * n * 128].rearrange("p (k m) -> p k m", k=nk)
            contiguous = all(cols[i + 1] == cols[i] + 128 for i in range(n - 1))
            if contiguous:
                src = wd[r0:r0 + nk * 128, cols[0]:cols[0] + n * 128].rearrange("(k p) m -> p k m", p=128)
                S_.dma(weng, lambda e, dst=dst, src=src: e.dma_start(out=dst, in_=src), reads=(), writes=[("w", b)])
            else:
                for i, c0 in enumerate(cols):
                    src = wd[r0:r0 + nk * 128, c0:c0 + 128].rearrange("(k p) m -> p k m", p=128)
                    d2 = dst[:, :, i * 128:(i + 1) * 128]
                    S_.dma(weng, lambda e, d2=d2, src=src: e.dma_start(out=d2, in_=src), reads=(),
                           writes=[("w", b, i)] + ([("w", b)] if i == 0 else []))
            return b, dst, (not contiguous)

        nxt = load(0)
        for gi, cols in enumerate(col_groups):
            b, wt, split = nxt
            if gi + 1 < len(col_groups):
                nxt = load(gi + 1)
            for mi in range(len(cols)):
                wkeys = [("w", b)] + ([("w", b, mi)] if split else [])
                for tt in range(NT):
                    pb = self.next_ps()
                    ps = self.ps[pb]
                    for k in range(nk):
                        rhs, rkey = rhs_fn(k, tt)
                        lhsT = wt[:, k, mi * 128:(mi + 1) * 128]
                        S_.op("pe", lambda e, ps=ps, lhsT=lhsT, rhs=rhs, k=k: e.matmul(ps[:, :], lhsT, rhs, start=(k == 0), stop=(k == nk - 1)),
                              reads=wkeys + [rkey], writes=[("ps", pb)])
                    epi_fn(gi, mi, tt, ps, ("ps", pb))

    def ffn(self, li):
        S_ = self.S
        self.rmsnorm(("ffn_norm", li), top=True)
        g = self.carve(0, 11264, BF16)
        Ug = self.carve(11264, 2052, F32)
        Uv = self.carve(11264 + 2052, 2052, F32)
        acc0 = 11264 + 2 * 2052
        accs = [[self.carve(acc0 + (2 * p + q) * 512, 512, F32) for q in range(2)] for p in range(2)]
        S_.op("dve", lambda e: e.memset(Ug[:, 0:2], 0.0), writes=[("U", 0, -1)])
        S_.op("dve", lambda e: e.memset(Uv[:, 0:2], 0.0), writes=[("U", 1, -1)])
        U = [Ug, Uv]
        wup = self.ffn_up[li]
        wdn = self.ffn_down[li]
        cnt = [0]
        for half in range(2):
            j0 = half * 11
            groups = []
            for jj in range(0, 11, 2):
                js = [j0 + jj] + ([j0 + jj + 1] if jj + 1 < 11 else [])
                cols = []
                for j in js:
                    cols += [j * 128, DFF + j * 128]
                groups.append(cols)

            def rhs_fn(k, tt):
                return self.hn[:, k, tt * TT:(tt + 1) * TT], ("hn", k, tt)

            self._ffn_up_group(wup, groups, rhs_fn, j0, li, U, accs, g)

            def rhs2(k, tt):
                return g[:, k * S + tt * TT: k * S + (tt + 1) * TT], ("g", k, tt)

            def epi2(gi, mi, tt, ps, pkey):
                c = gi * 4 + mi
                hv = self.hT[:, c, tt * TT:(tt + 1) * TT]
                S_.op("dve", lambda e: e.tensor_tensor(hv, ps[:, :], hv, ALU.add),
                      reads=[pkey, ("h", c, tt)], writes=[("h", c, tt)])

            self.linear(wdn, (half * 11 * 128, 11), [[0, 128, 256, 384], [512, 640, 768, 896]], rhs2, epi2)
        S_.fence()

    def _ffn_up_group(self, wup, groups, rhs_fn, j0, li, U, accs, g):
        S_ = self.S
        nk = KC

        def load(gi):
            cols = groups[gi]
            b = self.w_rr
            self.w_rr = 1 - b
            n = len(cols)
            dst = self.wbuf[b][:, 0:nk * n * 128].rearrange("p (k m) -> p k m", k=nk)
            npair = n // 2
            srcg = wup[:, cols[0]:cols[0] + npair * 128].rearrange("(k p) m -> p k m", p=128)
            srcv = wup[:, cols[1]:cols[1] + npair * 128].rearrange("(k p) m -> p k m", p=128)
            dg = dst[:, :, 0:npair * 128]
            dv = dst[:, :, npair * 128:2 * npair * 128]
            S_.dma("pool", lambda e: e.dma_start(out=dg, in_=srcg), reads=(), writes=[("w", b), ("w", b, 0)])
            S_.dma("pool", lambda e: e.dma_start(out=dv, in_=srcv), reads=(), writes=[("w", b, 1)])
            return b, dst, npair

        nxt = load(0)
        it = 0
        for gi in range(len(groups)):
            b, wt, npair = nxt
            if gi + 1 < len(groups):
                nxt = load(gi + 1)
            for pj in range(npair):
                j = j0 + gi * 2 + pj
                jj = j - j0
                for tt in range(NT):
                    t0 = tt * TT
                    par = it % 2
                    it += 1
                    pbs = []
                    for q in range(2):
                        pb = self.next_ps()
                        ps = self.ps[pb]
                        pbs.append(pb)
                        for k in range(nk):
                            rhs, rkey = rhs_fn(k, tt)
                            lhsT = wt[:, k, (q * npair + pj) * 128:(q * npair + pj + 1) * 128]
                            S_.op("pe", lambda e, ps=ps, lhsT=lhsT, rhs=rhs, k=k: e.matmul(ps[:, :], lhsT, rhs, start=(k == 0), stop=(k == nk - 1)),
                                  reads=[("w", b), ("w", b, q), rkey], writes=[("ps", pb)])
                    for q in range(2):
                        pb = pbs[q]
                        ps = self.ps[pb]
                        pkey = ("ps", pb)
                        acc = accs[par][q]
                        ch = q * NFF + j
                        Uq = U[q]
                        w2 = self.vcol(("ffn_cw", li, 2), ch)
                        w1 = self.vcol(("ffn_cw", li, 1), ch)
                        w0 = self.vcol(("ffn_cw", li, 0), ch)
                        bb = self.vcol(("ffn_cb", li), ch)
                        S_.op("act", lambda e, Uq=Uq, ps=ps, t0=t0: e.activation(Uq[:, 2 + t0:2 + t0 + TT], ps[:, :], AF.Identity),
                              reads=[pkey], writes=[("U", q, tt)])
                        S_.op("act", lambda e, acc=acc, ps=ps, bb=bb, w2=w2: e.activation(acc, ps[:, :], AF.Identity, bias=bb, scale=w2),
                              reads=[pkey, "vecs"], writes=[("acc", par, q)])
                        S_.op("dve", lambda e, acc=acc, Uq=Uq, w1=w1, t0=t0: e.scalar_tensor_tensor(acc, Uq[:, 1 + t0:1 + t0 + TT], w1, acc, ALU.mult, ALU.add),
                              reads=[("U", q, tt), ("U", q, tt - 1), "vecs"], writes=[("acc", par, q)])
                        S_.op("dve", lambda e, acc=acc, Uq=Uq, w0=w0, t0=t0: e.scalar_tensor_tensor(acc, Uq[:, t0:t0 + TT], w0, acc, ALU.mult, ALU.add),
                              reads=[("U", q, tt), ("U", q, tt - 1), "vecs"], writes=[("acc", par, q)])
                    ag, av = accs[par]
                    S_.op("act", lambda e, ag=ag: e.activation(ag, ag, AF.Silu), reads=[("acc", par, 0)], writes=[("acc", par, 0)])
                    go = g[:, jj * S + t0: jj * S + t0 + TT]
                    S_.op("pool", lambda e, go=go, ag=ag, av=av: e.tensor_tensor(go, ag, av, ALU.mult),
                          reads=[("acc", par, 0), ("acc", par, 1)], writes=[("g", jj, tt)])

    def mamba(self, li, si):
        S_ = self.S
        A = self.carve
        self.rmsnorm(("attn_norm", li))
        S_.fence()
        win = self.ssm_in[li]
        wout = self.ssm_out[li]
        bv = self.bvecs_d
        v3 = lambda ap, c: ap.rearrange("p (c h) -> p c h", c=c)
        dt = A(0, 512, F32); adt = A(512, 512, F32); acs = A(1024, 512, F32)
        dS = A(1536, 512, F32); Ea = A(2048, 512, F32); Etot = A(2560, 512, F32)
        dtb = A(3072, 512, F32); ea = A(3584, 512, F32)
        normw = A(4096, 512, F32)
        wdt = A(4608, 128, BF16).rearrange("p (k m) -> p k m", k=8)
        diagD = A(4736, 256, BF16)
        xc = A(4992, 4096, BF16)
        BT = A(9088, 1024, BF16)
        CT = A(10112, 1024, BF16)
        T0 = 11136
        U = A(T0, 2052, F32)
        acc = [A(T0 + 2052 + i * 512, 512, F32) for i in range(2)]
        o = T0
        xdt = [A(o + i * 256, 256, BF16) for i in range(2)]; o += 512
        xsc = [A(o + i * 256, 256, BF16) for i in range(2)]; o += 512
        Btok = [A(o + i * 64, 64, BF16) for i in range(2)]; o += 128
        CBm = [A(o + i * 128, 128, F32) for i in range(2)]; o += 256
        rhsD = A(o, 1024, F32); o += 1024
        MT = [A(o + i * 512, 512, BF16) for i in range(2)]; o += 1024
        zs = [A(o + i * 256, 256, BF16) for i in range(2)]; o += 512
        t1 = [A(o + i * 512, 512, F32) for i in range(2)]; o += 1024
        gn = [A(o + i * 256, 256, BF16) for i in range(2)]; o += 512
        gnT = [A(o + i * 1024, 1024, BF16) for i in range(2)]; o += 2048
        state = A(o, 512, F32); o += 512
        state_bf = A(o, 256, BF16); o += 256
        ssb = [A(o + i * 2, 1, F32) for i in range(2)]; o += 4
        rsb = [A(o + i * 2, 1, F32) for i in range(2)]; o += 4
        assert o <= 19968, o

        S_.dma("sp", lambda e: e.dma_start(out=dtb, in_=bv[:, li * 3072:li * 3072 + 512]), writes=["dtb"])
        S_.dma("sp", lambda e: e.dma_start(out=ea, in_=bv[:, li * 3072 + 512:li * 3072 + 1024]), writes=["ea"])
        wsrc = win[:, 5120:5152].rearrange("(k p) m -> p k m", p=128)
        S_.dma("pool", lambda e: e.dma_start(out=wdt, in_=wsrc), writes=["wdt"])
        pb = self.next_ps(); ps = self.ps[pb]
        for c in range(16):
            for k in range(KC):
                S_.op("pe", lambda e, c=c, k=k, ps=ps: e.matmul(ps[:, c * 32:(c + 1) * 32], self.hn[:, k, c * 128:(c + 1) * 128], wdt[:, k, :], start=(k == 0), stop=(k == KC - 1)),
                      reads=["wdt", ("hn", k, c // 4)], writes=[("ps", pb)])
        S_.op("dve", lambda e, ps=ps: e.tensor_tensor(dt, ps[:, :], dtb, ALU.add), reads=[("ps", pb), "dtb"], writes=["dt"])
        S_.op("act", lambda e: e.activation(dt, dt, AF.Exp), reads=["dt"], writes=["dt"])
        S_.op("act", lambda e: e.activation(dt, dt, AF.Ln, bias=1.0), reads=["dt"], writes=["dt"])
        S_.op("act", lambda e: e.activation(ea, ea, AF.Exp), reads=["ea"], writes=["ea"])
        S_.op("dve", lambda e: e.scalar_tensor_tensor(adt, dt, -1.0, ea, ALU.mult, ALU.mult), reads=["dt", "ea"], writes=["adt"])
        pa = self.next_ps(); psA = self.ps[pa]
        pbb = self.next_ps(); psB = self.ps[pbb]
        S_.op("pe", lambda e: e.matmul(psA[:, :], self.tri, adt, start=True, stop=True), reads=["cst", "adt"], writes=[("ps", pa)])
        S_.op("pe", lambda e: e.matmul(psB[:, :], self.onesf, adt, start=True, stop=True), reads=["cst", "adt"], writes=[("ps", pbb)])
        S_.op("act", lambda e: e.activation(acs, psA[:, :], AF.Identity), reads=[("ps", pa)], writes=["acs"])
        S_.op("act", lambda e: e.activation(Ea, psA[:, :], AF.Exp), reads=[("ps", pa)], writes=["Ea"])
        S_.op("act", lambda e: e.activation(Etot, psB[:, :], AF.Exp), reads=[("ps", pbb)], writes=["Etot"])
        S_.op("dve", lambda e: e.tensor_tensor(dS, psB[:, :], acs, ALU.subtract), reads=[("ps", pbb), "acs"], writes=["dS"])
        S_.op("act", lambda e: e.activation(dS, dS, AF.Exp), reads=["dS"], writes=["dS"])

        def do_group(g):
            S_.fence()
            b = self.w_rr
            self.w_rr = 1 - b
            wt = self.wbuf[b][:, 0:6144].rearrange("p (k m) -> p k m", k=8)
            for i, (c0, n, d0) in enumerate([(2048 + 512 * g, 512, 0), (4096 + 128 * g, 128, 512), (4608 + 128 * g, 128, 640)]):
                src = win[:, c0:c0 + n].rearrange("(k p) m -> p k m", p=128)
                dst = wt[:, :, d0:d0 + n]
                S_.dma("pool", lambda e, dst=dst, src=src: e.dma_start(out=dst, in_=src),
                       writes=[("w", b, i)] + ([("w", b)] if i == 0 else []))
            S_.dma("sp", lambda e, g=g: e.dma_start(out=normw, in_=bv[:, li * 3072 + 1024 + 512 * g:li * 3072 + 1536 + 512 * g]), writes=["normw"])
            S_.op("dve", lambda e: e.memset(U[:, 0:3], 0.0), writes=[("U", -1)])
            it = 0
            for j in range(6):
                ch = 4 * g + j if j < 4 else (16 + g if j == 4 else 20 + g)
                wi = 0 if j < 4 else j - 3
                for tt in range(NT):
                    t0 = tt * TT
                    if j < 4:
                        dest = xc[:, j * S + t0:j * S + t0 + TT]; dkey = ("xc", j, tt)
                    elif j == 4:
                        dest = BT[:, t0:t0 + TT]; dkey = ("BT", tt)
                    else:
                        dest = CT[:, t0:t0 + TT]; dkey = ("CT", tt)
                    pb = self.next_ps(); ps = self.ps[pb]
                    for k in range(KC):
                        S_.op("pe", lambda e, ps=ps, k=k, j=j, t0=t0: e.matmul(ps[:, :], wt[:, k, j * 128:(j + 1) * 128], self.hn[:, k, t0:t0 + TT], start=(k == 0), stop=(k == KC - 1)),
                              reads=[("w", b), ("w", b, wi), ("hn", k, tt)], writes=[("ps", pb)])
                    par = it % 2
                    it += 1
                    a_ = acc[par]
                    w3 = self.vcol(("ssm_cw", li, 3), ch); w2 = self.vcol(("ssm_cw", li, 2), ch)
                    w1 = self.vcol(("ssm_cw", li, 1), ch); w0 = self.vcol(("ssm_cw", li, 0), ch)
                    bb = self.vcol(("ssm_cb", li), ch)
                    S_.op("act", lambda e, ps=ps, t0=t0: e.activation(U[:, 3 + t0:3 + t0 + TT], ps[:, :], AF.Identity), reads=[("ps", pb)], writes=[("U", tt)])
                    S_.op("act", lambda e, ps=ps, a_=a_, bb=bb, w3=w3: e.activation(a_, ps[:, :], AF.Identity, bias=bb, scale=w3),
                          reads=[("ps", pb), "vecs"], writes=[("macc", par)])
                    for sh, w in ((2, w2), (1, w1), (0, w0)):
                        S_.op("dve", lambda e, a_=a_, w=w, sh=sh, t0=t0: e.scalar_tensor_tensor(a_, U[:, sh + t0:sh + t0 + TT], w, a_, ALU.mult, ALU.add),
                              reads=[("U", tt), ("U", tt - 1), "vecs"], writes=[("macc", par)])
                    S_.op("act", lambda e, dest=dest, a_=a_: e.activation(dest, a_, AF.Silu), reads=[("macc", par)], writes=[dkey])

            S_.fence()
            bz = self.w_rr
            bo = 1 - bz
            Wz = self.wbuf[bz][:, 0:4096].rearrange("p (k m) -> p k m", k=8)
            Wo = self.wbuf[bo][:, 0:4096].rearrange("p (k m) -> p k m", k=4)
            srcz = win[:, 512 * g:512 * g + 512].rearrange("(k p) m -> p k m", p=128)
            srco = wout[512 * g:512 * g + 512, :].rearrange("(k p) m -> p k m", p=128)
            S_.dma("pool", lambda e: e.dma_start(out=Wz, in_=srcz), writes=[("w", bz)])
            S_.dma("pool", lambda e: e.dma_start(out=Wo, in_=srco), writes=[("w", bo)])
            for j in range(4):
                dj = diagD[:, j * 128:(j + 1) * 128]
                sc = self.vcol(("ssm_dfm", li), 4 * g + j)
                S_.op("dve", lambda e, dj=dj, sc=sc: e.tensor_scalar(dj, self.identb[:, :], sc, None, ALU.mult), reads=["identb", "vecs"], writes=[("diagD", j)])
            S_.op("dve", lambda e: e.memset(state, 0.0), writes=["state"])
            S_.op("pool", lambda e: e.memset(state_bf, 0.0), writes=["state_bf"])
            h8 = slice(8 * g, 8 * g + 8)
            bc64 = lambda ap, c: v3(ap, 16)[:, c, h8].unsqueeze(2).to_broadcast([128, 8, 64])
            r64 = lambda ap: ap.rearrange("p (r d) -> p r d", r=8)

            def stageA1(c):
                par = c % 2
                tt = c // 4
                l0 = c * 128
                pbx = self.next_ps(); psx = self.ps[pbx][:, :].bitcast(BF16)
                for j in range(4):
                    S_.op("pe", lambda e, j=j: e.transpose(psx[:, j * 128:(j + 1) * 128], xc[:, j * S + l0:j * S + l0 + 128], self.identb[:, :]),
                          reads=[("xc", j, tt), "identb"], writes=[("ps", pbx)])
                S_.op("pe", lambda e: e.transpose(psx[:, 512:640], BT[:, l0:l0 + 128], self.identb[:, :]), reads=[("BT", tt), "identb"], writes=[("ps", pbx)])
                S_.op("dve", lambda e: e.tensor_tensor(r64(xdt[par]), r64(psx[:, 0:512]), bc64(dt, c), ALU.mult), reads=[("ps", pbx), "dt"], writes=[("xdt", par)])
                S_.op("pool", lambda e: e.tensor_tensor(r64(xsc[par]), r64(xdt[par]), bc64(dS, c), ALU.mult), reads=[("xdt", par), "dS"], writes=[("xsc", par)])
                S_.op("act", lambda e: e.activation(Btok[par], psx[:, 512:640], AF.Identity), reads=[("ps", pbx)], writes=[("Btok", par)])
                pbc = self.next_ps(); psc = self.ps[pbc]
                S_.op("pe", lambda e: e.matmul(psc[:, 0:128], BT[:, l0:l0 + 128], CT[:, l0:l0 + 128], start=True, stop=True),
                      reads=[("BT", tt), ("CT", tt)], writes=[("ps", pbc)])
                S_.op("dve", lambda e: e.tensor_tensor(CBm[par], psc[:, 0:128], self.tri, ALU.mult), reads=[("ps", pbc), "cst"], writes=[("CBm", par)])
                r128 = lambda ap: ap.rearrange("p (r d) -> p r d", r=8)
                S_.op("dve", lambda e: e.tensor_tensor(r128(rhsD), self.tri.unsqueeze(1).to_broadcast([128, 8, 128]),
                                                       v3(adt, 16)[:, c, h8].unsqueeze(2).to_broadcast([128, 8, 128]), ALU.mult),
                      reads=["cst", "adt"], writes=["rhsD"])
                for i in range(2):
                    pbd = self.next_ps(); psd = self.ps[pbd]
                    S_.op("pe", lambda e, psd=psd, i=i: e.matmul(psd[:, :], self.strictT, rhsD[:, i * 512:(i + 1) * 512], start=True, stop=True),
                          reads=["cst", "rhsD"], writes=[("ps", pbd)])
                    S_.op("act", lambda e, psd=psd, i=i: e.activation(MT[par][:, i * 512:(i + 1) * 512], psd[:, :], AF.Exp), reads=[("ps", pbd)], writes=[("MT", par, i)])
                pbz = self.next_ps(); psz = self.ps[pbz]
                for k in range(KC):
                    S_.op("pe", lambda e, k=k: e.matmul(psz[:, :], self.hn[:, k, l0:l0 + 128], Wz[:, k, :], start=(k == 0), stop=(k == KC - 1)),
                          reads=[("w", bz), ("hn", k, tt)], writes=[("ps", pbz)])
                S_.op("act", lambda e: e.activation(zs[par], psz[:, :], AF.Silu), reads=[("ps", pbz)], writes=[("zs", par)])

            def stageA2(c):
                par = c % 2
                tt = c // 4
                l0 = c * 128
                r128 = lambda ap: ap.rearrange("p (r d) -> p r d", r=8)
                S_.op("dve", lambda e: e.tensor_tensor(r128(MT[par]), r128(MT[par]), CBm[par].unsqueeze(1).to_broadcast([128, 8, 128]), ALU.mult),
                      reads=[("MT", par, 0), ("MT", par, 1), ("CBm", par)], writes=[("MT", par, 0), ("MT", par, 1)])
                pby = self.next_ps(); psy = self.ps[pby]
                for j in range(4):
                    S_.op("pe", lambda e, j=j: e.matmul(psy[:, j * 128:(j + 1) * 128], xc[:, j * S + l0:j * S + l0 + 128], diagD[:, j * 128:(j + 1) * 128], start=True, stop=False),
                          reads=[("xc", j, tt), ("diagD", j)], writes=[("ps", pby)])
                    for r in (2 * j, 2 * j + 1):
                        S_.op("pe", lambda e, r=r, j=j: e.matmul(psy[:, r * 64:(r + 1) * 64], MT[par][:, r * 128:(r + 1) * 128], xdt[par][:, r * 64:(r + 1) * 64], start=False, stop=(r == 2 * j + 1)),
                              reads=[("MT", par, 0), ("MT", par, 1), ("xdt", par)], writes=[("ps", pby)])
                pbo = self.next_ps(); pso = self.ps[pbo]
                S_.op("pe", lambda e: e.matmul(pso[:, :], CT[:, l0:l0 + 128], state_bf, start=True, stop=True), reads=[("CT", tt), "state_bf"], writes=[("ps", pbo)])
                pbs = self.next_ps(); pss = self.ps[pbs]
                S_.op("pe", lambda e: e.matmul(pss[:, :], Btok[par], xsc[par], start=True, stop=True), reads=[("Btok", par), ("xsc", par)], writes=[("ps", pbs)])
                T = t1[par]
                S_.op("dve", lambda e: e.tensor_tensor(r64(T), r64(pso[:, :]), bc64(Ea, c), ALU.mult), reads=[("ps", pbo), "Ea"], writes=[("t1", par)])
                S_.op("dve", lambda e: e.tensor_tensor(T, T, psy[:, :], ALU.add), reads=[("ps", pby), ("t1", par)], writes=[("t1", par)])
                S_.op("dve", lambda e: e.tensor_tensor(T, T, zs[par], ALU.mult), reads=[("zs", par), ("t1", par)], writes=[("t1", par)])
                S_.op("act", lambda e: e.activation(gn[par], T, AF.Square, accum_out=ssb[par]), reads=[("t1", par)], writes=[("gn", par), ("ss", par)])
                S_.op("act", lambda e: e.activation(rsb[par], ssb[par], AF.Sqrt, bias=1e-5, scale=1.0 / 512), reads=[("ss", par)], writes=[("rs", par)])
                S_.op("dve", lambda e: e.reciprocal(rsb[par], rsb[par]), reads=[("rs", par)], writes=[("rs", par)])
                S_.op("dve", lambda e: e.scalar_tensor_tensor(gn[par], T, rsb[par], normw, ALU.mult, ALU.mult), reads=[("t1", par), ("rs", par), "normw"], writes=[("gn", par)])
                S_.op("dve", lambda e: e.tensor_tensor(r64(state), r64(state), bc64(Etot, c), ALU.mult), reads=["state", "Etot"], writes=["state"])
                S_.op("dve", lambda e: e.tensor_tensor(state, state, pss[:, :], ALU.add), reads=["state", ("ps", pbs)], writes=["state"])
                S_.op("pool", lambda e: e.tensor_copy(state_bf, state), reads=["state"], writes=["state_bf"])

            def stageB(c):
                par = c % 2
                tt = c // 4
                q = c % 4
                G = gnT[tt % 2].rearrange("p (j t) -> p j t", j=4)
                pbt = self.next_ps(); pst = self.ps[pbt][:, :].bitcast(BF16)
                for j in range(4):
                    S_.op("pe", lambda e, j=j: e.transpose(pst[:, j * 128:(j + 1) * 128], gn[par][:, j * 128:(j + 1) * 128], self.identb[:, :]),
                          reads=[("gn", par), "identb"], writes=[("ps", pbt)])
                S_.op("act", lambda e: e.activation(G[:, :, q * 128:(q + 1) * 128], pst[:, 0:512].rearrange("p (j t) -> p j t", j=4), AF.Identity),
                      reads=[("ps", pbt)], writes=[("gnT", tt % 2, q)])
                if q == 3:
                    for m in range(KC):
                        pbm = self.next_ps(); psm = self.ps[pbm]
                        for k in range(4):
                            S_.op("pe", lambda e, m=m, k=k, psm=psm: e.matmul(psm[:, :], Wo[:, k, m * 128:(m + 1) * 128], G[:, k, :], start=(k == 0), stop=(k == 3)),
                                  reads=[("w", bo)] + [("gnT", tt % 2, qq) for qq in range(4)], writes=[("ps", pbm)])
                        hv = self.hT[:, m, tt * TT:(tt + 1) * TT]
                        S_.op("dve", lambda e, hv=hv, psm=psm: e.tensor_tensor(hv, psm[:, :], hv, ALU.add), reads=[("ps", pbm), ("h", m, tt)], writes=[("h", m, tt)])

            for c in range(18):
                if c < 16:
                    stageA1(c)
                if 0 <= c - 1 < 16:
                    stageA2(c - 1)
                if 0 <= c - 2 < 16:
                    stageB(c - 2)

        for g in range(4):
            do_group(g)
        S_.fence()

    def kv_stage(self, si):
        S_ = self.S
        A = self.carve
        self.rmsnorm("kv_norm", reverse=True)
        S_.fence()
        stg = [A(i * 256, 256, BF16) for i in range(4)]
        it = [0]

        def rhs_fn(k, tt):
            return self.hn[:, k, tt * TT:(tt + 1) * TT], ("hn", k, tt)

        def epi(gi, mi, tt, ps, pkey):
            hp = gi * 4 + mi
            par = it[0] % 4
            it[0] += 1
            sb = stg[par]
            S_.op("act", lambda e: e.activation(sb, ps[:, :], AF.Identity), reads=[pkey], writes=[("stg", par)])
            dst = self.kTd[hp, :, tt * TT:(tt + 1) * TT]
            S_.dma("sp", lambda e: e.dma_start(out=dst, in_=sb), reads=[("stg", par)], writes=[("kTd", hp)])

        self.linear(self.w_kv, (0, 8), [[0, 128, 256, 384], [512, 640, 768, 896]], rhs_fn, epi)
        Wv = []
        for cg in range(2):
            b = cg
            wt = self.wbuf[b][:, 0:4096].rearrange("p (k m) -> p k m", k=8)
            src = self.w_kv[:, 1024 + cg * 512:1024 + (cg + 1) * 512].rearrange("(k p) m -> p k m", p=128)
            S_.dma("pool", lambda e, wt=wt, src=src: e.dma_start(out=wt, in_=src), writes=[("w", b)])
            Wv.append(wt)
        for jb in range(16):
            for cg in range(2):
                pb = self.next_ps(); ps = self.ps[pb]
                for k in range(KC):
                    S_.op("pe", lambda e, ps=ps, k=k, jb=jb, cg=cg: e.matmul(ps[:, :], self.hn[:, k, jb * 128:(jb + 1) * 128], Wv[cg][:, k, :], start=(k == 0), stop=(k == KC - 1)),
                          reads=[("w", cg), ("hn", k, jb // 4)], writes=[("ps", pb)])
                par = it[0] % 4
                it[0] += 1
                sb = stg[par]
                S_.op("act", lambda e, sb=sb, ps=ps: e.activation(sb, ps[:, :], AF.Identity), reads=[("ps", pb)], writes=[("stg", par)])
                dst = self.Vd[cg * 4:(cg + 1) * 4, :, :, jb * 64:(jb + 1) * 64].rearrange("h f j d -> j h f d")
                srcv = sb.rearrange("p (h f d) -> p h f d", h=4, f=2)
                S_.dma("sp", lambda e, dst=dst, srcv=srcv: e.dma_start(out=dst, in_=srcv), reads=[("stg", par)],
                       writes=[("Vd", cg * 4 + hh) for hh in range(4)])
        S_.fence()

    def attention(self, li, si):
        S_ = self.S
        A = self.carve
        j_ = li - NA
        scale = 0.125
        self.rmsnorm(("attn_norm", li))
        S_.fence()
        oT = A(0, 8192, BF16).rearrange("p (c t) -> p c t", c=8)
        o = 8192
        kT = [A(o + i * 1024, 1024, BF16) for i in range(2)]; o += 2048
        Vb = [A(o + i * 2048, 2048, BF16) for i in range(2)]; o += 4096
        qT = [A(o + i * 1024, 1024, BF16) for i in range(2)]; o += 2048
        eb = [A(o + i * 512, 512, F32) for i in range(2)]; o += 1024
        cb = [A(o + i * 512, 512, F32) for i in range(2)]; o += 1024
        wb = [A(o + i * 256, 256, BF16) for i in range(2)]; o += 512
        wTb = [A(o + i * 256, 256, BF16) for i in range(2)]; o += 512
        assert o <= 19968, o
        Wq = []
        for cg in range(2):
            wt = self.wbuf[cg][:, 0:4096].rearrange("p (k m) -> p k m", k=8)
            src = self.w_q[j_][:, cg * 512:(cg + 1) * 512].rearrange("(k p) m -> p k m", p=128)
            S_.dma("pool", lambda e, wt=wt, src=src: e.dma_start(out=wt, in_=src), writes=[("w", cg)])
            Wq.append(wt)
        for i in range(2):
            vz = Vb[i].rearrange("p (f b c d) -> p f b c d", f=2, b=16, c=2)
            S_.op("dve", lambda e, vz=vz: e.memset(vz[:, 0, :, 1, :], 0.0), writes=[("Vz", i)])
            S_.op("dve", lambda e, vz=vz: e.memset(vz[:, 1, :, 0, :], 0.0), writes=[("Vz", i)])
        self.ps_lo = 2
        self.ps_rr = 0
        seg_it = [0]
        po_it = [0]

        def load_hp(hp):
            bpar = hp % 2
            S_.dma("sp", lambda e: e.dma_start(out=kT[bpar], in_=self.kTd[hp]), reads=[("kTd", hp)], writes=[("kT", bpar)])
            vz = Vb[bpar].rearrange("p (f b c d) -> p f b c d", f=2, b=16, c=2)
            for f in range(2):
                src = self.Vd[hp, f].rearrange("j (b d) -> j b d", b=16)
                dst = vz[:, f, :, f, :]
                S_.dma("sp", lambda e, dst=dst, src=src: e.dma_start(out=dst, in_=src), reads=[("Vd", hp)], writes=[(("VA", "VB")[f], bpar)])

        def q_proj(hp):
            cg, hl = hp // 4, hp % 4
            for tt in range(NT):
                pb = self.next_ps(); ps = self.ps[pb]
                for k in range(KC):
                    S_.op("pe", lambda e, ps=ps, k=k, tt=tt: e.matmul(ps[:, :], Wq[cg][:, k, hl * 128:(hl + 1) * 128], self.hn[:, k, tt * TT:(tt + 1) * TT], start=(k == 0), stop=(k == KC - 1)),
                          reads=[("w", cg), ("hn", k, tt)], writes=[("ps", pb)])
                S_.op("act", lambda e, ps=ps, tt=tt: e.activation(qT[hp % 2][:, tt * TT:(tt + 1) * TT], ps[:, :], AF.Identity), reads=[("ps", pb)], writes=[("qT", hp % 2, tt)])

        class Seg:
            pass

        segs = []
        for hp in range(8):
            for i in range(16):
                nseg = (i + 1 + 3) // 4
                lst = [(half, sg) for half in range(2) for sg in range(nseg)]
                for n_, (half, sg) in enumerate(lst):
                    g = Seg()
                    g.hp, g.i, g.half, g.sg = hp, i, half, sg
                    g.first = (n_ == 0)
                    g.last = (n_ == len(lst) - 1)
                    g.bpar = hp % 2
                    g.rows = slice(half * 64, half * 64 + 64)
                    j0 = S - (i + 1) * 128
                    g.js = j0 + sg * 512
                    g.n = min(512, S - g.js)
                    segs.append(g)
        for idx, g in enumerate(segs):
            g.par = idx % 2
            g.par4 = idx % 4
        qb = {}
        for g in segs:
            key = (g.hp, g.i)
            if key not in qb:
                qb[key] = len(qb) % 2
            g.pbo = qb[key]

        def stA(g):
            n = g.n
            g.pb = self.next_ps()
            ps = self.ps[g.pb]
            E = eb[g.par]
            S_.op("pe", lambda e: e.matmul(ps[:, 0:n], qT[g.hp % 2][g.rows, g.i * 128:(g.i + 1) * 128], kT[g.bpar][g.rows, g.js:g.js + n], start=True, stop=True),
                  reads=[("qT", g.hp % 2, g.i // 4), ("kT", g.bpar)], writes=[("ps", g.pb)])
            S_.op("act", lambda e: e.activation(E[:, 0:n], ps[:, 0:n], AF.Exp, scale=scale), reads=[("ps", g.pb)], writes=[("e", g.par)])
            S_.op("act", lambda e: e.activation(E[:, 0:n], E[:, 0:n], AF.Ln, bias=1.0), reads=[("e", g.par)], writes=[("e", g.par)])

        def stB(g):
            n = g.n
            ps = self.ps[g.pb]
            E = eb[g.par]; C = cb[g.par]
            if g.sg == 0:
                S_.op("dve", lambda e: e.tensor_tensor(E[:, 0:128], E[:, 0:128], self.amask, ALU.mult), reads=[("e", g.par), "cst"], writes=[("e", g.par)])
                init = 0.0
                rd = []
            else:
                init = cb[1 - g.par][:, 511:512]
                rd = [("c", 1 - g.par)]
            S_.op("dve", lambda e: e.tensor_tensor_scan(C[:, 0:n], self.onesb[:, 0:n], E[:, 0:n], init, ALU.mult, ALU.add),
                  reads=[("e", g.par), "onesb"] + rd, writes=[("c", g.par)])
            S_.op("dve", lambda e: e.scalar_tensor_tensor(E[:, 0:n], ps[:, 0:n], scale, C[:, 0:n], ALU.mult, ALU.subtract),
                  reads=[("ps", g.pb), ("c", g.par)], writes=[("e", g.par)])
            if g.sg == 0:
                S_.op("dve", lambda e: e.tensor_tensor(E[:, 0:128], E[:, 0:128], self.negmask, ALU.add), reads=[("e", g.par), "cst"], writes=[("e", g.par)])

        def stC(g):
            n = g.n
            nb = n // 128
            E = eb[g.par]; W = wb[g.par]
            S_.op("act", lambda e: e.activation(W[:, 0:n], E[:, 0:n], AF.Exp), reads=[("e", g.par)], writes=[("wsb", g.par)])
            g.pbt = self.next_ps()
            pst = self.ps[g.pbt][:, :].bitcast(BF16)
            for b_ in range(nb):
                S_.op("pe", lambda e, b_=b_: e.transpose(pst[:, b_ * 128:(b_ + 1) * 128], W[:, b_ * 128:(b_ + 1) * 128], self.identb[:, :]),
                      reads=[("wsb", g.par), "identb"], writes=[("ps", g.pbt)])

        def stD(g):
            n = g.n
            nb = n // 128
            pst = self.ps[g.pbt][:, :].bitcast(BF16)
            WT = wTb[g.par]
            pso = self.ps[g.pbo]; pok = ("ps", g.pbo)
            S_.op("act", lambda e: e.activation(WT[:, 0:n], pst[:, 0:n], AF.Identity), reads=[("ps", g.pbt)], writes=[("wT", g.par)])
            base = 0 if g.half == 0 else 2048
            vkey = ("VA", g.bpar) if g.half == 0 else ("VB", g.bpar)
            for b_ in range(nb):
                jb = (g.js + b_ * 128) // 128
                off = base + jb * 128
                lhsT = Vb[g.bpar][:, off:off + 128]
                S_.op("pe", lambda e, b_=b_, lhsT=lhsT: e.matmul(pso[:, 0:128], lhsT, WT[:, b_ * 128:(b_ + 1) * 128],
                                                               start=(g.first and b_ == 0), stop=(g.last and b_ == nb - 1)),
                      reads=[("wT", g.par), vkey, ("Vz", g.bpar)], writes=[pok])
            if g.last:
                S_.op("act", lambda e: e.activation(oT[:, g.hp, g.i * 128:(g.i + 1) * 128], pso[:, 0:128], AF.Identity), reads=[pok], writes=[("oT", g.hp, g.i // 4)])

        load_hp(0)
        NS = len(segs)
        for k in range(NS + 2):
            if k < NS:
                g = segs[k]
                if g.i == 0 and g.first:
                    if g.hp + 1 < 8:
                        load_hp(g.hp + 1)
                    q_proj(g.hp)
                stA(g)
                stB(g)
            if 0 <= k - 1 < NS:
                stC(segs[k - 1])
            if 0 <= k - 2 < NS:
                stD(segs[k - 2])
        self.ps_lo = 0
        self.ps_rr = 0

        def rhs_o(k, tt):
            return oT[:, k, tt * TT:(tt + 1) * TT], ("oT", k, tt)

        def epi_o(gi, mi, tt, ps, pkey):
            c = gi * 4 + mi
            hv = self.hT[:, c, tt * TT:(tt + 1) * TT]
            S_.op("dve", lambda e: e.tensor_tensor(hv, ps[:, :], hv, ALU.add), reads=[pkey, ("h", c, tt)], writes=[("h", c, tt)])

        self.linear(self.w_o[j_], (0, 8), [[0, 128, 256, 384], [512, 640, 768, 896]], rhs_o, epi_o)
        S_.fence()

    def ple(self, li, si):
        S_ = self.S
        self.rmsnorm(("ple_norm", li), top=True)
        pb16 = self.carve(0, 2048, BF16)
        sg = [self.carve(2048 + i * 512, 512, F32) for i in range(2)]
        src = self.pT[li, si].rearrange("(k p) t -> p k t", p=128)
        dst = pb16.rearrange("p (k t) -> p k t", k=2)
        S_.dma("pool", lambda e: e.dma_start(out=dst, in_=src), reads=(), writes=["pT"])
        wp = self.carve(3072, 1024, BF16).rearrange("p (k m) -> p k m", k=2)
        srcw = self.ple_proj[li].rearrange("(k p) m -> p k m", p=128)
        S_.dma("pool", lambda e: e.dma_start(out=wp, in_=srcw), reads=(), writes=["wp"])
        it = [0]

        def rhs_fn(k, tt):
            return self.hn[:, k, tt * TT:(tt + 1) * TT], ("hn", k, tt)

        def epi(gi, mi, tt, ps, pkey):
            c = gi * 4 + mi
            t0 = tt * TT
            par = it[0] % 2
            it[0] += 1
            s = sg[par]
            S_.op("act", lambda e: e.activation(s, ps[:, :], AF.Sigmoid), reads=[pkey], writes=[("sg", par)])
            pb2 = self.next_ps()
            ps2 = self.ps[pb2]
            for k in range(2):
                S_.op("pe", lambda e, k=k: e.matmul(ps2[:, :], wp[:, k, c * 128:(c + 1) * 128], pb16[:, k * S + t0:k * S + t0 + TT], start=(k == 0), stop=(k == 1)),
                      reads=["wp", "pT"], writes=[("ps", pb2)])
            S_.op("dve", lambda e: e.tensor_tensor(s, s, ps2[:, :], ALU.mult), reads=[("sg", par), ("ps", pb2)], writes=[("sg", par)])
            hv = self.hT[:, c, t0:t0 + TT]
            S_.op("dve", lambda e: e.tensor_tensor(hv, hv, s, ALU.add), reads=[("sg", par), ("h", c, tt)], writes=[("h", c, tt)])

        self.linear(self.ple_gate[li], (0, 8), [[0, 128, 256, 384], [512, 640, 768, 896]], rhs_fn, epi)
        S_.fence()

    def build(self):
        S_ = self.S
        nc = self.nc
        cfg = self.cfg
        S_.op("dve", lambda e: e.memset(self.ones[:, :], 1.0), writes=["ones"])
        S_.op("dve", lambda e: e.memset(self.onesb[:, :], 1.0), writes=["onesb"])
        S_.dma("sp", lambda e: e.dma_start(out=self.vecs[:, :], in_=self.vecs_d[:, :]), writes=["vecs"])
        S_.dma("sp", lambda e: e.dma_start(out=self.cst[:, :], in_=self.consts_d[:, :]), writes=["cst"])
        S_.op("dve", lambda e: e.tensor_copy(self.identb[:, :], self.cst[:, 384:512]), reads=["cst"], writes=["identb"])
        for si in range(self.nseq):
            for c in range(KC):
                src = self.xT[si, c * 128:(c + 1) * 128, :]
                dst = self.hT[:, c, :]
                S_.dma("sp", lambda e, dst=dst, src=src: e.dma_start(out=dst, in_=src),
                       writes=[("h", c, tt) for tt in range(NT)])
            for li in cfg["layers"]:
                if cfg.get("mixer", True):
                    if li < NA:
                        self.mamba(li, si)
                    else:
                        self.attention(li, si)
                if cfg.get("ffn", True):
                    self.ffn(li)
                if cfg.get("ple", True):
                    self.ple(li, si)
                if li == NA - 1 and cfg.get("mixer", True) and any(l >= NA for l in cfg["layers"]):
                    self.kv_stage(si)
            S_.fence()
            ob2 = self.carve(4096, 2 * 4096, F32)

            self._final_norm_store(si, ob2)
            S_.fence()
        S_.emit(nc)
        return nc

    def _final_norm_store(self, si, ob2):
        S_ = self.S
        eps = 1e-6
        sq = self.carve(0, 2048, BF16)
        rr = [self.carve(2048, 512, F32), self.carve(2560, 512, F32)]
        for tt in range(NT):
            t0 = tt * TT
            pb = self.next_ps()
            ps = self.ps[pb]
            for c in range(KC):
                o = sq[:, c * 512:(c + 1) * 512]
                i = self.hT[:, c, t0:t0 + TT]
                S_.op("act", lambda e, o=o, i=i: e.activation(o, i, AF.Square), reads=[("h", c, tt)], writes=[("sq", c)])
            for c in range(KC):
                r_ = sq[:, c * 512:(c + 1) * 512]
                S_.op("pe", lambda e, ps=ps, r_=r_, c=c: e.matmul(ps[:, :], self.ones[:, :], r_, start=(c == 0), stop=(c == KC - 1)),
                      reads=[("sq", c), "ones"], writes=[("ps", pb)])
            r = rr[tt % 2]
            rk = ("rstd", tt % 2)
            S_.op("act", lambda e, r=r, ps=ps: e.activation(r, ps[:, :], AF.Sqrt, bias=eps, scale=1.0 / D), reads=[("ps", pb)], writes=[rk])
            S_.op("dve", lambda e, r=r: e.reciprocal(r, r), reads=[rk], writes=[rk])
            for c in range(KC):
                o = ob2[:, (tt % 2) * 4096 + c * 512:(tt % 2) * 4096 + (c + 1) * 512]
                i = self.hT[:, c, t0:t0 + TT]
                g = self.vcol("final_norm", c)
                S_.op("dve", lambda e, o=o, i=i, g=g, r=r: e.scalar_tensor_tensor(o, i, g, r, ALU.mult, ALU.mult),
                      reads=[("h", c, tt), rk, "vecs"], writes=[("ob", tt % 2, c)])
                dst = self.outT[si, c * 128:(c + 1) * 128, t0:t0 + TT]
                S_.dma("sp", lambda e, dst=dst, o=o: e.dma_start(out=dst, in_=o), reads=[("ob", tt % 2, c)], writes=[])


NCORES = 8
_prog_cache = {}


def get_prog(nseq, cfg):
    key = (nseq, repr(sorted(cfg.items())))
    if key not in _prog_cache:
        p = Prog(nseq, cfg)
        p.build()
        _prog_cache[key] = p
    return _prog_cache[key]


def make_in_maps(inp, batch_ids_per_core):
    vecs = pack_vecs(inp)
    shared = dict(vecs=vecs, consts=make_consts(), bvecs=pack_bvecs(inp),
                  ssm_in_proj=np.ascontiguousarray(inp["ssm_in_proj"], np.float32),
                  ssm_out_proj=np.ascontiguousarray(inp["ssm_out_proj"], np.float32),
                  w_kv=np.ascontiguousarray(inp["w_kv"], np.float32),
                  w_q=np.ascontiguousarray(inp["w_q"], np.float32),
                  w_o=np.ascontiguousarray(inp["w_o"], np.float32),
                  ffn_up=np.ascontiguousarray(inp["ffn_up"], np.float32),
                  ffn_down=np.ascontiguousarray(inp["ffn_down"], np.float32),
                  ple_gate=np.ascontiguousarray(inp["ple_gate"], np.float32),
                  ple_proj=np.ascontiguousarray(inp["ple_proj"], np.float32))
    in_maps = []
    for ids in batch_ids_per_core:
        xT = np.ascontiguousarray(np.transpose(inp["x"][ids], (0, 2, 1)))
        pT = np.ascontiguousarray(np.transpose(inp["p"][:, ids], (0, 1, 3, 2)))
        m = dict(shared)
        m["xT"] = xT
        m["pT"] = pT
        in_maps.append(m)
    return in_maps


def run_cores(inp, batch_ids_per_core, cfg):
    nseq = len(batch_ids_per_core[0])
    prog = get_prog(nseq, cfg)
    in_maps = make_in_maps(inp, batch_ids_per_core)
    res = run_bass_kernel_spmd(prog.nc, in_maps, core_ids=list(range(len(in_maps))))
    outs = []
    for r in res.results:
        outs.append(np.transpose(r["outT"], (0, 2, 1)))
    return outs


FULL_CFG = dict(layers=(0, 1, 2, 3), mixer=True, ffn=True, ple=True)


def kernel(**inp):
    inp = {k: np.asarray(v) for k, v in inp.items()}
    B = inp["x"].shape[0]
    per = B // NCORES
    ids = [[c * per + g for g in range(per)] for c in range(NCORES)]
    outs = run_cores(inp, ids, FULL_CFG)
    out = np.empty((B, S, D), np.float32)
    for c in range(NCORES):
        for g in range(per):
            out[c * per + g] = outs[c][g]
    return out
```

```python
import numpy as np
import concourse.bass as bass
import concourse.mybir as mybir
from concourse.bass_utils import run_bass_kernel_spmd

F32 = mybir.dt.float32
BF16 = mybir.dt.bfloat16
AF = mybir.ActivationFunctionType
ALU = mybir.AluOpType

ENG = ["pe", "act", "dve", "pool", "sp"]
N_DMA_SEMS = 8

D = 1024
S = 2048
NT = 4
TT = 512
KC = 8
DFF = 2816
NFF = 22
PLE = 256
DEPTH = 4
NA = 2


class Sched:
    def __init__(self):
        self.ops = {e: [] for e in ENG}
        self.lastw = {}
        self.readers = {}
        self.dma_names = ["dma_%s%d" % (q, j) for q in ("sp", "pool") for j in range(N_DMA_SEMS)]
        self.dma_cnt = {n: 0 for n in self.dma_names}
        self.dma_rr = {"sp": 0, "pool": 0}

    def _need(self, deps, me, res, pos, allow_same=False):
        if res == me and not allow_same:
            return
        if deps.get(res, 0) < pos:
            deps[res] = pos

    def _deps(self, me, reads, writes, allow_same=False):
        deps = {}
        for k in reads:
            w = self.lastw.get(k)
            if w:
                self._need(deps, me, w[0], w[1], allow_same)
            if isinstance(k, tuple) and k[0] == "ps":
                for r, p in self.readers.get(k, {}).items():
                    self._need(deps, me, r, p, False)
        for k in writes:
            w = self.lastw.get(k)
            if w:
                self._need(deps, me, w[0], w[1], allow_same)
            for r, p in self.readers.get(k, {}).items():
                self._need(deps, me, r, p, allow_same)
        return deps

    def _mark(self, res, pos, reads, writes):
        for k in reads:
            self.readers.setdefault(k, {})[res] = pos
        for k in writes:
            self.lastw[k] = (res, pos)
            self.readers[k] = {}

    def op(self, eng, fn, reads=(), writes=()):
        deps = self._deps(eng, reads, writes, allow_same=(eng != "pe"))
        self.ops[eng].append(dict(fn=fn, deps=deps, dma=None))
        self._mark(eng, len(self.ops[eng]), reads, writes)

    def dma(self, eng, fn, reads=(), writes=()):
        j = self.dma_rr[eng]
        self.dma_rr[eng] = (j + 1) % N_DMA_SEMS
        res = "dma_%s%d" % (eng, j)
        k = self.dma_cnt[res] + 1
        self.dma_cnt[res] = k
        deps = self._deps(eng, reads, writes, allow_same=True)
        if k > 1:
            self._need(deps, eng, res, k - 1)
        self.ops[eng].append(dict(fn=fn, deps=deps, dma=res))
        self._mark(res, k, reads, writes)

    def fence(self):
        tail = {}
        for e in ENG:
            for pos in range(len(self.ops[e]), 0, -1):
                o = self.ops[e][pos - 1]
                if o["fn"] is not None and o["dma"] is None:
                    tail[e] = pos
                    break
        dtail = {n: c for n, c in self.dma_cnt.items() if c}
        for e in ENG:
            deps = {r: p for r, p in tail.items() if r != e}
            deps.update(dtail)
            self.ops[e].append(dict(fn=None, deps=deps, dma=None))

    def emit(self, nc):
        sig = {e: set() for e in ENG}
        for e in ENG:
            for o in self.ops[e]:
                for r, p in o["deps"].items():
                    if r in sig:
                        sig[r].add(p)
        rank = {}
        for e in ENG:
            for i, p in enumerate(sorted(sig[e])):
                rank[(e, p)] = i + 1
        from contextlib import ExitStack
        with ExitStack() as st:
            sems = {e: st.enter_context(nc.semaphore("s_" + e)) for e in ENG}
            for n in self.dma_names:
                sems[n] = st.enter_context(nc.semaphore("s_" + n))
            block = st.enter_context(nc.Block())
            engobj = {}

            def run(ename, eng):
                seen = {}
                for pos, o in enumerate(self.ops[ename], start=1):
                    for r, p in o["deps"].items():
                        val = 16 * p if r.startswith("dma") else rank[(r, p)]
                        if seen.get(r, 0) >= val:
                            continue
                        eng.wait_ge(sems[r], val)
                        seen[r] = val
                    if o["fn"] is None:
                        continue
                    ins = o["fn"](eng)
                    if o["dma"] is not None:
                        ins.then_inc(sems[o["dma"]], 16)
                    elif pos in sig[ename]:
                        ins.then_inc(sems[ename], 1)
                if ename == "sp":
                    for n, c in self.dma_cnt.items():
                        if c and seen.get(n, 0) < 16 * c:
                            eng.wait_ge(sems[n], 16 * c)

            @block.tensor
            def _(e):
                run("pe", e)

            @block.scalar
            def _(e):
                run("act", e)

            @block.vector
            def _(e):
                run("dve", e)

            @block.gpsimd
            def _(e):
                run("pool", e)

            @block.sync
            def _(e):
                run("sp", e)


def _fm(v):
    v = np.asarray(v, np.float32)
    return np.ascontiguousarray(v.reshape(-1, 128).T)


class VecLayout:
    def __init__(self):
        self.off = {}
        self.n = 0

    def add(self, name, ncols):
        self.off[name] = self.n
        self.n += ncols


def vec_layout():
    L = VecLayout()
    for i in range(DEPTH):
        L.add(("attn_norm", i), 8)
        L.add(("ffn_norm", i), 8)
        L.add(("ple_norm", i), 8)
        for k in range(3):
            L.add(("ffn_cw", i, k), 44)
        L.add(("ffn_cb", i), 44)
    L.add("kv_norm", 8)
    L.add("final_norm", 8)
    for i in range(NA):
        for k in range(4):
            L.add(("ssm_cw", i, k), 24)
        L.add(("ssm_cb", i), 24)
        L.add(("ssm_dfm", i), 16)
    return L


def pack_vecs(inp):
    L = vec_layout()
    out = np.zeros((128, L.n), np.float32)

    def put(name, v):
        a = _fm(v)
        out[:, L.off[name]:L.off[name] + a.shape[1]] = a

    for i in range(DEPTH):
        put(("attn_norm", i), inp["attn_norm"][i])
        put(("ffn_norm", i), inp["ffn_norm"][i])
        put(("ple_norm", i), inp["ple_norm"][i])
        for k in range(3):
            put(("ffn_cw", i, k), inp["ffn_conv_w"][i][k])
        put(("ffn_cb", i), inp["ffn_conv_b"][i])
    put("kv_norm", inp["kv_norm"])
    put("final_norm", inp["final_norm"])
    for i in range(NA):
        for k in range(4):
            put(("ssm_cw", i, k), inp["ssm_conv_w"][i][k])
        put(("ssm_cb", i), inp["ssm_conv_b"][i])
        put(("ssm_dfm", i), np.repeat(np.asarray(inp["ssm_d"][i], np.float32), 64))
    return out


def pack_bvecs(inp):
    out = np.zeros((128, NA * 3072), np.float32)
    for i in range(NA):
        o = i * 3072
        out[:, o:o + 512] = np.tile(np.asarray(inp["ssm_dt_bias"][i], np.float32), 16)[None, :]
        out[:, o + 512:o + 1024] = np.tile(np.asarray(inp["ssm_a_log"][i], np.float32), 16)[None, :]
        out[:, o + 1024:o + 3072] = np.asarray(inp["ssm_norm"][i], np.float32)[None, :]
    return out


def make_consts():
    k = np.arange(128)
    c = np.zeros((128, 768), np.float32)
    am = ((k[:, None] + k[None, :]) >= 128).astype(np.float32)
    c[:, 512:640] = am
    c[:, 640:768] = (am - 1.0) * 30000.0
    c[:, 0:128] = (k[:, None] <= k[None, :])
    c[:, 128:256] = (k[:, None] > k[None, :])
    c[:, 256:384] = 1.0
    c[:, 384:512] = np.eye(128, dtype=np.float32)
    return c


class Prog:
    def __init__(self, nseq, cfg):
        self.nseq = nseq
        self.cfg = cfg
        self.nc = nc = bass.Bass("TRN2", target_bir_lowering=False)
        self.S = Sched()
        self.VL = vec_layout()
        di = lambda name, shape: nc.dram_tensor(name, shape, F32, kind="ExternalInput").ap()
        self.xT = di("xT", [nseq, D, S])
        self.pT = di("pT", [DEPTH, nseq, PLE, S])
        self.vecs_d = di("vecs", [128, self.VL.n])
        self.ffn_up = di("ffn_up", [DEPTH, D, 2 * DFF])
        self.ffn_down = di("ffn_down", [DEPTH, DFF, D])
        self.ple_gate = di("ple_gate", [DEPTH, D, D])
        self.ple_proj = di("ple_proj", [DEPTH, PLE, D])
        self.consts_d = di("consts", [128, 768])
        self.bvecs_d = di("bvecs", [128, NA * 3072])
        self.ssm_in = di("ssm_in_proj", [NA, D, 5152])
        self.ssm_out = di("ssm_out_proj", [NA, 2048, D])
        self.w_kv = di("w_kv", [D, 2048])
        self.w_q = di("w_q", [2, D, D])
        self.w_o = di("w_o", [2, D, D])
        self.kTd = nc.dram_tensor("kTd", [8, 128, S], BF16).ap()
        self.Vd = nc.dram_tensor("Vd", [8, 2, 128, 16 * 64], BF16).ap()
        self.outT = nc.dram_tensor("outT", [nseq, D, S], F32, kind="ExternalOutput").ap()

        A = nc.alloc_sbuf_tensor
        self.hT = A("hT", [128, KC, S], F32)
        self.hn = A("hn", [128, KC, S], BF16)
        self.vecs = A("vecsb", [128, self.VL.n], F32)
        self.wbuf = [A("wbuf%d" % i, [128, 6144], BF16) for i in range(2)]
        self.ones = A("ones", [128, 128], BF16)
        self.arena = A("arena", [128, 19968], F32)
        self.cst = A("cst", [128, 768], F32)
        self.amask = self.cst[:, 512:640]
        self.negmask = self.cst[:, 640:768]
        self.onesb = A("onesb", [128, 512], BF16)
        self.identb = A("identb", [128, 128], BF16)
        self.tri = self.cst[:, 0:128]
        self.strictT = self.cst[:, 128:256]
        self.onesf = self.cst[:, 256:384]
        self.ps = [nc.alloc_psum_tensor("ps%d" % i, [128, 512], F32) for i in range(8)]
        self.ps_rr = 0
        self.ps_lo = 0
        self.w_rr = 0
        print("sbuf remaining", nc.sbuf_bytes_remaining)

    def vcol(self, name, c):
        o = self.VL.off[name] + c
        return self.vecs[:, o:o + 1]

    def next_ps(self):
        lo = self.ps_lo
        i = self.ps_rr
        self.ps_rr = (i + 1) % (8 - lo)
        return lo + i

    def carve(self, off_f32, n, dtype, shape=None):
        ap = self.arena[:, off_f32:off_f32 + n]
        if dtype == BF16:
            ap = ap.bitcast(BF16)
        return ap

    def load_w(self, wd, rows, c0, ncols, eng="pool"):
        r0, nk = rows
        b = self.w_rr
        self.w_rr = 1 - b
        dst = self.wbuf[b][:, 0:nk * ncols].rearrange("p (k m) -> p k m", k=nk)
        src = wd[r0:r0 + nk * 128, c0:c0 + ncols].rearrange("(k p) m -> p k m", p=128)
        self.S.dma(eng, lambda e: e.dma_start(out=dst, in_=src), reads=(), writes=[("w", b)])
        return b, dst

    def rmsnorm(self, gain_name, out_bf16=True, out_ap_fn=None, eps=1e-6, reverse=False, top=False):
        S_ = self.S
        if top:
            nsl = 4
            sq = self.carve(17920, 1024, BF16)
            rr = [self.carve(18944, 512, F32), self.carve(19456, 512, F32)]
            tg = "T"
        else:
            nsl = 8
            sq = self.carve(0, 2048, BF16)
            rr = [self.carve(2048, 512, F32), self.carve(2560, 512, F32)]
            tg = "L"
        for tt in range(NT):
            t0 = tt * TT
            pb = self.next_ps()
            ps = self.ps[pb]
            for c in range(KC):
                sl = c % nsl
                o = sq[:, sl * 512:(sl + 1) * 512]
                i = self.hT[:, c, t0:t0 + TT]
                S_.op("act", lambda e, o=o, i=i: e.activation(o, i, AF.Square),
                      reads=[("h", c, tt)], writes=[("sq", tg, sl)])
                S_.op("pe", lambda e, ps=ps, o=o, c=c: e.matmul(ps[:, :], self.ones[:, :], o, start=(c == 0), stop=(c == KC - 1)),
                      reads=[("sq", tg, sl), "ones"], writes=[("ps", pb)])
            r = rr[tt % 2]
            rk = ("rstd", tg, tt % 2)
            S_.op("act", lambda e, r=r, ps=ps: e.activation(r, ps[:, :], AF.Sqrt, bias=eps, scale=1.0 / D),
                  reads=[("ps", pb)], writes=[rk])
            S_.op("dve", lambda e, r=r: e.reciprocal(r, r), reads=[rk], writes=[rk])
            for c in range(KC):
                if reverse:
                    o = self.hn[:, c, (3 - tt) * TT:(4 - tt) * TT][:, ::-1]
                    wk = ("hn", c, 3 - tt)
                elif out_ap_fn is None:
                    o = self.hn[:, c, t0:t0 + TT]
                    wk = ("hn", c, tt)
                else:
                    o, wk = out_ap_fn(c, tt)
                i = self.hT[:, c, t0:t0 + TT]
                g = self.vcol(gain_name, c)
                S_.op("dve", lambda e, o=o, i=i, g=g, r=r: e.scalar_tensor_tensor(o, i, g, r, ALU.mult, ALU.mult),
                      reads=[("h", c, tt), rk, "vecs"], writes=[wk])

    def linear(self, wd, k_rows, col_groups, rhs_fn, epi_fn, weng="pool"):
        S_ = self.S
        r0, nk = k_rows

        def load(gi):
            cols = col_groups[gi]
            b = self.w_rr
            self.w_rr = 1 - b
            n = len(cols)
            dst = self.wbuf[b][:, 0:nk * n * 128].rearrange("p (k m) -> p k m", k=nk)
            contiguous = all(cols[i + 1] == cols[i] + 128 for i in range(n - 1))
            if contiguous:
                src = wd[r0:r0 + nk * 128, cols[0]:cols[0] + n * 128].rearrange("(k p) m -> p k m", p=128)
                S_.dma(weng, lambda e, dst=dst, src=src: e.dma_start(out=dst, in_=src), reads=(), writes=[("w", b)])
            else:
                for i, c0 in enumerate(cols):
                    src = wd[r0:r0 + nk * 128, c0:c0 + 128].rearrange("(k p) m -> p k m", p=128)
                    d2 = dst[:, :, i * 128:(i + 1) * 128]
                    S_.dma(weng, lambda e, d2=d2, src=src: e.dma_start(out=d2, in_=src), reads=(),
                           writes=[("w", b, i)] + ([("w", b)] if i == 0 else []))
            return b, dst, (not contiguous)

        nxt = load(0)
        for gi, cols in enumerate(col_groups):
            b, wt, split = nxt
            if gi + 1 < len(col_groups):
                nxt = load(gi + 1)
            for mi in range(len(cols)):
                wkeys = [("w", b)] + ([("w", b, mi)] if split else [])
                for tt in range(NT):
                    pb = self.next_ps()
                    ps = self.ps[pb]
                    for k in range(nk):
                        rhs, rkey = rhs_fn(k, tt)
                        lhsT = wt[:, k, mi * 128:(mi + 1) * 128]
                        S_.op("pe", lambda e, ps=ps, lhsT=lhsT, rhs=rhs, k=k: e.matmul(ps[:, :], lhsT, rhs, start=(k == 0), stop=(k == nk - 1)),
                              reads=wkeys + [rkey], writes=[("ps", pb)])
                    epi_fn(gi, mi, tt, ps, ("ps", pb))

    def ffn(self, li):
        S_ = self.S
        self.rmsnorm(("ffn_norm", li), top=True)
        g = self.carve(0, 11264, BF16)
        Ug = self.carve(11264, 2052, F32)
        Uv = self.carve(11264 + 2052, 2052, F32)
        acc0 = 11264 + 2 * 2052
        accs = [[self.carve(acc0 + (2 * p + q) * 512, 512, F32) for q in range(2)] for p in range(2)]
        S_.op("dve", lambda e: e.memset(Ug[:, 0:2], 0.0), writes=[("U", 0, -1)])
        S_.op("dve", lambda e: e.memset(Uv[:, 0:2], 0.0), writes=[("U", 1, -1)])
        U = [Ug, Uv]
        wup = self.ffn_up[li]
        wdn = self.ffn_down[li]
        cnt = [0]
        for half in range(2):
            j0 = half * 11
            groups = []
            for jj in range(0, 11, 2):
                js = [j0 + jj] + ([j0 + jj + 1] if jj + 1 < 11 else [])
                cols = []
                for j in js:
                    cols += [j * 128, DFF + j * 128]
                groups.append(cols)

            def rhs_fn(k, tt):
                return self.hn[:, k, tt * TT:(tt + 1) * TT], ("hn", k, tt)

            self._ffn_up_group(wup, groups, rhs_fn, j0, li, U, accs, g)

            def rhs2(k, tt):
                return g[:, k * S + tt * TT: k * S + (tt + 1) * TT], ("g", k, tt)

            def epi2(gi, mi, tt, ps, pkey):
                c = gi * 4 + mi
                hv = self.hT[:, c, tt * TT:(tt + 1) * TT]
                S_.op("dve", lambda e: e.tensor_tensor(hv, ps[:, :], hv, ALU.add),
                      reads=[pkey, ("h", c, tt)], writes=[("h", c, tt)])

            self.linear(wdn, (half * 11 * 128, 11), [[0, 128, 256, 384], [512, 640, 768, 896]], rhs2, epi2)
        S_.fence()

    def _ffn_up_group(self, wup, groups, rhs_fn, j0, li, U, accs, g):
        S_ = self.S
        nk = KC

        def load(gi):
            cols = groups[gi]
            b = self.w_rr
            self.w_rr = 1 - b
            n = len(cols)
            dst = self.wbuf[b][:, 0:nk * n * 128].rearrange("p (k m) -> p k m", k=nk)
            npair = n // 2
            srcg = wup[:, cols[0]:cols[0] + npair * 128].rearrange("(k p) m -> p k m", p=128)
            srcv = wup[:, cols[1]:cols[1] + npair * 128].rearrange("(k p) m -> p k m", p=128)
            dg = dst[:, :, 0:npair * 128]
            dv = dst[:, :, npair * 128:2 * npair * 128]
            S_.dma("pool", lambda e: e.dma_start(out=dg, in_=srcg), reads=(), writes=[("w", b), ("w", b, 0)])
            S_.dma("pool", lambda e: e.dma_start(out=dv, in_=srcv), reads=(), writes=[("w", b, 1)])
            return b, dst, npair

        nxt = load(0)
        it = 0
        for gi in range(len(groups)):
            b, wt, npair = nxt
            if gi + 1 < len(groups):
                nxt = load(gi + 1)
            for pj in range(npair):
                j = j0 + gi * 2 + pj
                jj = j - j0
                for tt in range(NT):
                    t0 = tt * TT
                    par = it % 2
                    it += 1
                    pbs = []
                    for q in range(2):
                        pb = self.next_ps()
                        ps = self.ps[pb]
                        pbs.append(pb)
                        for k in range(nk):
                            rhs, rkey = rhs_fn(k, tt)
                            lhsT = wt[:, k, (q * npair + pj) * 128:(q * npair + pj + 1) * 128]
                            S_.op("pe", lambda e, ps=ps, lhsT=lhsT, rhs=rhs, k=k: e.matmul(ps[:, :], lhsT, rhs, start=(k == 0), stop=(k == nk - 1)),
                                  reads=[("w", b), ("w", b, q), rkey], writes=[("ps", pb)])
                    for q in range(2):
                        pb = pbs[q]
                        ps = self.ps[pb]
                        pkey = ("ps", pb)
                        acc = accs[par][q]
                        ch = q * NFF + j
                        Uq = U[q]
                        w2 = self.vcol(("ffn_cw", li, 2), ch)
                        w1 = self.vcol(("ffn_cw", li, 1), ch)
                        w0 = self.vcol(("ffn_cw", li, 0), ch)
                        bb = self.vcol(("ffn_cb", li), ch)
                        S_.op("act", lambda e, Uq=Uq, ps=ps, t0=t0: e.activation(Uq[:, 2 + t0:2 + t0 + TT], ps[:, :], AF.Identity),
                              reads=[pkey], writes=[("U", q, tt)])
                        S_.op("act", lambda e, acc=acc, ps=ps, bb=bb, w2=w2: e.activation(acc, ps[:, :], AF.Identity, bias=bb, scale=w2),
                              reads=[pkey, "vecs"], writes=[("acc", par, q)])
                        S_.op("dve", lambda e, acc=acc, Uq=Uq, w1=w1, t0=t0: e.scalar_tensor_tensor(acc, Uq[:, 1 + t0:1 + t0 + TT], w1, acc, ALU.mult, ALU.add),
                              reads=[("U", q, tt), ("U", q, tt - 1), "vecs"], writes=[("acc", par, q)])
                        S_.op("dve", lambda e, acc=acc, Uq=Uq, w0=w0, t0=t0: e.scalar_tensor_tensor(acc, Uq[:, t0:t0 + TT], w0, acc, ALU.mult, ALU.add),
                              reads=[("U", q, tt), ("U", q, tt - 1), "vecs"], writes=[("acc", par, q)])
                    ag, av = accs[par]
                    S_.op("act", lambda e, ag=ag: e.activation(ag, ag, AF.Silu), reads=[("acc", par, 0)], writes=[("acc", par, 0)])
                    go = g[:, jj * S + t0: jj * S + t0 + TT]
                    S_.op("pool", lambda e, go=go, ag=ag, av=av: e.tensor_tensor(go, ag, av, ALU.mult),
                          reads=[("acc", par, 0), ("acc", par, 1)], writes=[("g", jj, tt)])

    def mamba(self, li, si):
        S_ = self.S
        A = self.carve
        self.rmsnorm(("attn_norm", li))
        S_.fence()
        win = self.ssm_in[li]
        wout = self.ssm_out[li]
        bv = self.bvecs_d
        v3 = lambda ap, c: ap.rearrange("p (c h) -> p c h", c=c)
        dt = A(0, 512, F32); adt = A(512, 512, F32); acs = A(1024, 512, F32)
        dS = A(1536, 512, F32); Ea = A(2048, 512, F32); Etot = A(2560, 512, F32)
        dtb = A(3072, 512, F32); ea = A(3584, 512, F32)
        normw = A(4096, 512, F32)
        wdt = A(4608, 128, BF16).rearrange("p (k m) -> p k m", k=8)
        diagD = A(4736, 256, BF16)
        xc = A(4992, 4096, BF16)
        BT = A(9088, 1024, BF16)
        CT = A(10112, 1024, BF16)
        T0 = 11136
        U = A(T0, 2052, F32)
        acc = [A(T0 + 2052 + i * 512, 512, F32) for i in range(2)]
        o = T0
        xdt = [A(o + i * 256, 256, BF16) for i in range(2)]; o += 512
        xsc = [A(o + i * 256, 256, BF16) for i in range(2)]; o += 512
        Btok = [A(o + i * 64, 64, BF16) for i in range(2)]; o += 128
        CBm = [A(o + i * 128, 128, F32) for i in range(2)]; o += 256
        rhsD = A(o, 1024, F32); o += 1024
        MT = [A(o + i * 512, 512, BF16) for i in range(2)]; o += 1024
        zs = [A(o + i * 256, 256, BF16) for i in range(2)]; o += 512
        t1 = [A(o + i * 512, 512, F32) for i in range(2)]; o += 1024
        gn = [A(o + i * 256, 256, BF16) for i in range(2)]; o += 512
        gnT = [A(o + i * 1024, 1024, BF16) for i in range(2)]; o += 2048
        state = A(o, 512, F32); o += 512
        state_bf = A(o, 256, BF16); o += 256
        ssb = [A(o + i * 2, 1, F32) for i in range(2)]; o += 4
        rsb = [A(o + i * 2, 1, F32) for i in range(2)]; o += 4
        assert o <= 19968, o

        S_.dma("sp", lambda e: e.dma_start(out=dtb, in_=bv[:, li * 3072:li * 3072 + 512]), writes=["dtb"])
        S_.dma("sp", lambda e: e.dma_start(out=ea, in_=bv[:, li * 3072 + 512:li * 3072 + 1024]), writes=["ea"])
        wsrc = win[:, 5120:5152].rearrange("(k p) m -> p k m", p=128)
        S_.dma("pool", lambda e: e.dma_start(out=wdt, in_=wsrc), writes=["wdt"])
        pb = self.next_ps(); ps = self.ps[pb]
        for c in range(16):
            for k in range(KC):
                S_.op("pe", lambda e, c=c, k=k, ps=ps: e.matmul(ps[:, c * 32:(c + 1) * 32], self.hn[:, k, c * 128:(c + 1) * 128], wdt[:, k, :], start=(k == 0), stop=(k == KC - 1)),
                      reads=["wdt", ("hn", k, c // 4)], writes=[("ps", pb)])
        S_.op("dve", lambda e, ps=ps: e.tensor_tensor(dt, ps[:, :], dtb, ALU.add), reads=[("ps", pb), "dtb"], writes=["dt"])
        S_.op("act", lambda e: e.activation(dt, dt, AF.Exp), reads=["dt"], writes=["dt"])
        S_.op("act", lambda e: e.activation(dt, dt, AF.Ln, bias=1.0), reads=["dt"], writes=["dt"])
        S_.op("act", lambda e: e.activation(ea, ea, AF.Exp), reads=["ea"], writes=["ea"])
        S_.op("dve", lambda e: e.scalar_tensor_tensor(adt, dt, -1.0, ea, ALU.mult, ALU.mult), reads=["dt", "ea"], writes=["adt"])
        pa = self.next_ps(); psA = self.ps[pa]
        pbb = self.next_ps(); psB = self.ps[pbb]
        S_.op("pe", lambda e: e.matmul(psA[:, :], self.tri, adt, start=True, stop=True), reads=["cst", "adt"], writes=[("ps", pa)])
        S_.op("pe", lambda e: e.matmul(psB[:, :], self.onesf, adt, start=True, stop=True), reads=["cst", "adt"], writes=[("ps", pbb)])
        S_.op("act", lambda e: e.activation(acs, psA[:, :], AF.Identity), reads=[("ps", pa)], writes=["acs"])
        S_.op("act", lambda e: e.activation(Ea, psA[:, :], AF.Exp), reads=[("ps", pa)], writes=["Ea"])
        S_.op("act", lambda e: e.activation(Etot, psB[:, :], AF.Exp), reads=[("ps", pbb)], writes=["Etot"])
        S_.op("dve", lambda e: e.tensor_tensor(dS, psB[:, :], acs, ALU.subtract), reads=[("ps", pbb), "acs"], writes=["dS"])
        S_.op("act", lambda e: e.activation(dS, dS, AF.Exp), reads=["dS"], writes=["dS"])

        def do_group(g):
            S_.fence()
            b = self.w_rr
            self.w_rr = 1 - b
            wt = self.wbuf[b][:, 0:6144].rearrange("p (k m) -> p k m", k=8)
            for i, (c0, n, d0) in enumerate([(2048 + 512 * g, 512, 0), (4096 + 128 * g, 128, 512), (4608 + 128 * g, 128, 640)]):
                src = win[:, c0:c0 + n].rearrange("(k p) m -> p k m", p=128)
                dst = wt[:, :, d0:d0 + n]
                S_.dma("pool", lambda e, dst=dst, src=src: e.dma_start(out=dst, in_=src),
                       writes=[("w", b, i)] + ([("w", b)] if i == 0 else []))
            S_.dma("sp", lambda e, g=g: e.dma_start(out=normw, in_=bv[:, li * 3072 + 1024 + 512 * g:li * 3072 + 1536 + 512 * g]), writes=["normw"])
            S_.op("dve", lambda e: e.memset(U[:, 0:3], 0.0), writes=[("U", -1)])
            it = 0
            for j in range(6):
                ch = 4 * g + j if j < 4 else (16 + g if j == 4 else 20 + g)
                wi = 0 if j < 4 else j - 3
                for tt in range(NT):
                    t0 = tt * TT
                    if j < 4:
                        dest = xc[:, j * S + t0:j * S + t0 + TT]; dkey = ("xc", j, tt)
                    elif j == 4:
                        dest = BT[:, t0:t0 + TT]; dkey = ("BT", tt)
                    else:
                        dest = CT[:, t0:t0 + TT]; dkey = ("CT", tt)
                    pb = self.next_ps(); ps = self.ps[pb]
                    for k in range(KC):
                        S_.op("pe", lambda e, ps=ps, k=k, j=j, t0=t0: e.matmul(ps[:, :], wt[:, k, j * 128:(j + 1) * 128], self.hn[:, k, t0:t0 + TT], start=(k == 0), stop=(k == KC - 1)),
                              reads=[("w", b), ("w", b, wi), ("hn", k, tt)], writes=[("ps", pb)])
                    par = it % 2
                    it += 1
                    a_ = acc[par]
                    w3 = self.vcol(("ssm_cw", li, 3), ch); w2 = self.vcol(("ssm_cw", li, 2), ch)
                    w1 = self.vcol(("ssm_cw", li, 1), ch); w0 = self.vcol(("ssm_cw", li, 0), ch)
                    bb = self.vcol(("ssm_cb", li), ch)
                    S_.op("act", lambda e, ps=ps, t0=t0: e.activation(U[:, 3 + t0:3 + t0 + TT], ps[:, :], AF.Identity), reads=[("ps", pb)], writes=[("U", tt)])
                    S_.op("act", lambda e, ps=ps, a_=a_, bb=bb, w3=w3: e.activation(a_, ps[:, :], AF.Identity, bias=bb, scale=w3),
                          reads=[("ps", pb), "vecs"], writes=[("macc", par)])
                    for sh, w in ((2, w2), (1, w1), (0, w0)):
                        S_.op("dve", lambda e, a_=a_, w=w, sh=sh, t0=t0: e.scalar_tensor_tensor(a_, U[:, sh + t0:sh + t0 + TT], w, a_, ALU.mult, ALU.add),
                              reads=[("U", tt), ("U", tt - 1), "vecs"], writes=[("macc", par)])
                    S_.op("act", lambda e, dest=dest, a_=a_: e.activation(dest, a_, AF.Silu), reads=[("macc", par)], writes=[dkey])

            S_.fence()
            bz = self.w_rr
            bo = 1 - bz
            Wz = self.wbuf[bz][:, 0:4096].rearrange("p (k m) -> p k m", k=8)
            Wo = self.wbuf[bo][:, 0:4096].rearrange("p (k m) -> p k m", k=4)
            srcz = win[:, 512 * g:512 * g + 512].rearrange("(k p) m -> p k m", p=128)
            srco = wout[512 * g:512 * g + 512, :].rearrange("(k p) m -> p k m", p=128)
            S_.dma("pool", lambda e: e.dma_start(out=Wz, in_=srcz), writes=[("w", bz)])
            S_.dma("pool", lambda e: e.dma_start(out=Wo, in_=srco), writes=[("w", bo)])
            for j in range(4):
                dj = diagD[:, j * 128:(j + 1) * 128]
                sc = self.vcol(("ssm_dfm", li), 4 * g + j)
                S_.op("dve", lambda e, dj=dj, sc=sc: e.tensor_scalar(dj, self.identb[:, :], sc, None, ALU.mult), reads=["identb", "vecs"], writes=[("diagD", j)])
            S_.op("dve", lambda e: e.memset(state, 0.0), writes=["state"])
            S_.op("pool", lambda e: e.memset(state_bf, 0.0), writes=["state_bf"])
            h8 = slice(8 * g, 8 * g + 8)
            bc64 = lambda ap, c: v3(ap, 16)[:, c, h8].unsqueeze(2).to_broadcast([128, 8, 64])
            r64 = lambda ap: ap.rearrange("p (r d) -> p r d", r=8)

            def stageA1(c):
                par = c % 2
                tt = c // 4
                l0 = c * 128
                pbx = self.next_ps(); psx = self.ps[pbx][:, :].bitcast(BF16)
                for j in range(4):
                    S_.op("pe", lambda e, j=j: e.transpose(psx[:, j * 128:(j + 1) * 128], xc[:, j * S + l0:j * S + l0 + 128], self.identb[:, :]),
                          reads=[("xc", j, tt), "identb"], writes=[("ps", pbx)])
                S_.op("pe", lambda e: e.transpose(psx[:, 512:640], BT[:, l0:l0 + 128], self.identb[:, :]), reads=[("BT", tt), "identb"], writes=[("ps", pbx)])
                S_.op("dve", lambda e: e.tensor_tensor(r64(xdt[par]), r64(psx[:, 0:512]), bc64(dt, c), ALU.mult), reads=[("ps", pbx), "dt"], writes=[("xdt", par)])
                S_.op("pool", lambda e: e.tensor_tensor(r64(xsc[par]), r64(xdt[par]), bc64(dS, c), ALU.mult), reads=[("xdt", par), "dS"], writes=[("xsc", par)])
                S_.op("act", lambda e: e.activation(Btok[par], psx[:, 512:640], AF.Identity), reads=[("ps", pbx)], writes=[("Btok", par)])
                pbc = self.next_ps(); psc = self.ps[pbc]
                S_.op("pe", lambda e: e.matmul(psc[:, 0:128], BT[:, l0:l0 + 128], CT[:, l0:l0 + 128], start=True, stop=True),
                      reads=[("BT", tt), ("CT", tt)], writes=[("ps", pbc)])
                S_.op("dve", lambda e: e.tensor_tensor(CBm[par], psc[:, 0:128], self.tri, ALU.mult), reads=[("ps", pbc), "cst"], writes=[("CBm", par)])
                r128 = lambda ap: ap.rearrange("p (r d) -> p r d", r=8)
                S_.op("dve", lambda e: e.tensor_tensor(r128(rhsD), self.tri.unsqueeze(1).to_broadcast([128, 8, 128]),
                                                       v3(adt, 16)[:, c, h8].unsqueeze(2).to_broadcast([128, 8, 128]), ALU.mult),
                      reads=["cst", "adt"], writes=["rhsD"])
                for i in range(2):
                    pbd = self.next_ps(); psd = self.ps[pbd]
                    S_.op("pe", lambda e, psd=psd, i=i: e.matmul(psd[:, :], self.strictT, rhsD[:, i * 512:(i + 1) * 512], start=True, stop=True),
                          reads=["cst", "rhsD"], writes=[("ps", pbd)])
                    S_.op("act", lambda e, psd=psd, i=i: e.activation(MT[par][:, i * 512:(i + 1) * 512], psd[:, :], AF.Exp), reads=[("ps", pbd)], writes=[("MT", par, i)])
                pbz = self.next_ps(); psz = self.ps[pbz]
                for k in range(KC):
                    S_.op("pe", lambda e, k=k: e.matmul(psz[:, :], self.hn[:, k, l0:l0 + 128], Wz[:, k, :], start=(k == 0), stop=(k == KC - 1)),
                          reads=[("w", bz), ("hn", k, tt)], writes=[("ps", pbz)])
                S_.op("act", lambda e: e.activation(zs[par], psz[:, :], AF.Silu), reads=[("ps", pbz)], writes=[("zs", par)])

            def stageA2(c):
                par = c % 2
                tt = c // 4
                l0 = c * 128
                r128 = lambda ap: ap.rearrange("p (r d) -> p r d", r=8)
                S_.op("dve", lambda e: e.tensor_tensor(r128(MT[par]), r128(MT[par]), CBm[par].unsqueeze(1).to_broadcast([128, 8, 128]), ALU.mult),
                      reads=[("MT", par, 0), ("MT", par, 1), ("CBm", par)], writes=[("MT", par, 0), ("MT", par, 1)])
                pby = self.next_ps(); psy = self.ps[pby]
                for j in range(4):
                    S_.op("pe", lambda e, j=j: e.matmul(psy[:, j * 128:(j + 1) * 128], xc[:, j * S + l0:j * S + l0 + 128], diagD[:, j * 128:(j + 1) * 128], start=True, stop=False),
                          reads=[("xc", j, tt), ("diagD", j)], writes=[("ps", pby)])
                    for r in (2 * j, 2 * j + 1):
                        S_.op("pe", lambda e, r=r, j=j: e.matmul(psy[:, r * 64:(r + 1) * 64], MT[par][:, r * 128:(r + 1) * 128], xdt[par][:, r * 64:(r + 1) * 64], start=False, stop=(r == 2 * j + 1)),
                              reads=[("MT", par, 0), ("MT", par, 1), ("xdt", par)], writes=[("ps", pby)])
                pbo = self.next_ps(); pso = self.ps[pbo]
                S_.op("pe", lambda e: e.matmul(pso[:, :], CT[:, l0:l0 + 128], state_bf, start=True, stop=True), reads=[("CT", tt), "state_bf"], writes=[("ps", pbo)])
                pbs = self.next_ps(); pss = self.ps[pbs]
                S_.op("pe", lambda e: e.matmul(pss[:, :], Btok[par], xsc[par], start=True, stop=True), reads=[("Btok", par), ("xsc", par)], writes=[("ps", pbs)])
                T = t1[par]
                S_.op("dve", lambda e: e.tensor_tensor(r64(T), r64(pso[:, :]), bc64(Ea, c), ALU.mult), reads=[("ps", pbo), "Ea"], writes=[("t1", par)])
                S_.op("dve", lambda e: e.tensor_tensor(T, T, psy[:, :], ALU.add), reads=[("ps", pby), ("t1", par)], writes=[("t1", par)])
                S_.op("dve", lambda e: e.tensor_tensor(T, T, zs[par], ALU.mult), reads=[("zs", par), ("t1", par)], writes=[("t1", par)])
                S_.op("act", lambda e: e.activation(gn[par], T, AF.Square, accum_out=ssb[par]), reads=[("t1", par)], writes=[("gn", par), ("ss", par)])
                S_.op("act", lambda e: e.activation(rsb[par], ssb[par], AF.Sqrt, bias=1e-5, scale=1.0 / 512), reads=[("ss", par)], writes=[("rs", par)])
                S_.op("dve", lambda e: e.reciprocal(rsb[par], rsb[par]), reads=[("rs", par)], writes=[("rs", par)])
                S_.op("dve", lambda e: e.scalar_tensor_tensor(gn[par], T, rsb[par], normw, ALU.mult, ALU.mult), reads=[("t1", par), ("rs", par), "normw"], writes=[("gn", par)])
                S_.op("dve", lambda e: e.tensor_tensor(r64(state), r64(state), bc64(Etot, c), ALU.mult), reads=["state", "Etot"], writes=["state"])
                S_.op("dve", lambda e: e.tensor_tensor(state, state, pss[:, :], ALU.add), reads=["state", ("ps", pbs)], writes=["state"])
                S_.op("pool", lambda e: e.tensor_copy(state_bf, state), reads=["state"], writes=["state_bf"])

            def stageB(c):
                par = c % 2
                tt = c // 4
                q = c % 4
                G = gnT[tt % 2].rearrange("p (j t) -> p j t", j=4)
                pbt = self.next_ps(); pst = self.ps[pbt][:, :].bitcast(BF16)
                for j in range(4):
                    S_.op("pe", lambda e, j=j: e.transpose(pst[:, j * 128:(j + 1) * 128], gn[par][:, j * 128:(j + 1) * 128], self.identb[:, :]),
                          reads=[("gn", par), "identb"], writes=[("ps", pbt)])
                S_.op("act", lambda e: e.activation(G[:, :, q * 128:(q + 1) * 128], pst[:, 0:512].rearrange("p (j t) -> p j t", j=4), AF.Identity),
                      reads=[("ps", pbt)], writes=[("gnT", tt % 2, q)])
                if q == 3:
                    for m in range(KC):
                        pbm = self.next_ps(); psm = self.ps[pbm]
                        for k in range(4):
                            S_.op("pe", lambda e, m=m, k=k, psm=psm: e.matmul(psm[:, :], Wo[:, k, m * 128:(m + 1) * 128], G[:, k, :], start=(k == 0), stop=(k == 3)),
                                  reads=[("w", bo)] + [("gnT", tt % 2, qq) for qq in range(4)], writes=[("ps", pbm)])
                        hv = self.hT[:, m, tt * TT:(tt + 1) * TT]
                        S_.op("dve", lambda e, hv=hv, psm=psm: e.tensor_tensor(hv, psm[:, :], hv, ALU.add), reads=[("ps", pbm), ("h", m, tt)], writes=[("h", m, tt)])

            for c in range(18):
                if c < 16:
                    stageA1(c)
                if 0 <= c - 1 < 16:
                    stageA2(c - 1)
                if 0 <= c - 2 < 16:
                    stageB(c - 2)

        for g in range(4):
            do_group(g)
        S_.fence()

    def kv_stage(self, si):
        S_ = self.S
        A = self.carve
        self.rmsnorm("kv_norm", reverse=True)
        S_.fence()
        stg = [A(i * 256, 256, BF16) for i in range(4)]
        it = [0]

        def rhs_fn(k, tt):
            return self.hn[:, k, tt * TT:(tt + 1) * TT], ("hn", k, tt)

        def epi(gi, mi, tt, ps, pkey):
            hp = gi * 4 + mi
            par = it[0] % 4
            it[0] += 1
            sb = stg[par]
            S_.op("act", lambda e: e.activation(sb, ps[:, :], AF.Identity), reads=[pkey], writes=[("stg", par)])
            dst = self.kTd[hp, :, tt * TT:(tt + 1) * TT]
            S_.dma("sp", lambda e: e.dma_start(out=dst, in_=sb), reads=[("stg", par)], writes=[("kTd", hp)])

        self.linear(self.w_kv, (0, 8), [[0, 128, 256, 384], [512, 640, 768, 896]], rhs_fn, epi)
        Wv = []
        for cg in range(2):
            b = cg
            wt = self.wbuf[b][:, 0:4096].rearrange("p (k m) -> p k m", k=8)
            src = self.w_kv[:, 1024 + cg * 512:1024 + (cg + 1) * 512].rearrange("(k p) m -> p k m", p=128)
            S_.dma("pool", lambda e, wt=wt, src=src: e.dma_start(out=wt, in_=src), writes=[("w", b)])
            Wv.append(wt)
        for jb in range(16):
            for cg in range(2):
                pb = self.next_ps(); ps = self.ps[pb]
                for k in range(KC):
                    S_.op("pe", lambda e, ps=ps, k=k, jb=jb, cg=cg: e.matmul(ps[:, :], self.hn[:, k, jb * 128:(jb + 1) * 128], Wv[cg][:, k, :], start=(k == 0), stop=(k == KC - 1)),
                          reads=[("w", cg), ("hn", k, jb // 4)], writes=[("ps", pb)])
                par = it[0] % 4
                it[0] += 1
                sb = stg[par]
                S_.op("act", lambda e, sb=sb, ps=ps: e.activation(sb, ps[:, :], AF.Identity), reads=[("ps", pb)], writes=[("stg", par)])
                dst = self.Vd[cg * 4:(cg + 1) * 4, :, :, jb * 64:(jb + 1) * 64].rearrange("h f j d -> j h f d")
                srcv = sb.rearrange("p (h f d) -> p h f d", h=4, f=2)
                S_.dma("sp", lambda e, dst=dst, srcv=srcv: e.dma_start(out=dst, in_=srcv), reads=[("stg", par)],
                       writes=[("Vd", cg * 4 + hh) for hh in range(4)])
        S_.fence()

    def attention(self, li, si):
        S_ = self.S
        A = self.carve
        j_ = li - NA
        scale = 0.125
        self.rmsnorm(("attn_norm", li))
        S_.fence()
        oT = A(0, 8192, BF16).rearrange("p (c t) -> p c t", c=8)
        o = 8192
        kT = [A(o + i * 1024, 1024, BF16) for i in range(2)]; o += 2048
        Vb = [A(o + i * 2048, 2048, BF16) for i in range(2)]; o += 4096
        qT = [A(o + i * 1024, 1024, BF16) for i in range(2)]; o += 2048
        eb = [A(o + i * 512, 512, F32) for i in range(3)]; o += 1536
        cb = [A(o + i * 512, 512, F32) for i in range(2)]; o += 1024
        wb = [A(o + i * 256, 256, BF16) for i in range(2)]; o += 512
        wTb = [A(o + i * 256, 256, BF16) for i in range(2)]; o += 512
        assert o <= 19968, o
        Wq = []
        for cg in range(2):
            wt = self.wbuf[cg][:, 0:4096].rearrange("p (k m) -> p k m", k=8)
            src = self.w_q[j_][:, cg * 512:(cg + 1) * 512].rearrange("(k p) m -> p k m", p=128)
            S_.dma("pool", lambda e, wt=wt, src=src: e.dma_start(out=wt, in_=src), writes=[("w", cg)])
            Wq.append(wt)
        for i in range(2):
            vz = Vb[i].rearrange("p (f b c d) -> p f b c d", f=2, b=16, c=2)
            S_.op("dve", lambda e, vz=vz: e.memset(vz[:, 0, :, 1, :], 0.0), writes=[("Vz", i)])
            S_.op("dve", lambda e, vz=vz: e.memset(vz[:, 1, :, 0, :], 0.0), writes=[("Vz", i)])
        self.ps_lo = 2
        self.ps_rr = 0
        seg_it = [0]
        po_it = [0]

        def load_hp(hp):
            bpar = hp % 2
            S_.dma("sp", lambda e: e.dma_start(out=kT[bpar], in_=self.kTd[hp]), reads=[("kTd", hp)], writes=[("kT", bpar)])
            vz = Vb[bpar].rearrange("p (f b c d) -> p f b c d", f=2, b=16, c=2)
            for f in range(2):
                src = self.Vd[hp, f].rearrange("j (b d) -> j b d", b=16)
                dst = vz[:, f, :, f, :]
                S_.dma("sp", lambda e, dst=dst, src=src: e.dma_start(out=dst, in_=src), reads=[("Vd", hp)], writes=[(("VA", "VB")[f], bpar)])

        def q_proj(hp):
            cg, hl = hp // 4, hp % 4
            for tt in range(NT):
                pb = self.next_ps(); ps = self.ps[pb]
                for k in range(KC):
                    S_.op("pe", lambda e, ps=ps, k=k, tt=tt: e.matmul(ps[:, :], Wq[cg][:, k, hl * 128:(hl + 1) * 128], self.hn[:, k, tt * TT:(tt + 1) * TT], start=(k == 0), stop=(k == KC - 1)),
                          reads=[("w", cg), ("hn", k, tt)], writes=[("ps", pb)])
                S_.op("act", lambda e, ps=ps, tt=tt: e.activation(qT[hp % 2][:, tt * TT:(tt + 1) * TT], ps[:, :], AF.Identity), reads=[("ps", pb)], writes=[("qT", hp % 2, tt)])

        class Seg:
            pass

        segs = []
        for hp in range(8):
            for i in range(16):
                nseg = (i + 1 + 3) // 4
                lst = [(half, sg) for half in range(2) for sg in range(nseg)]
                for n_, (half, sg) in enumerate(lst):
                    g = Seg()
                    g.hp, g.i, g.half, g.sg = hp, i, half, sg
                    g.first = (n_ == 0)
                    g.last = (n_ == len(lst) - 1)
                    g.bpar = hp % 2
                    g.rows = slice(half * 64, half * 64 + 64)
                    j0 = S - (i + 1) * 128
                    g.js = j0 + sg * 512
                    g.n = min(512, S - g.js)
                    segs.append(g)
        for idx, g in enumerate(segs):
            g.par = idx % 2
            g.pe3 = idx % 3
        qb = {}
        for g in segs:
            key = (g.hp, g.i)
            if key not in qb:
                qb[key] = len(qb) % 2
            g.pbo = qb[key]

        def stA(g):
            n = g.n
            g.pb = self.next_ps()
            ps = self.ps[g.pb]
            E = eb[g.pe3]
            S_.op("pe", lambda e: e.matmul(ps[:, 0:n], qT[g.hp % 2][g.rows, g.i * 128:(g.i + 1) * 128], kT[g.bpar][g.rows, g.js:g.js + n], start=True, stop=True),
                  reads=[("qT", g.hp % 2, g.i // 4), ("kT", g.bpar)], writes=[("ps", g.pb)])
            S_.op("act", lambda e: e.activation(E[:, 0:n], ps[:, 0:n], AF.Exp, scale=scale), reads=[("ps", g.pb)], writes=[("e", g.pe3)])
            S_.op("act", lambda e: e.activation(E[:, 0:n], E[:, 0:n], AF.Ln, bias=1.0), reads=[("e", g.pe3)], writes=[("e", g.pe3)])

        def stB(g):
            n = g.n
            ps = self.ps[g.pb]
            E = eb[g.pe3]; C = cb[g.par]
            if g.sg == 0:
                S_.op("dve", lambda e: e.tensor_tensor(E[:, 0:128], E[:, 0:128], self.amask, ALU.mult), reads=[("e", g.pe3), "cst"], writes=[("e", g.pe3)])
                init = 0.0
                rd = []
            else:
                init = cb[1 - g.par][:, 511:512]
                rd = [("c", 1 - g.par)]
            S_.op("dve", lambda e: e.tensor_tensor_scan(C[:, 0:n], self.onesb[:, 0:n], E[:, 0:n], init, ALU.mult, ALU.add),
                  reads=[("e", g.pe3), "onesb"] + rd, writes=[("c", g.par)])
            S_.op("dve", lambda e: e.scalar_tensor_tensor(E[:, 0:n], ps[:, 0:n], scale, C[:, 0:n], ALU.mult, ALU.subtract),
                  reads=[("ps", g.pb), ("c", g.par)], writes=[("e", g.pe3)])
            if g.sg == 0:
                S_.op("dve", lambda e: e.tensor_tensor(E[:, 0:128], E[:, 0:128], self.negmask, ALU.add), reads=[("e", g.pe3), "cst"], writes=[("e", g.pe3)])

        def stC(g):
            n = g.n
            nb = n // 128
            E = eb[g.pe3]; W = wb[g.par]
            S_.op("act", lambda e: e.activation(W[:, 0:n], E[:, 0:n], AF.Exp), reads=[("e", g.pe3)], writes=[("wsb", g.par)])
            g.pbt = self.next_ps()
            pst = self.ps[g.pbt][:, :].bitcast(BF16)
            for b_ in range(nb):
                S_.op("pe", lambda e, b_=b_: e.transpose(pst[:, b_ * 128:(b_ + 1) * 128], W[:, b_ * 128:(b_ + 1) * 128], self.identb[:, :]),
                      reads=[("wsb", g.par), "identb"], writes=[("ps", g.pbt)])

        def stD(g):
            n = g.n
            nb = n // 128
            pst = self.ps[g.pbt][:, :].bitcast(BF16)
            WT = wTb[g.par]
            pso = self.ps[g.pbo]; pok = ("ps", g.pbo)
            S_.op("act", lambda e: e.activation(WT[:, 0:n], pst[:, 0:n], AF.Identity), reads=[("ps", g.pbt)], writes=[("wT", g.par)])
            base = 0 if g.half == 0 else 2048
            vkey = ("VA", g.bpar) if g.half == 0 else ("VB", g.bpar)
            for b_ in range(nb):
                jb = (g.js + b_ * 128) // 128
                off = base + jb * 128
                lhsT = Vb[g.bpar][:, off:off + 128]
                S_.op("pe", lambda e, b_=b_, lhsT=lhsT: e.matmul(pso[:, 0:128], lhsT, WT[:, b_ * 128:(b_ + 1) * 128],
                                                               start=(g.first and b_ == 0), stop=(g.last and b_ == nb - 1)),
                      reads=[("wT", g.par), vkey, ("Vz", g.bpar)], writes=[pok])
            if g.last:
                S_.op("act", lambda e: e.activation(oT[:, g.hp, g.i * 128:(g.i + 1) * 128], pso[:, 0:128], AF.Identity), reads=[pok], writes=[("oT", g.hp, g.i // 4)])

        load_hp(0)
        hp_start = 0
        NS = len(segs)
        for k in range(NS + 3):
            if k < NS:
                g = segs[k]
                if g.i == 0 and g.first:
                    q_proj(g.hp)
                    hp_start = k
                if k == hp_start + 5 and g.hp + 1 < 8:
                    load_hp(g.hp + 1)
                stA(g)
            if 0 <= k - 1 < NS:
                stB(segs[k - 1])
            if 0 <= k - 2 < NS:
                stC(segs[k - 2])
            if 0 <= k - 3 < NS:
                stD(segs[k - 3])
        self.ps_lo = 0
        self.ps_rr = 0

        def rhs_o(k, tt):
            return oT[:, k, tt * TT:(tt + 1) * TT], ("oT", k, tt)

        def epi_o(gi, mi, tt, ps, pkey):
            c = gi * 4 + mi
            hv = self.hT[:, c, tt * TT:(tt + 1) * TT]
            S_.op("dve", lambda e: e.tensor_tensor(hv, ps[:, :], hv, ALU.add), reads=[pkey, ("h", c, tt)], writes=[("h", c, tt)])

        self.linear(self.w_o[j_], (0, 8), [[0, 128, 256, 384], [512, 640, 768, 896]], rhs_o, epi_o)
        S_.fence()

    def ple(self, li, si):
        S_ = self.S
        self.rmsnorm(("ple_norm", li), top=True)
        pb16 = self.carve(0, 2048, BF16)
        sg = [self.carve(2048 + i * 512, 512, F32) for i in range(2)]
        src = self.pT[li, si].rearrange("(k p) t -> p k t", p=128)
        dst = pb16.rearrange("p (k t) -> p k t", k=2)
        S_.dma("pool", lambda e: e.dma_start(out=dst, in_=src), reads=(), writes=["pT"])
        wp = self.carve(3072, 1024, BF16).rearrange("p (k m) -> p k m", k=2)
        srcw = self.ple_proj[li].rearrange("(k p) m -> p k m", p=128)
        S_.dma("pool", lambda e: e.dma_start(out=wp, in_=srcw), reads=(), writes=["wp"])
        it = [0]

        def rhs_fn(k, tt):
            return self.hn[:, k, tt * TT:(tt + 1) * TT], ("hn", k, tt)

        def epi(gi, mi, tt, ps, pkey):
            c = gi * 4 + mi
            t0 = tt * TT
            par = it[0] % 2
            it[0] += 1
            s = sg[par]
            S_.op("act", lambda e: e.activation(s, ps[:, :], AF.Sigmoid), reads=[pkey], writes=[("sg", par)])
            pb2 = self.next_ps()
            ps2 = self.ps[pb2]
            for k in range(2):
                S_.op("pe", lambda e, k=k: e.matmul(ps2[:, :], wp[:, k, c * 128:(c + 1) * 128], pb16[:, k * S + t0:k * S + t0 + TT], start=(k == 0), stop=(k == 1)),
                      reads=["wp", "pT"], writes=[("ps", pb2)])
            S_.op("dve", lambda e: e.tensor_tensor(s, s, ps2[:, :], ALU.mult), reads=[("sg", par), ("ps", pb2)], writes=[("sg", par)])
            hv = self.hT[:, c, t0:t0 + TT]
            S_.op("dve", lambda e: e.tensor_tensor(hv, hv, s, ALU.add), reads=[("sg", par), ("h", c, tt)], writes=[("h", c, tt)])

        self.linear(self.ple_gate[li], (0, 8), [[0, 128, 256, 384], [512, 640, 768, 896]], rhs_fn, epi)
        S_.fence()

    def build(self):
        S_ = self.S
        nc = self.nc
        cfg = self.cfg
        S_.op("dve", lambda e: e.memset(self.ones[:, :], 1.0), writes=["ones"])
        S_.op("dve", lambda e: e.memset(self.onesb[:, :], 1.0), writes=["onesb"])
        S_.dma("sp", lambda e: e.dma_start(out=self.vecs[:, :], in_=self.vecs_d[:, :]), writes=["vecs"])
        S_.dma("sp", lambda e: e.dma_start(out=self.cst[:, :], in_=self.consts_d[:, :]), writes=["cst"])
        S_.op("dve", lambda e: e.tensor_copy(self.identb[:, :], self.cst[:, 384:512]), reads=["cst"], writes=["identb"])
        for si in range(self.nseq):
            for c in range(KC):
                src = self.xT[si, c * 128:(c + 1) * 128, :]
                dst = self.hT[:, c, :]
                S_.dma("sp", lambda e, dst=dst, src=src: e.dma_start(out=dst, in_=src),
                       writes=[("h", c, tt) for tt in range(NT)])
            for li in cfg["layers"]:
                if cfg.get("mixer", True):
                    if li < NA:
                        self.mamba(li, si)
                    else:
                        self.attention(li, si)
                if cfg.get("ffn", True):
                    self.ffn(li)
                if cfg.get("ple", True):
                    self.ple(li, si)
                if li == NA - 1 and cfg.get("mixer", True) and any(l >= NA for l in cfg["layers"]):
                    self.kv_stage(si)
            S_.fence()
            ob2 = self.carve(4096, 2 * 4096, F32)

            self._final_norm_store(si, ob2)
            S_.fence()
        S_.emit(nc)
        return nc

    def _final_norm_store(self, si, ob2):
        S_ = self.S
        eps = 1e-6
        sq = self.carve(0, 2048, BF16)
        rr = [self.carve(2048, 512, F32), self.carve(2560, 512, F32)]
        for tt in range(NT):
            t0 = tt * TT
            pb = self.next_ps()
            ps = self.ps[pb]
            for c in range(KC):
                o = sq[:, c * 512:(c + 1) * 512]
                i = self.hT[:, c, t0:t0 + TT]
                S_.op("act", lambda e, o=o, i=i: e.activation(o, i, AF.Square), reads=[("h", c, tt)], writes=[("sq", c)])
            for c in range(KC):
                r_ = sq[:, c * 512:(c + 1) * 512]
                S_.op("pe", lambda e, ps=ps, r_=r_, c=c: e.matmul(ps[:, :], self.ones[:, :], r_, start=(c == 0), stop=(c == KC - 1)),
                      reads=[("sq", c), "ones"], writes=[("ps", pb)])
            r = rr[tt % 2]
            rk = ("rstd", tt % 2)
            S_.op("act", lambda e, r=r, ps=ps: e.activation(r, ps[:, :], AF.Sqrt, bias=eps, scale=1.0 / D), reads=[("ps", pb)], writes=[rk])
            S_.op("dve", lambda e, r=r: e.reciprocal(r, r), reads=[rk], writes=[rk])
            for c in range(KC):
                o = ob2[:, (tt % 2) * 4096 + c * 512:(tt % 2) * 4096 + (c + 1) * 512]
                i = self.hT[:, c, t0:t0 + TT]
                g = self.vcol("final_norm", c)
                S_.op("dve", lambda e, o=o, i=i, g=g, r=r: e.scalar_tensor_tensor(o, i, g, r, ALU.mult, ALU.mult),
                      reads=[("h", c, tt), rk, "vecs"], writes=[("ob", tt % 2, c)])
                dst = self.outT[si, c * 128:(c + 1) * 128, t0:t0 + TT]
                S_.dma("sp", lambda e, dst=dst, o=o: e.dma_start(out=dst, in_=o), reads=[("ob", tt % 2, c)], writes=[])


NCORES = 8
_prog_cache = {}


def get_prog(nseq, cfg):
    key = (nseq, repr(sorted(cfg.items())))
    if key not in _prog_cache:
        p = Prog(nseq, cfg)
        p.build()
        _prog_cache[key] = p
    return _prog_cache[key]


def make_in_maps(inp, batch_ids_per_core):
    vecs = pack_vecs(inp)
    shared = dict(vecs=vecs, consts=make_consts(), bvecs=pack_bvecs(inp),
                  ssm_in_proj=np.ascontiguousarray(inp["ssm_in_proj"], np.float32),
                  ssm_out_proj=np.ascontiguousarray(inp["ssm_out_proj"], np.float32),
                  w_kv=np.ascontiguousarray(inp["w_kv"], np.float32),
                  w_q=np.ascontiguousarray(inp["w_q"], np.float32),
                  w_o=np.ascontiguousarray(inp["w_o"], np.float32),
                  ffn_up=np.ascontiguousarray(inp["ffn_up"], np.float32),
                  ffn_down=np.ascontiguousarray(inp["ffn_down"], np.float32),
                  ple_gate=np.ascontiguousarray(inp["ple_gate"], np.float32),
                  ple_proj=np.ascontiguousarray(inp["ple_proj"], np.float32))
    in_maps = []
    for ids in batch_ids_per_core:
        xT = np.ascontiguousarray(np.transpose(inp["x"][ids], (0, 2, 1)))
        pT = np.ascontiguousarray(np.transpose(inp["p"][:, ids], (0, 1, 3, 2)))
        m = dict(shared)
        m["xT"] = xT
        m["pT"] = pT
        in_maps.append(m)
    return in_maps


def run_cores(inp, batch_ids_per_core, cfg):
    nseq = len(batch_ids_per_core[0])
    prog = get_prog(nseq, cfg)
    in_maps = make_in_maps(inp, batch_ids_per_core)
    res = run_bass_kernel_spmd(prog.nc, in_maps, core_ids=list(range(len(in_maps))))
    outs = []
    for r in res.results:
        outs.append(np.transpose(r["outT"], (0, 2, 1)))
    return outs


FULL_CFG = dict(layers=(0, 1, 2, 3), mixer=True, ffn=True, ple=True)


def kernel(**inp):
    inp = {k: np.asarray(v) for k, v in inp.items()}
    B = inp["x"].shape[0]
    per = B // NCORES
    ids = [[c * per + g for g in range(per)] for c in range(NCORES)]
    outs = run_cores(inp, ids, FULL_CFG)
    out = np.empty((B, S, D), np.float32)
    for c in range(NCORES):
        for g in range(per):
            out[c * per + g] = outs[c][g]
    return out
```

```python
import numpy as np
import concourse.bass as bass
import concourse.mybir as mybir
from concourse.bass_utils import run_bass_kernel_spmd

F32 = mybir.dt.float32
BF16 = mybir.dt.bfloat16
AF = mybir.ActivationFunctionType
ALU = mybir.AluOpType

ENG = ["pe", "act", "dve", "pool", "sp"]
N_DMA_SEMS = 8

D = 1024
S = 2048
NT = 4
TT = 512
KC = 8
DFF = 2816
NFF = 22
PLE = 256
DEPTH = 4
NA = 2


class Sched:
    def __init__(self):
        self.ops = {e: [] for e in ENG}
        self.lastw = {}
        self.readers = {}
        self.dma_names = ["dma_%s%d" % (q, j) for q in ("sp", "pool") for j in range(N_DMA_SEMS)]
        self.dma_cnt = {n: 0 for n in self.dma_names}
        self.dma_rr = {"sp": 0, "pool": 0}

    def _need(self, deps, me, res, pos, allow_same=False):
        if res == me and not allow_same:
            return
        if deps.get(res, 0) < pos:
            deps[res] = pos

    def _deps(self, me, reads, writes, allow_same=False):
        deps = {}
        for k in reads:
            w = self.lastw.get(k)
            if w:
                self._need(deps, me, w[0], w[1], allow_same)
            if isinstance(k, tuple) and k[0] == "ps":
                for r, p in self.readers.get(k, {}).items():
                    self._need(deps, me, r, p, False)
        for k in writes:
            w = self.lastw.get(k)
            if w:
                self._need(deps, me, w[0], w[1], allow_same)
            for r, p in self.readers.get(k, {}).items():
                self._need(deps, me, r, p, allow_same)
        return deps

    def _mark(self, res, pos, reads, writes):
        for k in reads:
            self.readers.setdefault(k, {})[res] = pos
        for k in writes:
            self.lastw[k] = (res, pos)
            self.readers[k] = {}

    def op(self, eng, fn, reads=(), writes=()):
        deps = self._deps(eng, reads, writes, allow_same=(eng != "pe"))
        self.ops[eng].append(dict(fn=fn, deps=deps, dma=None))
        self._mark(eng, len(self.ops[eng]), reads, writes)

    def dma(self, eng, fn, reads=(), writes=()):
        j = self.dma_rr[eng]
        self.dma_rr[eng] = (j + 1) % N_DMA_SEMS
        res = "dma_%s%d" % (eng, j)
        k = self.dma_cnt[res] + 1
        self.dma_cnt[res] = k
        deps = self._deps(eng, reads, writes, allow_same=True)
        if k > 1:
            self._need(deps, eng, res, k - 1)
        self.ops[eng].append(dict(fn=fn, deps=deps, dma=res))
        self._mark(res, k, reads, writes)

    def fence(self):
        tail = {}
        for e in ENG:
            for pos in range(len(self.ops[e]), 0, -1):
                o = self.ops[e][pos - 1]
                if o["fn"] is not None and o["dma"] is None:
                    tail[e] = pos
                    break
        dtail = {n: c for n, c in self.dma_cnt.items() if c}
        for e in ENG:
            deps = {r: p for r, p in tail.items() if r != e}
            deps.update(dtail)
            self.ops[e].append(dict(fn=None, deps=deps, dma=None))

    def emit(self, nc):
        sig = {e: set() for e in ENG}
        for e in ENG:
            for o in self.ops[e]:
                for r, p in o["deps"].items():
                    if r in sig:
                        sig[r].add(p)
        rank = {}
        for e in ENG:
            for i, p in enumerate(sorted(sig[e])):
                rank[(e, p)] = i + 1
        from contextlib import ExitStack
        with ExitStack() as st:
            sems = {e: st.enter_context(nc.semaphore("s_" + e)) for e in ENG}
            for n in self.dma_names:
                sems[n] = st.enter_context(nc.semaphore("s_" + n))
            block = st.enter_context(nc.Block())
            engobj = {}

            def run(ename, eng):
                seen = {}
                for pos, o in enumerate(self.ops[ename], start=1):
                    for r, p in o["deps"].items():
                        val = 16 * p if r.startswith("dma") else rank[(r, p)]
                        if seen.get(r, 0) >= val:
                            continue
                        eng.wait_ge(sems[r], val)
                        seen[r] = val
                    if o["fn"] is None:
                        continue
                    ins = o["fn"](eng)
                    if o["dma"] is not None:
                        ins.then_inc(sems[o["dma"]], 16)
                    elif pos in sig[ename]:
                        ins.then_inc(sems[ename], 1)
                if ename == "sp":
                    for n, c in self.dma_cnt.items():
                        if c and seen.get(n, 0) < 16 * c:
                            eng.wait_ge(sems[n], 16 * c)

            @block.tensor
            def _(e):
                run("pe", e)

            @block.scalar
            def _(e):
                run("act", e)

            @block.vector
            def _(e):
                run("dve", e)

            @block.gpsimd
            def _(e):
                run("pool", e)

            @block.sync
            def _(e):
                run("sp", e)


def _fm(v):
    v = np.asarray(v, np.float32)
    return np.ascontiguousarray(v.reshape(-1, 128).T)


class VecLayout:
    def __init__(self):
        self.off = {}
        self.n = 0

    def add(self, name, ncols):
        self.off[name] = self.n
        self.n += ncols


def vec_layout():
    L = VecLayout()
    for i in range(DEPTH):
        L.add(("attn_norm", i), 8)
        L.add(("ffn_norm", i), 8)
        L.add(("ple_norm", i), 8)
        for k in range(3):
            L.add(("ffn_cw", i, k), 44)
        L.add(("ffn_cb", i), 44)
    L.add("kv_norm", 8)
    L.add("final_norm", 8)
    for i in range(NA):
        for k in range(4):
            L.add(("ssm_cw", i, k), 24)
        L.add(("ssm_cb", i), 24)
        L.add(("ssm_dfm", i), 16)
    return L


def pack_vecs(inp):
    L = vec_layout()
    out = np.zeros((128, L.n), np.float32)

    def put(name, v):
        a = _fm(v)
        out[:, L.off[name]:L.off[name] + a.shape[1]] = a

    for i in range(DEPTH):
        put(("attn_norm", i), inp["attn_norm"][i])
        put(("ffn_norm", i), inp["ffn_norm"][i])
        put(("ple_norm", i), inp["ple_norm"][i])
        for k in range(3):
            put(("ffn_cw", i, k), inp["ffn_conv_w"][i][k])
        put(("ffn_cb", i), inp["ffn_conv_b"][i])
    put("kv_norm", inp["kv_norm"])
    put("final_norm", inp["final_norm"])
    for i in range(NA):
        for k in range(4):
            put(("ssm_cw", i, k), inp["ssm_conv_w"][i][k])
        put(("ssm_cb", i), inp["ssm_conv_b"][i])
        put(("ssm_dfm", i), np.repeat(np.asarray(inp["ssm_d"][i], np.float32), 64))
    return out


def pack_bvecs(inp):
    out = np.zeros((128, NA * 3072), np.float32)
    for i in range(NA):
        o = i * 3072
        out[:, o:o + 512] = np.tile(np.asarray(inp["ssm_dt_bias"][i], np.float32), 16)[None, :]
        out[:, o + 512:o + 1024] = np.tile(np.asarray(inp["ssm_a_log"][i], np.float32), 16)[None, :]
        out[:, o + 1024:o + 3072] = np.asarray(inp["ssm_norm"][i], np.float32)[None, :]
    return out


def make_consts():
    k = np.arange(128)
    c = np.zeros((128, 768), np.float32)
    am = ((k[:, None] + k[None, :]) >= 128).astype(np.float32)
    c[:, 512:640] = am
    c[:, 640:768] = (am - 1.0) * 30000.0
    c[:, 0:128] = (k[:, None] <= k[None, :])
    c[:, 128:256] = (k[:, None] > k[None, :])
    c[:, 256:384] = 1.0
    c[:, 384:512] = np.eye(128, dtype=np.float32)
    return c


class Prog:
    def __init__(self, nseq, cfg):
        self.nseq = nseq
        self.cfg = cfg
        self.nc = nc = bass.Bass("TRN2", target_bir_lowering=False)
        self.S = Sched()
        self.VL = vec_layout()
        di = lambda name, shape: nc.dram_tensor(name, shape, F32, kind="ExternalInput").ap()
        self.xT = di("xT", [nseq, D, S])
        self.pT = di("pT", [DEPTH, nseq, PLE, S])
        self.vecs_d = di("vecs", [128, self.VL.n])
        self.ffn_up = di("ffn_up", [DEPTH, D, 2 * DFF])
        self.ffn_down = di("ffn_down", [DEPTH, DFF, D])
        self.ple_gate = di("ple_gate", [DEPTH, D, D])
        self.ple_proj = di("ple_proj", [DEPTH, PLE, D])
        self.consts_d = di("consts", [128, 768])
        self.bvecs_d = di("bvecs", [128, NA * 3072])
        self.ssm_in = di("ssm_in_proj", [NA, D, 5152])
        self.ssm_out = di("ssm_out_proj", [NA, 2048, D])
        self.w_kv = di("w_kv", [D, 2048])
        self.w_q = di("w_q", [2, D, D])
        self.w_o = di("w_o", [2, D, D])
        self.kTd = nc.dram_tensor("kTd", [8, 128, S], BF16).ap()
        self.Vd = nc.dram_tensor("Vd", [8, 2, 128, 16 * 64], BF16).ap()
        self.outT = nc.dram_tensor("outT", [nseq, D, S], F32, kind="ExternalOutput").ap()

        A = nc.alloc_sbuf_tensor
        self.hT = A("hT", [128, KC, S], F32)
        self.hn = A("hn", [128, KC, S], BF16)
        self.vecs = A("vecsb", [128, self.VL.n], F32)
        self.wbuf = [A("wbuf%d" % i, [128, 6144], BF16) for i in range(2)]
        self.ones = A("ones", [128, 128], BF16)
        self.arena = A("arena", [128, 19968], F32)
        self.cst = A("cst", [128, 768], F32)
        self.amask = self.cst[:, 512:640]
        self.negmask = self.cst[:, 640:768]
        self.onesb = A("onesb", [128, 512], BF16)
        self.identb = A("identb", [128, 128], BF16)
        self.tri = self.cst[:, 0:128]
        self.strictT = self.cst[:, 128:256]
        self.onesf = self.cst[:, 256:384]
        self.ps = [nc.alloc_psum_tensor("ps%d" % i, [128, 512], F32) for i in range(8)]
        self.ps_rr = 0
        self.ps_lo = 0
        self.w_rr = 0
        print("sbuf remaining", nc.sbuf_bytes_remaining)

    def vcol(self, name, c):
        o = self.VL.off[name] + c
        return self.vecs[:, o:o + 1]

    def next_ps(self):
        lo = self.ps_lo
        i = self.ps_rr
        self.ps_rr = (i + 1) % (8 - lo)
        return lo + i

    def carve(self, off_f32, n, dtype, shape=None):
        ap = self.arena[:, off_f32:off_f32 + n]
        if dtype == BF16:
            ap = ap.bitcast(BF16)
        return ap

    def load_w(self, wd, rows, c0, ncols, eng="pool"):
        r0, nk = rows
        b = self.w_rr
        self.w_rr = 1 - b
        dst = self.wbuf[b][:, 0:nk * ncols].rearrange("p (k m) -> p k m", k=nk)
        src = wd[r0:r0 + nk * 128, c0:c0 + ncols].rearrange("(k p) m -> p k m", p=128)
        self.S.dma(eng, lambda e: e.dma_start(out=dst, in_=src), reads=(), writes=[("w", b)])
        return b, dst

    def rmsnorm(self, gain_name, out_bf16=True, out_ap_fn=None, eps=1e-6, reverse=False, top=False):
        S_ = self.S
        if top:
            nsl = 4
            sq = self.carve(17920, 1024, BF16)
            rr = [self.carve(18944, 512, F32), self.carve(19456, 512, F32)]
            tg = "T"
        else:
            nsl = 8
            sq = self.carve(0, 2048, BF16)
            rr = [self.carve(2048, 512, F32), self.carve(2560, 512, F32)]
            tg = "L"
        for tt in range(NT):
            t0 = tt * TT
            pb = self.next_ps()
            ps = self.ps[pb]
            for c in range(KC):
                sl = c % nsl
                o = sq[:, sl * 512:(sl + 1) * 512]
                i = self.hT[:, c, t0:t0 + TT]
                S_.op("act", lambda e, o=o, i=i: e.activation(o, i, AF.Square),
                      reads=[("h", c, tt)], writes=[("sq", tg, sl)])
                S_.op("pe", lambda e, ps=ps, o=o, c=c: e.matmul(ps[:, :], self.ones[:, :], o, start=(c == 0), stop=(c == KC - 1)),
                      reads=[("sq", tg, sl), "ones"], writes=[("ps", pb)])
            r = rr[tt % 2]
            rk = ("rstd", tg, tt % 2)
            S_.op("act", lambda e, r=r, ps=ps: e.activation(r, ps[:, :], AF.Sqrt, bias=eps, scale=1.0 / D),
                  reads=[("ps", pb)], writes=[rk])
            S_.op("dve", lambda e, r=r: e.reciprocal(r, r), reads=[rk], writes=[rk])
            for c in range(KC):
                if reverse:
                    o = self.hn[:, c, (3 - tt) * TT:(4 - tt) * TT][:, ::-1]
                    wk = ("hn", c, 3 - tt)
                elif out_ap_fn is None:
                    o = self.hn[:, c, t0:t0 + TT]
                    wk = ("hn", c, tt)
                else:
                    o, wk = out_ap_fn(c, tt)
                i = self.hT[:, c, t0:t0 + TT]
                g = self.vcol(gain_name, c)
                S_.op("dve", lambda e, o=o, i=i, g=g, r=r: e.scalar_tensor_tensor(o, i, g, r, ALU.mult, ALU.mult),
                      reads=[("h", c, tt), rk, "vecs"], writes=[wk])

    def linear(self, wd, k_rows, col_groups, rhs_fn, epi_fn, weng="pool"):
        S_ = self.S
        r0, nk = k_rows

        def load(gi):
            cols = col_groups[gi]
            b = self.w_rr
            self.w_rr = 1 - b
            n = len(cols)
            dst = self.wbuf[b][:, 0:nk * n * 128].rearrange("p (k m) -> p k m", k=nk)
            contiguous = all(cols[i + 1] == cols[i] + 128 for i in range(n - 1))
            if contiguous:
                src = wd[r0:r0 + nk * 128, cols[0]:cols[0] + n * 128].rearrange("(k p) m -> p k m", p=128)
                S_.dma(weng, lambda e, dst=dst, src=src: e.dma_start(out=dst, in_=src), reads=(), writes=[("w", b)])
            else:
                for i, c0 in enumerate(cols):
                    src = wd[r0:r0 + nk * 128, c0:c0 + 128].rearrange("(k p) m -> p k m", p=128)
                    d2 = dst[:, :, i * 128:(i + 1) * 128]
                    S_.dma(weng, lambda e, d2=d2, src=src: e.dma_start(out=d2, in_=src), reads=(),
                           writes=[("w", b, i)] + ([("w", b)] if i == 0 else []))
            return b, dst, (not contiguous)

        nxt = load(0)
        for gi, cols in enumerate(col_groups):
            b, wt, split = nxt
            if gi + 1 < len(col_groups):
                nxt = load(gi + 1)
            for mi in range(len(cols)):
                wkeys = [("w", b)] + ([("w", b, mi)] if split else [])
                for tt in range(NT):
                    pb = self.next_ps()
                    ps = self.ps[pb]
                    for k in range(nk):
                        rhs, rkey = rhs_fn(k, tt)
                        lhsT = wt[:, k, mi * 128:(mi + 1) * 128]
                        S_.op("pe", lambda e, ps=ps, lhsT=lhsT, rhs=rhs, k=k: e.matmul(ps[:, :], lhsT, rhs, start=(k == 0), stop=(k == nk - 1)),
                              reads=wkeys + [rkey], writes=[("ps", pb)])
                    epi_fn(gi, mi, tt, ps, ("ps", pb))

    def ffn(self, li):
        S_ = self.S
        self.rmsnorm(("ffn_norm", li), top=True)
        g = self.carve(0, 11264, BF16)
        Ug = self.carve(11264, 2052, F32)
        Uv = self.carve(11264 + 2052, 2052, F32)
        acc0 = 11264 + 2 * 2052
        accs = [[self.carve(acc0 + (2 * p + q) * 512, 512, F32) for q in range(2)] for p in range(2)]
        S_.op("dve", lambda e: e.memset(Ug[:, 0:2], 0.0), writes=[("U", 0, -1)])
        S_.op("dve", lambda e: e.memset(Uv[:, 0:2], 0.0), writes=[("U", 1, -1)])
        U = [Ug, Uv]
        wup = self.ffn_up[li]
        wdn = self.ffn_down[li]
        cnt = [0]
        for half in range(2):
            j0 = half * 11
            groups = []
            for jj in range(0, 11, 2):
                js = [j0 + jj] + ([j0 + jj + 1] if jj + 1 < 11 else [])
                cols = []
                for j in js:
                    cols += [j * 128, DFF + j * 128]
                groups.append(cols)

            def rhs_fn(k, tt):
                return self.hn[:, k, tt * TT:(tt + 1) * TT], ("hn", k, tt)

            self._ffn_up_group(wup, groups, rhs_fn, j0, li, U, accs, g)

            def rhs2(k, tt):
                return g[:, k * S + tt * TT: k * S + (tt + 1) * TT], ("g", k, tt)

            def epi2(gi, mi, tt, ps, pkey):
                c = gi * 4 + mi
                hv = self.hT[:, c, tt * TT:(tt + 1) * TT]
                S_.op("dve", lambda e: e.tensor_tensor(hv, ps[:, :], hv, ALU.add),
                      reads=[pkey, ("h", c, tt)], writes=[("h", c, tt)])

            self.linear(wdn, (half * 11 * 128, 11), [[0, 128, 256, 384], [512, 640, 768, 896]], rhs2, epi2)
        S_.fence()

    def _ffn_up_group(self, wup, groups, rhs_fn, j0, li, U, accs, g):
        S_ = self.S
        nk = KC

        def load(gi):
            cols = groups[gi]
            b = self.w_rr
            self.w_rr = 1 - b
            n = len(cols)
            dst = self.wbuf[b][:, 0:nk * n * 128].rearrange("p (k m) -> p k m", k=nk)
            npair = n // 2
            srcg = wup[:, cols[0]:cols[0] + npair * 128].rearrange("(k p) m -> p k m", p=128)
            srcv = wup[:, cols[1]:cols[1] + npair * 128].rearrange("(k p) m -> p k m", p=128)
            dg = dst[:, :, 0:npair * 128]
            dv = dst[:, :, npair * 128:2 * npair * 128]
            S_.dma("pool", lambda e: e.dma_start(out=dg, in_=srcg), reads=(), writes=[("w", b), ("w", b, 0)])
            S_.dma("pool", lambda e: e.dma_start(out=dv, in_=srcv), reads=(), writes=[("w", b, 1)])
            return b, dst, npair

        nxt = load(0)
        it = 0
        for gi in range(len(groups)):
            b, wt, npair = nxt
            if gi + 1 < len(groups):
                nxt = load(gi + 1)
            for pj in range(npair):
                j = j0 + gi * 2 + pj
                jj = j - j0
                for tt in range(NT):
                    t0 = tt * TT
                    par = it % 2
                    it += 1
                    pbs = []
                    for q in range(2):
                        pb = self.next_ps()
                        ps = self.ps[pb]
                        pbs.append(pb)
                        for k in range(nk):
                            rhs, rkey = rhs_fn(k, tt)
                            lhsT = wt[:, k, (q * npair + pj) * 128:(q * npair + pj + 1) * 128]
                            S_.op("pe", lambda e, ps=ps, lhsT=lhsT, rhs=rhs, k=k: e.matmul(ps[:, :], lhsT, rhs, start=(k == 0), stop=(k == nk - 1)),
                                  reads=[("w", b), ("w", b, q), rkey], writes=[("ps", pb)])
                    for q in range(2):
                        pb = pbs[q]
                        ps = self.ps[pb]
                        pkey = ("ps", pb)
                        acc = accs[par][q]
                        ch = q * NFF + j
                        Uq = U[q]
                        w2 = self.vcol(("ffn_cw", li, 2), ch)
                        w1 = self.vcol(("ffn_cw", li, 1), ch)
                        w0 = self.vcol(("ffn_cw", li, 0), ch)
                        bb = self.vcol(("ffn_cb", li), ch)
                        S_.op("act", lambda e, Uq=Uq, ps=ps, t0=t0: e.activation(Uq[:, 2 + t0:2 + t0 + TT], ps[:, :], AF.Identity),
                              reads=[pkey], writes=[("U", q, tt)])
                        S_.op("act", lambda e, acc=acc, ps=ps, bb=bb, w2=w2: e.activation(acc, ps[:, :], AF.Identity, bias=bb, scale=w2),
                              reads=[pkey, "vecs"], writes=[("acc", par, q)])
                        S_.op("dve", lambda e, acc=acc, Uq=Uq, w1=w1, t0=t0: e.scalar_tensor_tensor(acc, Uq[:, 1 + t0:1 + t0 + TT], w1, acc, ALU.mult, ALU.add),
                              reads=[("U", q, tt), ("U", q, tt - 1), "vecs"], writes=[("acc", par, q)])
                        S_.op("dve", lambda e, acc=acc, Uq=Uq, w0=w0, t0=t0: e.scalar_tensor_tensor(acc, Uq[:, t0:t0 + TT], w0, acc, ALU.mult, ALU.add),
                              reads=[("U", q, tt), ("U", q, tt - 1), "vecs"], writes=[("acc", par, q)])
                    ag, av = accs[par]
                    S_.op("act", lambda e, ag=ag: e.activation(ag, ag, AF.Silu), reads=[("acc", par, 0)], writes=[("acc", par, 0)])
                    go = g[:, jj * S + t0: jj * S + t0 + TT]
                    S_.op("pool", lambda e, go=go, ag=ag, av=av: e.tensor_tensor(go, ag, av, ALU.mult),
                          reads=[("acc", par, 0), ("acc", par, 1)], writes=[("g", jj, tt)])

    def mamba(self, li, si):
        S_ = self.S
        A = self.carve
        self.rmsnorm(("attn_norm", li))
        S_.fence()
        win = self.ssm_in[li]
        wout = self.ssm_out[li]
        bv = self.bvecs_d
        v3 = lambda ap, c: ap.rearrange("p (c h) -> p c h", c=c)
        dt = A(0, 512, F32); adt = A(512, 512, F32); acs = A(1024, 512, F32)
        dS = A(1536, 512, F32); Ea = A(2048, 512, F32); Etot = A(2560, 512, F32)
        dtb = A(3072, 512, F32); ea = A(3584, 512, F32)
        normw = A(4096, 512, F32)
        wdt = A(4608, 128, BF16).rearrange("p (k m) -> p k m", k=8)
        diagD = A(4736, 256, BF16)
        xc = A(4992, 4096, BF16)
        BT = A(9088, 1024, BF16)
        CT = A(10112, 1024, BF16)
        T0 = 11136
        U = A(T0, 2052, F32)
        acc = [A(T0 + 2052 + i * 512, 512, F32) for i in range(2)]
        o = T0
        xdt = [A(o + i * 256, 256, BF16) for i in range(2)]; o += 512
        xsc = [A(o + i * 256, 256, BF16) for i in range(2)]; o += 512
        Btok = [A(o + i * 64, 64, BF16) for i in range(2)]; o += 128
        CBm = [A(o + i * 128, 128, F32) for i in range(2)]; o += 256
        rhsD = A(o, 1024, F32); o += 1024
        MT = [A(o + i * 512, 512, BF16) for i in range(2)]; o += 1024
        zs = [A(o + i * 256, 256, BF16) for i in range(2)]; o += 512
        t1 = [A(o + i * 512, 512, F32) for i in range(2)]; o += 1024
        gn = [A(o + i * 256, 256, BF16) for i in range(2)]; o += 512
        gnT = [A(o + i * 1024, 1024, BF16) for i in range(2)]; o += 2048
        state = A(o, 512, F32); o += 512
        state_bf = A(o, 256, BF16); o += 256
        ssb = [A(o + i * 2, 1, F32) for i in range(2)]; o += 4
        rsb = [A(o + i * 2, 1, F32) for i in range(2)]; o += 4
        assert o <= 19968, o

        S_.dma("sp", lambda e: e.dma_start(out=dtb, in_=bv[:, li * 3072:li * 3072 + 512]), writes=["dtb"])
        S_.dma("sp", lambda e: e.dma_start(out=ea, in_=bv[:, li * 3072 + 512:li * 3072 + 1024]), writes=["ea"])
        wsrc = win[:, 5120:5152].rearrange("(k p) m -> p k m", p=128)
        S_.dma("pool", lambda e: e.dma_start(out=wdt, in_=wsrc), writes=["wdt"])
        pb = self.next_ps(); ps = self.ps[pb]
        for c in range(16):
            for k in range(KC):
                S_.op("pe", lambda e, c=c, k=k, ps=ps: e.matmul(ps[:, c * 32:(c + 1) * 32], self.hn[:, k, c * 128:(c + 1) * 128], wdt[:, k, :], start=(k == 0), stop=(k == KC - 1)),
                      reads=["wdt", ("hn", k, c // 4)], writes=[("ps", pb)])
        S_.op("dve", lambda e, ps=ps: e.tensor_tensor(dt, ps[:, :], dtb, ALU.add), reads=[("ps", pb), "dtb"], writes=["dt"])
        S_.op("act", lambda e: e.activation(dt, dt, AF.Exp), reads=["dt"], writes=["dt"])
        S_.op("act", lambda e: e.activation(dt, dt, AF.Ln, bias=1.0), reads=["dt"], writes=["dt"])
        S_.op("act", lambda e: e.activation(ea, ea, AF.Exp), reads=["ea"], writes=["ea"])
        S_.op("dve", lambda e: e.scalar_tensor_tensor(adt, dt, -1.0, ea, ALU.mult, ALU.mult), reads=["dt", "ea"], writes=["adt"])
        pa = self.next_ps(); psA = self.ps[pa]
        pbb = self.next_ps(); psB = self.ps[pbb]
        S_.op("pe", lambda e: e.matmul(psA[:, :], self.tri, adt, start=True, stop=True), reads=["cst", "adt"], writes=[("ps", pa)])
        S_.op("pe", lambda e: e.matmul(psB[:, :], self.onesf, adt, start=True, stop=True), reads=["cst", "adt"], writes=[("ps", pbb)])
        S_.op("act", lambda e: e.activation(acs, psA[:, :], AF.Identity), reads=[("ps", pa)], writes=["acs"])
        S_.op("act", lambda e: e.activation(Ea, psA[:, :], AF.Exp), reads=[("ps", pa)], writes=["Ea"])
        S_.op("act", lambda e: e.activation(Etot, psB[:, :], AF.Exp), reads=[("ps", pbb)], writes=["Etot"])
        S_.op("dve", lambda e: e.tensor_tensor(dS, psB[:, :], acs, ALU.subtract), reads=[("ps", pbb), "acs"], writes=["dS"])
        S_.op("act", lambda e: e.activation(dS, dS, AF.Exp), reads=["dS"], writes=["dS"])

        def do_group(g):
            S_.fence()
            b = self.w_rr
            self.w_rr = 1 - b
            wt = self.wbuf[b][:, 0:6144].rearrange("p (k m) -> p k m", k=8)
            for i, (c0, n, d0) in enumerate([(2048 + 512 * g, 512, 0), (4096 + 128 * g, 128, 512), (4608 + 128 * g, 128, 640)]):
                src = win[:, c0:c0 + n].rearrange("(k p) m -> p k m", p=128)
                dst = wt[:, :, d0:d0 + n]
                S_.dma("pool", lambda e, dst=dst, src=src: e.dma_start(out=dst, in_=src),
                       writes=[("w", b, i)] + ([("w", b)] if i == 0 else []))
            S_.dma("sp", lambda e, g=g: e.dma_start(out=normw, in_=bv[:, li * 3072 + 1024 + 512 * g:li * 3072 + 1536 + 512 * g]), writes=["normw"])
            S_.op("dve", lambda e: e.memset(U[:, 0:3], 0.0), writes=[("U", -1)])
            it = 0
            for j in range(6):
                ch = 4 * g + j if j < 4 else (16 + g if j == 4 else 20 + g)
                wi = 0 if j < 4 else j - 3
                for tt in range(NT):
                    t0 = tt * TT
                    if j < 4:
                        dest = xc[:, j * S + t0:j * S + t0 + TT]; dkey = ("xc", j, tt)
                    elif j == 4:
                        dest = BT[:, t0:t0 + TT]; dkey = ("BT", tt)
                    else:
                        dest = CT[:, t0:t0 + TT]; dkey = ("CT", tt)
                    pb = self.next_ps(); ps = self.ps[pb]
                    for k in range(KC):
                        S_.op("pe", lambda e, ps=ps, k=k, j=j, t0=t0: e.matmul(ps[:, :], wt[:, k, j * 128:(j + 1) * 128], self.hn[:, k, t0:t0 + TT], start=(k == 0), stop=(k == KC - 1)),
                              reads=[("w", b), ("w", b, wi), ("hn", k, tt)], writes=[("ps", pb)])
                    par = it % 2
                    it += 1
                    a_ = acc[par]
                    w3 = self.vcol(("ssm_cw", li, 3), ch); w2 = self.vcol(("ssm_cw", li, 2), ch)
                    w1 = self.vcol(("ssm_cw", li, 1), ch); w0 = self.vcol(("ssm_cw", li, 0), ch)
                    bb = self.vcol(("ssm_cb", li), ch)
                    S_.op("act", lambda e, ps=ps, t0=t0: e.activation(U[:, 3 + t0:3 + t0 + TT], ps[:, :], AF.Identity), reads=[("ps", pb)], writes=[("U", tt)])
                    S_.op("act", lambda e, ps=ps, a_=a_, bb=bb, w3=w3: e.activation(a_, ps[:, :], AF.Identity, bias=bb, scale=w3),
                          reads=[("ps", pb), "vecs"], writes=[("macc", par)])
                    for sh, w in ((2, w2), (1, w1), (0, w0)):
                        S_.op("dve", lambda e, a_=a_, w=w, sh=sh, t0=t0: e.scalar_tensor_tensor(a_, U[:, sh + t0:sh + t0 + TT], w, a_, ALU.mult, ALU.add),
                              reads=[("U", tt), ("U", tt - 1), "vecs"], writes=[("macc", par)])
                    S_.op("act", lambda e, dest=dest, a_=a_: e.activation(dest, a_, AF.Silu), reads=[("macc", par)], writes=[dkey])

            S_.fence()
            bz = self.w_rr
            bo = 1 - bz
            Wz = self.wbuf[bz][:, 0:4096].rearrange("p (k m) -> p k m", k=8)
            Wo = self.wbuf[bo][:, 0:4096].rearrange("p (k m) -> p k m", k=4)
            srcz = win[:, 512 * g:512 * g + 512].rearrange("(k p) m -> p k m", p=128)
            srco = wout[512 * g:512 * g + 512, :].rearrange("(k p) m -> p k m", p=128)
            S_.dma("pool", lambda e: e.dma_start(out=Wz, in_=srcz), writes=[("w", bz)])
            S_.dma("pool", lambda e: e.dma_start(out=Wo, in_=srco), writes=[("w", bo)])
            for j in range(4):
                dj = diagD[:, j * 128:(j + 1) * 128]
                sc = self.vcol(("ssm_dfm", li), 4 * g + j)
                S_.op("dve", lambda e, dj=dj, sc=sc: e.tensor_scalar(dj, self.identb[:, :], sc, None, ALU.mult), reads=["identb", "vecs"], writes=[("diagD", j)])
            S_.op("dve", lambda e: e.memset(state, 0.0), writes=["state"])
            S_.op("pool", lambda e: e.memset(state_bf, 0.0), writes=["state_bf"])
            h8 = slice(8 * g, 8 * g + 8)
            bc64 = lambda ap, c: v3(ap, 16)[:, c, h8].unsqueeze(2).to_broadcast([128, 8, 64])
            r64 = lambda ap: ap.rearrange("p (r d) -> p r d", r=8)

            def stageA1(c):
                par = c % 2
                tt = c // 4
                l0 = c * 128
                pbx = self.next_ps(); psx = self.ps[pbx][:, :].bitcast(BF16)
                for j in range(4):
                    S_.op("pe", lambda e, j=j: e.transpose(psx[:, j * 128:(j + 1) * 128], xc[:, j * S + l0:j * S + l0 + 128], self.identb[:, :]),
                          reads=[("xc", j, tt), "identb"], writes=[("ps", pbx)])
                S_.op("pe", lambda e: e.transpose(psx[:, 512:640], BT[:, l0:l0 + 128], self.identb[:, :]), reads=[("BT", tt), "identb"], writes=[("ps", pbx)])
                S_.op("dve", lambda e: e.tensor_tensor(r64(xdt[par]), r64(psx[:, 0:512]), bc64(dt, c), ALU.mult), reads=[("ps", pbx), "dt"], writes=[("xdt", par)])
                S_.op("pool", lambda e: e.tensor_tensor(r64(xsc[par]), r64(xdt[par]), bc64(dS, c), ALU.mult), reads=[("xdt", par), "dS"], writes=[("xsc", par)])
                S_.op("act", lambda e: e.activation(Btok[par], psx[:, 512:640], AF.Identity), reads=[("ps", pbx)], writes=[("Btok", par)])
                pbc = self.next_ps(); psc = self.ps[pbc]
                S_.op("pe", lambda e: e.matmul(psc[:, 0:128], BT[:, l0:l0 + 128], CT[:, l0:l0 + 128], start=True, stop=True),
                      reads=[("BT", tt), ("CT", tt)], writes=[("ps", pbc)])
                S_.op("dve", lambda e: e.tensor_tensor(CBm[par], psc[:, 0:128], self.tri, ALU.mult), reads=[("ps", pbc), "cst"], writes=[("CBm", par)])
                r128 = lambda ap: ap.rearrange("p (r d) -> p r d", r=8)
                S_.op("dve", lambda e: e.tensor_tensor(r128(rhsD), self.tri.unsqueeze(1).to_broadcast([128, 8, 128]),
                                                       v3(adt, 16)[:, c, h8].unsqueeze(2).to_broadcast([128, 8, 128]), ALU.mult),
                      reads=["cst", "adt"], writes=["rhsD"])
                for i in range(2):
                    pbd = self.next_ps(); psd = self.ps[pbd]
                    S_.op("pe", lambda e, psd=psd, i=i: e.matmul(psd[:, :], self.strictT, rhsD[:, i * 512:(i + 1) * 512], start=True, stop=True),
                          reads=["cst", "rhsD"], writes=[("ps", pbd)])
                    S_.op("act", lambda e, psd=psd, i=i: e.activation(MT[par][:, i * 512:(i + 1) * 512], psd[:, :], AF.Exp), reads=[("ps", pbd)], writes=[("MT", par, i)])
                pbz = self.next_ps(); psz = self.ps[pbz]
                for k in range(KC):
                    S_.op("pe", lambda e, k=k: e.matmul(psz[:, :], self.hn[:, k, l0:l0 + 128], Wz[:, k, :], start=(k == 0), stop=(k == KC - 1)),
                          reads=[("w", bz), ("hn", k, tt)], writes=[("ps", pbz)])
                S_.op("act", lambda e: e.activation(zs[par], psz[:, :], AF.Silu), reads=[("ps", pbz)], writes=[("zs", par)])

            def stageA2(c):
                par = c % 2
                tt = c // 4
                l0 = c * 128
                r128 = lambda ap: ap.rearrange("p (r d) -> p r d", r=8)
                S_.op("dve", lambda e: e.tensor_tensor(r128(MT[par]), r128(MT[par]), CBm[par].unsqueeze(1).to_broadcast([128, 8, 128]), ALU.mult),
                      reads=[("MT", par, 0), ("MT", par, 1), ("CBm", par)], writes=[("MT", par, 0), ("MT", par, 1)])
                pbo = self.next_ps(); pso = self.ps[pbo]
                S_.op("pe", lambda e: e.matmul(pso[:, :], CT[:, l0:l0 + 128], state_bf, start=True, stop=True), reads=[("CT", tt), "state_bf"], writes=[("ps", pbo)])
                pbs = self.next_ps(); pss = self.ps[pbs]
                S_.op("pe", lambda e: e.matmul(pss[:, :], Btok[par], xsc[par], start=True, stop=True), reads=[("Btok", par), ("xsc", par)], writes=[("ps", pbs)])
                S_.op("dve", lambda e: e.tensor_tensor(r64(state), r64(state), bc64(Etot, c), ALU.mult), reads=["state", "Etot"], writes=["state"])
                S_.op("dve", lambda e: e.tensor_tensor(state, state, pss[:, :], ALU.add), reads=["state", ("ps", pbs)], writes=["state"])
                S_.op("pool", lambda e: e.tensor_copy(state_bf, state), reads=["state"], writes=["state_bf"])
                pby = self.next_ps(); psy = self.ps[pby]
                for j in range(4):
                    S_.op("pe", lambda e, j=j: e.matmul(psy[:, j * 128:(j + 1) * 128], xc[:, j * S + l0:j * S + l0 + 128], diagD[:, j * 128:(j + 1) * 128], start=True, stop=False),
                          reads=[("xc", j, tt), ("diagD", j)], writes=[("ps", pby)])
                    for r in (2 * j, 2 * j + 1):
                        S_.op("pe", lambda e, r=r, j=j: e.matmul(psy[:, r * 64:(r + 1) * 64], MT[par][:, r * 128:(r + 1) * 128], xdt[par][:, r * 64:(r + 1) * 64], start=False, stop=(r == 2 * j + 1)),
                              reads=[("MT", par, 0), ("MT", par, 1), ("xdt", par)], writes=[("ps", pby)])
                T = t1[par]
                S_.op("dve", lambda e: e.tensor_tensor(r64(T), r64(pso[:, :]), bc64(Ea, c), ALU.mult), reads=[("ps", pbo), "Ea"], writes=[("t1", par)])
                S_.op("dve", lambda e: e.tensor_tensor(T, T, psy[:, :], ALU.add), reads=[("ps", pby), ("t1", par)], writes=[("t1", par)])
                S_.op("dve", lambda e: e.tensor_tensor(T, T, zs[par], ALU.mult), reads=[("zs", par), ("t1", par)], writes=[("t1", par)])
                S_.op("act", lambda e: e.activation(gn[par], T, AF.Square, accum_out=ssb[par]), reads=[("t1", par)], writes=[("gn", par), ("ss", par)])
                S_.op("act", lambda e: e.activation(rsb[par], ssb[par], AF.Sqrt, bias=1e-5, scale=1.0 / 512), reads=[("ss", par)], writes=[("rs", par)])
                S_.op("dve", lambda e: e.reciprocal(rsb[par], rsb[par]), reads=[("rs", par)], writes=[("rs", par)])
                S_.op("dve", lambda e: e.scalar_tensor_tensor(gn[par], T, rsb[par], normw, ALU.mult, ALU.mult), reads=[("t1", par), ("rs", par), "normw"], writes=[("gn", par)])

            def stageB(c):
                par = c % 2
                tt = c // 4
                q = c % 4
                G = gnT[tt % 2].rearrange("p (j t) -> p j t", j=4)
                pbt = self.next_ps(); pst = self.ps[pbt][:, :].bitcast(BF16)
                for j in range(4):
                    S_.op("pe", lambda e, j=j: e.transpose(pst[:, j * 128:(j + 1) * 128], gn[par][:, j * 128:(j + 1) * 128], self.identb[:, :]),
                          reads=[("gn", par), "identb"], writes=[("ps", pbt)])
                S_.op("act", lambda e: e.activation(G[:, :, q * 128:(q + 1) * 128], pst[:, 0:512].rearrange("p (j t) -> p j t", j=4), AF.Identity),
                      reads=[("ps", pbt)], writes=[("gnT", tt % 2, q)])
                if q == 3:
                    for m in range(KC):
                        pbm = self.next_ps(); psm = self.ps[pbm]
                        for k in range(4):
                            S_.op("pe", lambda e, m=m, k=k, psm=psm: e.matmul(psm[:, :], Wo[:, k, m * 128:(m + 1) * 128], G[:, k, :], start=(k == 0), stop=(k == 3)),
                                  reads=[("w", bo)] + [("gnT", tt % 2, qq) for qq in range(4)], writes=[("ps", pbm)])
                        hv = self.hT[:, m, tt * TT:(tt + 1) * TT]
                        S_.op("dve", lambda e, hv=hv, psm=psm: e.tensor_tensor(hv, psm[:, :], hv, ALU.add), reads=[("ps", pbm), ("h", m, tt)], writes=[("h", m, tt)])

            for c in range(18):
                if c < 16:
                    stageA1(c)
                if 0 <= c - 1 < 16:
                    stageA2(c - 1)
                if 0 <= c - 2 < 16:
                    stageB(c - 2)

        for g in range(4):
            do_group(g)
        S_.fence()

    def kv_stage(self, si):
        S_ = self.S
        A = self.carve
        self.rmsnorm("kv_norm", reverse=True)
        S_.fence()
        stg = [A(i * 256, 256, BF16) for i in range(4)]
        it = [0]

        def rhs_fn(k, tt):
            return self.hn[:, k, tt * TT:(tt + 1) * TT], ("hn", k, tt)

        def epi(gi, mi, tt, ps, pkey):
            hp = gi * 4 + mi
            par = it[0] % 4
            it[0] += 1
            sb = stg[par]
            S_.op("act", lambda e: e.activation(sb, ps[:, :], AF.Identity), reads=[pkey], writes=[("stg", par)])
            dst = self.kTd[hp, :, tt * TT:(tt + 1) * TT]
            S_.dma("sp", lambda e: e.dma_start(out=dst, in_=sb), reads=[("stg", par)], writes=[("kTd", hp)])

        self.linear(self.w_kv, (0, 8), [[0, 128, 256, 384], [512, 640, 768, 896]], rhs_fn, epi)
        Wv = []
        for cg in range(2):
            b = cg
            wt = self.wbuf[b][:, 0:4096].rearrange("p (k m) -> p k m", k=8)
            src = self.w_kv[:, 1024 + cg * 512:1024 + (cg + 1) * 512].rearrange("(k p) m -> p k m", p=128)
            S_.dma("pool", lambda e, wt=wt, src=src: e.dma_start(out=wt, in_=src), writes=[("w", b)])
            Wv.append(wt)
        for jb in range(16):
            for cg in range(2):
                pb = self.next_ps(); ps = self.ps[pb]
                for k in range(KC):
                    S_.op("pe", lambda e, ps=ps, k=k, jb=jb, cg=cg: e.matmul(ps[:, :], self.hn[:, k, jb * 128:(jb + 1) * 128], Wv[cg][:, k, :], start=(k == 0), stop=(k == KC - 1)),
                          reads=[("w", cg), ("hn", k, jb // 4)], writes=[("ps", pb)])
                par = it[0] % 4
                it[0] += 1
                sb = stg[par]
                S_.op("act", lambda e, sb=sb, ps=ps: e.activation(sb, ps[:, :], AF.Identity), reads=[("ps", pb)], writes=[("stg", par)])
                dst = self.Vd[cg * 4:(cg + 1) * 4, :, :, jb * 64:(jb + 1) * 64].rearrange("h f j d -> j h f d")
                srcv = sb.rearrange("p (h f d) -> p h f d", h=4, f=2)
                S_.dma("sp", lambda e, dst=dst, srcv=srcv: e.dma_start(out=dst, in_=srcv), reads=[("stg", par)],
                       writes=[("Vd", cg * 4 + hh) for hh in range(4)])
        S_.fence()

    def attention(self, li, si):
        S_ = self.S
        A = self.carve
        j_ = li - NA
        scale = 0.125
        self.rmsnorm(("attn_norm", li))
        S_.fence()
        oT = A(0, 8192, BF16).rearrange("p (c t) -> p c t", c=8)
        o = 8192
        kT = [A(o + i * 1024, 1024, BF16) for i in range(2)]; o += 2048
        Vb = [A(o + i * 2048, 2048, BF16) for i in range(2)]; o += 4096
        qT = [A(o, 1024, BF16)] * 2; o += 1024
        eb = [A(o + i * 512, 512, F32) for i in range(4)]; o += 2048
        cb = [A(o + i * 512, 512, F32) for i in range(2)]; o += 1024
        wb = [A(o + i * 256, 256, BF16) for i in range(2)]; o += 512
        wTb = [A(o + i * 256, 256, BF16) for i in range(2)]; o += 512
        assert o <= 19968, o
        Wq = []
        for cg in range(2):
            wt = self.wbuf[cg][:, 0:4096].rearrange("p (k m) -> p k m", k=8)
            src = self.w_q[j_][:, cg * 512:(cg + 1) * 512].rearrange("(k p) m -> p k m", p=128)
            S_.dma("pool", lambda e, wt=wt, src=src: e.dma_start(out=wt, in_=src), writes=[("w", cg)])
            Wq.append(wt)
        for i in range(2):
            vz = Vb[i].rearrange("p (f b c d) -> p f b c d", f=2, b=16, c=2)
            S_.op("dve", lambda e, vz=vz: e.memset(vz[:, 0, :, 1, :], 0.0), writes=[("Vz", i)])
            S_.op("dve", lambda e, vz=vz: e.memset(vz[:, 1, :, 0, :], 0.0), writes=[("Vz", i)])
        self.ps_lo = 2
        self.ps_rr = 0
        pools = {"z": [2, 3, 4], "t": [5, 6], "q": [7]}
        pool_rr = {"z": 0, "t": 0, "q": 0}

        def pool_ps(name):
            lst = pools[name]
            i = pool_rr[name]
            pool_rr[name] = (i + 1) % len(lst)
            return lst[i]

        def load_hp(hp):
            bpar = hp % 2
            S_.dma("sp", lambda e: e.dma_start(out=kT[bpar], in_=self.kTd[hp]), reads=[("kTd", hp)], writes=[("kT", bpar)])
            vz = Vb[bpar].rearrange("p (f b c d) -> p f b c d", f=2, b=16, c=2)
            for f in range(2):
                src = self.Vd[hp, f].rearrange("j (b d) -> j b d", b=16)
                dst = vz[:, f, :, f, :]
                S_.dma("sp", lambda e, dst=dst, src=src: e.dma_start(out=dst, in_=src), reads=[("Vd", hp)], writes=[(("VA", "VB")[f], bpar)])

        def q_proj(hp):
            cg, hl = hp // 4, hp % 4
            for tt in range(NT):
                pb = pool_ps("q"); ps = self.ps[pb]
                for k in range(KC):
                    S_.op("pe", lambda e, ps=ps, k=k, tt=tt: e.matmul(ps[:, :], Wq[cg][:, k, hl * 128:(hl + 1) * 128], self.hn[:, k, tt * TT:(tt + 1) * TT], start=(k == 0), stop=(k == KC - 1)),
                          reads=[("w", cg), ("hn", k, tt)], writes=[("ps", pb)])
                S_.op("act", lambda e, ps=ps, tt=tt: e.activation(qT[hp % 2][:, tt * TT:(tt + 1) * TT], ps[:, :], AF.Identity), reads=[("ps", pb)], writes=[("qT", 0, tt)])

        class Seg:
            pass

        segs = []
        for hp in range(8):
            for i in range(16):
                nseg = (i + 1 + 3) // 4
                lst = [(half, sg) for half in range(2) for sg in range(nseg)]
                for n_, (half, sg) in enumerate(lst):
                    g = Seg()
                    g.hp, g.i, g.half, g.sg = hp, i, half, sg
                    g.first = (n_ == 0)
                    g.last = (n_ == len(lst) - 1)
                    g.bpar = hp % 2
                    g.rows = slice(half * 64, half * 64 + 64)
                    j0 = S - (i + 1) * 128
                    g.js = j0 + sg * 512
                    g.n = min(512, S - g.js)
                    segs.append(g)
        for idx, g in enumerate(segs):
            g.par = idx % 2
            g.pe3 = idx % 4
        qb = {}
        for g in segs:
            key = (g.hp, g.i)
            if key not in qb:
                qb[key] = len(qb) % 2
            g.pbo = qb[key]

        def stA(g):
            n = g.n
            g.pb = pool_ps("z")
            ps = self.ps[g.pb]
            E = eb[g.pe3]
            S_.op("pe", lambda e: e.matmul(ps[:, 0:n], qT[g.hp % 2][g.rows, g.i * 128:(g.i + 1) * 128], kT[g.bpar][g.rows, g.js:g.js + n], start=True, stop=True),
                  reads=[("qT", 0, g.i // 4), ("kT", g.bpar)], writes=[("ps", g.pb)])
            S_.op("act", lambda e: e.activation(E[:, 0:n], ps[:, 0:n], AF.Exp, scale=scale), reads=[("ps", g.pb)], writes=[("e", g.pe3)])

        def stA2(g):
            n = g.n
            E = eb[g.pe3]
            S_.op("act", lambda e: e.activation(E[:, 0:n], E[:, 0:n], AF.Ln, bias=1.0), reads=[("e", g.pe3)], writes=[("e", g.pe3)])

        def stB(g):
            n = g.n
            ps = self.ps[g.pb]
            E = eb[g.pe3]; C = cb[g.par]
            if g.sg == 0:
                S_.op("dve", lambda e: e.tensor_tensor(E[:, 0:128], E[:, 0:128], self.amask, ALU.mult), reads=[("e", g.pe3), "cst"], writes=[("e", g.pe3)])
                init = 0.0
                rd = []
            else:
                init = cb[1 - g.par][:, 511:512]
                rd = [("c", 1 - g.par)]
            S_.op("dve", lambda e: e.tensor_tensor_scan(C[:, 0:n], self.onesb[:, 0:n], E[:, 0:n], init, ALU.mult, ALU.add),
                  reads=[("e", g.pe3), "onesb"] + rd, writes=[("c", g.par)])

        def stB2(g):
            n = g.n
            ps = self.ps[g.pb]
            E = eb[g.pe3]; C = cb[g.par]
            S_.op("dve", lambda e: e.scalar_tensor_tensor(E[:, 0:n], ps[:, 0:n], scale, C[:, 0:n], ALU.mult, ALU.subtract),
                  reads=[("ps", g.pb), ("c", g.par)], writes=[("e", g.pe3)])
            if g.sg == 0:
                S_.op("dve", lambda e: e.tensor_tensor(E[:, 0:128], E[:, 0:128], self.negmask, ALU.add), reads=[("e", g.pe3), "cst"], writes=[("e", g.pe3)])

        def stC(g):
            n = g.n
            nb = n // 128
            E = eb[g.pe3]; W = wb[g.par]
            S_.op("act", lambda e: e.activation(W[:, 0:n], E[:, 0:n], AF.Exp), reads=[("e", g.pe3)], writes=[("wsb", g.par)])
            g.pbt = pool_ps("t")
            pst = self.ps[g.pbt][:, :].bitcast(BF16)
            for b_ in range(nb):
                S_.op("pe", lambda e, b_=b_: e.transpose(pst[:, b_ * 128:(b_ + 1) * 128], W[:, b_ * 128:(b_ + 1) * 128], self.identb[:, :]),
                      reads=[("wsb", g.par), "identb"], writes=[("ps", g.pbt)])

        def stD(g):
            n = g.n
            nb = n // 128
            pst = self.ps[g.pbt][:, :].bitcast(BF16)
            WT = wTb[g.par]
            pso = self.ps[g.pbo]; pok = ("ps", g.pbo)
            S_.op("act", lambda e: e.activation(WT[:, 0:n], pst[:, 0:n], AF.Identity), reads=[("ps", g.pbt)], writes=[("wT", g.par)])
            base = 0 if g.half == 0 else 2048
            vkey = ("VA", g.bpar) if g.half == 0 else ("VB", g.bpar)
            for b_ in range(nb):
                jb = (g.js + b_ * 128) // 128
                off = base + jb * 128
                lhsT = Vb[g.bpar][:, off:off + 128]
                S_.op("pe", lambda e, b_=b_, lhsT=lhsT: e.matmul(pso[:, 0:128], lhsT, WT[:, b_ * 128:(b_ + 1) * 128],
                                                               start=(g.first and b_ == 0), stop=(g.last and b_ == nb - 1)),
                      reads=[("wT", g.par), vkey, ("Vz", g.bpar)], writes=[pok])
            if g.last:
                S_.op("act", lambda e: e.activation(oT[:, g.hp, g.i * 128:(g.i + 1) * 128], pso[:, 0:128], AF.Identity), reads=[pok], writes=[("oT", g.hp, g.i // 4)])

        load_hp(0)
        hp_start = 0
        NS = len(segs)
        for k in range(NS + 4):
            if k < NS:
                g = segs[k]
                if g.i == 0 and g.first:
                    q_proj(g.hp)
                    hp_start = k
                if k == hp_start + 5 and g.hp + 1 < 8:
                    load_hp(g.hp + 1)
                stA(g)
            if 0 <= k - 3 < NS:
                stC(segs[k - 3])
            if k < NS:
                stA2(segs[k])
            if 0 <= k - 2 < NS:
                stB2(segs[k - 2])
            if 0 <= k - 1 < NS:
                stB(segs[k - 1])
            if 0 <= k - 4 < NS:
                stD(segs[k - 4])
        self.ps_lo = 0
        self.ps_rr = 0

        def rhs_o(k, tt):
            return oT[:, k, tt * TT:(tt + 1) * TT], ("oT", k, tt)

        def epi_o(gi, mi, tt, ps, pkey):
            c = gi * 4 + mi
            hv = self.hT[:, c, tt * TT:(tt + 1) * TT]
            S_.op("dve", lambda e: e.tensor_tensor(hv, ps[:, :], hv, ALU.add), reads=[pkey, ("h", c, tt)], writes=[("h", c, tt)])

        self.linear(self.w_o[j_], (0, 8), [[0, 128, 256, 384], [512, 640, 768, 896]], rhs_o, epi_o)
        S_.fence()

    def ple(self, li, si):
        S_ = self.S
        self.rmsnorm(("ple_norm", li), top=True)
        pb16 = self.carve(0, 2048, BF16)
        sg = [self.carve(2048 + i * 512, 512, F32) for i in range(2)]
        src = self.pT[li, si].rearrange("(k p) t -> p k t", p=128)
        dst = pb16.rearrange("p (k t) -> p k t", k=2)
        S_.dma("pool", lambda e: e.dma_start(out=dst, in_=src), reads=(), writes=["pT"])
        wp = self.carve(3072, 1024, BF16).rearrange("p (k m) -> p k m", k=2)
        srcw = self.ple_proj[li].rearrange("(k p) m -> p k m", p=128)
        S_.dma("pool", lambda e: e.dma_start(out=wp, in_=srcw), reads=(), writes=["wp"])
        it = [0]

        def rhs_fn(k, tt):
            return self.hn[:, k, tt * TT:(tt + 1) * TT], ("hn", k, tt)

        def epi(gi, mi, tt, ps, pkey):
            c = gi * 4 + mi
            t0 = tt * TT
            par = it[0] % 2
            it[0] += 1
            s = sg[par]
            S_.op("act", lambda e: e.activation(s, ps[:, :], AF.Sigmoid), reads=[pkey], writes=[("sg", par)])
            pb2 = self.next_ps()
            ps2 = self.ps[pb2]
            for k in range(2):
                S_.op("pe", lambda e, k=k: e.matmul(ps2[:, :], wp[:, k, c * 128:(c + 1) * 128], pb16[:, k * S + t0:k * S + t0 + TT], start=(k == 0), stop=(k == 1)),
                      reads=["wp", "pT"], writes=[("ps", pb2)])
            S_.op("dve", lambda e: e.tensor_tensor(s, s, ps2[:, :], ALU.mult), reads=[("sg", par), ("ps", pb2)], writes=[("sg", par)])
            hv = self.hT[:, c, t0:t0 + TT]
            S_.op("dve", lambda e: e.tensor_tensor(hv, hv, s, ALU.add), reads=[("sg", par), ("h", c, tt)], writes=[("h", c, tt)])

        self.linear(self.ple_gate[li], (0, 8), [[0, 128, 256, 384], [512, 640, 768, 896]], rhs_fn, epi)
        S_.fence()

    def build(self):
        S_ = self.S
        nc = self.nc
        cfg = self.cfg
        S_.op("dve", lambda e: e.memset(self.ones[:, :], 1.0), writes=["ones"])
        S_.op("dve", lambda e: e.memset(self.onesb[:, :], 1.0), writes=["onesb"])
        S_.dma("sp", lambda e: e.dma_start(out=self.vecs[:, :], in_=self.vecs_d[:, :]), writes=["vecs"])
        S_.dma("sp", lambda e: e.dma_start(out=self.cst[:, :], in_=self.consts_d[:, :]), writes=["cst"])
        S_.op("dve", lambda e: e.tensor_copy(self.identb[:, :], self.cst[:, 384:512]), reads=["cst"], writes=["identb"])
        for si in range(self.nseq):
            for c in range(KC):
                src = self.xT[si, c * 128:(c + 1) * 128, :]
                dst = self.hT[:, c, :]
                S_.dma("sp", lambda e, dst=dst, src=src: e.dma_start(out=dst, in_=src),
                       writes=[("h", c, tt) for tt in range(NT)])
            for li in cfg["layers"]:
                if cfg.get("mixer", True):
                    if li < NA:
                        self.mamba(li, si)
                    else:
                        self.attention(li, si)
                if cfg.get("ffn", True):
                    self.ffn(li)
                if cfg.get("ple", True):
                    self.ple(li, si)
                if li == NA - 1 and cfg.get("mixer", True) and any(l >= NA for l in cfg["layers"]):
                    self.kv_stage(si)
            S_.fence()
            ob2 = self.carve(4096, 2 * 4096, F32)

            self._final_norm_store(si, ob2)
            S_.fence()
        S_.emit(nc)
        return nc

    def _final_norm_store(self, si, ob2):
        S_ = self.S
        eps = 1e-6
        sq = self.carve(0, 2048, BF16)
        rr = [self.carve(2048, 512, F32), self.carve(2560, 512, F32)]
        for tt in range(NT):
            t0 = tt * TT
            pb = self.next_ps()
            ps = self.ps[pb]
            for c in range(KC):
                o = sq[:, c * 512:(c + 1) * 512]
                i = self.hT[:, c, t0:t0 + TT]
                S_.op("act", lambda e, o=o, i=i: e.activation(o, i, AF.Square), reads=[("h", c, tt)], writes=[("sq", c)])
            for c in range(KC):
                r_ = sq[:, c * 512:(c + 1) * 512]
                S_.op("pe", lambda e, ps=ps, r_=r_, c=c: e.matmul(ps[:, :], self.ones[:, :], r_, start=(c == 0), stop=(c == KC - 1)),
                      reads=[("sq", c), "ones"], writes=[("ps", pb)])
            r = rr[tt % 2]
            rk = ("rstd", tt % 2)
            S_.op("act", lambda e, r=r, ps=ps: e.activation(r, ps[:, :], AF.Sqrt, bias=eps, scale=1.0 / D), reads=[("ps", pb)], writes=[rk])
            S_.op("dve", lambda e, r=r: e.reciprocal(r, r), reads=[rk], writes=[rk])
            for c in range(KC):
                o = ob2[:, (tt % 2) * 4096 + c * 512:(tt % 2) * 4096 + (c + 1) * 512]
                i = self.hT[:, c, t0:t0 + TT]
                g = self.vcol("final_norm", c)
                S_.op("dve", lambda e, o=o, i=i, g=g, r=r: e.scalar_tensor_tensor(o, i, g, r, ALU.mult, ALU.mult),
                      reads=[("h", c, tt), rk, "vecs"], writes=[("ob", tt % 2, c)])
                dst = self.outT[si, c * 128:(c + 1) * 128, t0:t0 + TT]
                S_.dma("sp", lambda e, dst=dst, o=o: e.dma_start(out=dst, in_=o), reads=[("ob", tt % 2, c)], writes=[])


NCORES = 8
_prog_cache = {}


def get_prog(nseq, cfg):
    key = (nseq, repr(sorted(cfg.items())))
    if key not in _prog_cache:
        p = Prog(nseq, cfg)
        p.build()
        _prog_cache[key] = p
    return _prog_cache[key]


def make_in_maps(inp, batch_ids_per_core):
    vecs = pack_vecs(inp)
    shared = dict(vecs=vecs, consts=make_consts(), bvecs=pack_bvecs(inp),
                  ssm_in_proj=np.ascontiguousarray(inp["ssm_in_proj"], np.float32),
                  ssm_out_proj=np.ascontiguousarray(inp["ssm_out_proj"], np.float32),
                  w_kv=np.ascontiguousarray(inp["w_kv"], np.float32),
                  w_q=np.ascontiguousarray(inp["w_q"], np.float32),
                  w_o=np.ascontiguousarray(inp["w_o"], np.float32),
                  ffn_up=np.ascontiguousarray(inp["ffn_up"], np.float32),
                  ffn_down=np.ascontiguousarray(inp["ffn_down"], np.float32),
                  ple_gate=np.ascontiguousarray(inp["ple_gate"], np.float32),
                  ple_proj=np.ascontiguousarray(inp["ple_proj"], np.float32))
    in_maps = []
    for ids in batch_ids_per_core:
        xT = np.ascontiguousarray(np.transpose(inp["x"][ids], (0, 2, 1)))
        pT = np.ascontiguousarray(np.transpose(inp["p"][:, ids], (0, 1, 3, 2)))
        m = dict(shared)
        m["xT"] = xT
        m["pT"] = pT
        in_maps.append(m)
    return in_maps


def run_cores(inp, batch_ids_per_core, cfg):
    nseq = len(batch_ids_per_core[0])
    prog = get_prog(nseq, cfg)
    in_maps = make_in_maps(inp, batch_ids_per_core)
    res = run_bass_kernel_spmd(prog.nc, in_maps, core_ids=list(range(len(in_maps))))
    outs = []
    for r in res.results:
        outs.append(np.transpose(r["outT"], (0, 2, 1)))
    return outs


FULL_CFG = dict(layers=(0, 1, 2, 3), mixer=True, ffn=True, ple=True)


def kernel(**inp):
    inp = {k: np.asarray(v) for k, v in inp.items()}
    B = inp["x"].shape[0]
    per = B // NCORES
    ids = [[c * per + g for g in range(per)] for c in range(NCORES)]
    outs = run_cores(inp, ids, FULL_CFG)
    out = np.empty((B, S, D), np.float32)
    for c in range(NCORES):
        for g in range(per):
            out[c * per + g] = outs[c][g]
    return out
```
